# Optimizing a Trainium2 kernel written in Bass

```python
import math
import jax, jax.numpy as jnp
from jax import lax
import numpy as np

D_MODEL = 1024
BATCH = 16
SEQ = 4096
DEPTH = 2
DEC_BATCH = 8
DEC_SEQ = 16
PAST_LEN = 4096

CHUNK = 64
D_MIX = D_MODEL
D_LRU = D_MIX // 4
LRU_HEADS = 4
LRU_BLOCK = D_LRU // LRU_HEADS
CONV_W = 4
LRU_C = 8.0
D_SSM = D_MIX // 4
SSM_GROUP = 16
SSM_GROUPS = D_SSM // SSM_GROUP
SSM_STATE = 64
HEAD_DIM = 64
N_Q_HEADS = (D_MIX // 2) // HEAD_DIM
N_KV_HEADS = 2
GQA_GROUP = N_Q_HEADS // N_KV_HEADS
D_ATT = N_Q_HEADS * HEAD_DIM
D_KV = N_KV_HEADS * HEAD_DIM
WINDOW = 128
BAND_PREV = WINDOW // CHUNK
ROPE_THETA = 10000.0
D_IN = 2 * D_LRU + D_SSM + D_ATT + 2 * D_KV
D_FF = 4 * D_MODEL
RMS_EPS = 1e-6
NEG_INF = -1e30

kernel_name = "hybrid_rglru_s5_swa_stream_step"

F32 = jnp.float32


def _rms_norm(x, g):
    xf = x.astype(F32)
    y = xf * lax.rsqrt(jnp.mean(xf * xf, axis=-1, keepdims=True) + RMS_EPS)
    return (y * g.astype(F32)).astype(x.dtype)


def _rope(x, pos):
    half = HEAD_DIM // 2
    inv = ROPE_THETA ** (-jnp.arange(half, dtype=F32) / half)
    ang = pos.astype(F32)[:, None] * inv[None, :]
    cos = jnp.cos(ang)[None, :, None, :]
    sin = jnp.sin(ang)[None, :, None, :]
    xf = x.astype(F32)
    x1, x2 = xf[..., :half], xf[..., half:]
    return jnp.concatenate([x1 * cos - x2 * sin, x2 * cos + x1 * sin], axis=-1).astype(x.dtype)


def _lin_combine(e1, e2):
    a1, b1 = e1
    a2, b2 = e2
    return a1 * a2, a2 * b1 + b2


def _cplx_combine(e1, e2):
    a1r, a1i, b1r, b1i = e1
    a2r, a2i, b2r, b2i = e2
    return (a2r * a1r - a2i * a1i,
            a2r * a1i + a2i * a1r,
            a2r * b1r - a2i * b1i + b2r,
            a2r * b1i + a2i * b1r + b2i)


def _rglru_mixer(u, gate, conv_buf, h0, conv_w, conv_b, w_r, b_r, w_i, b_i, lam):
    B, L = u.shape[0], u.shape[1]
    xx = jnp.concatenate([conv_buf.astype(u.dtype), u], axis=1)
    xc = conv_b.astype(F32)
    for j in range(CONV_W):
        xc = xc + xx[:, j:j + L].astype(F32) * conv_w[j].astype(F32)
    new_buf = xx[:, L:]
    xb = xc.reshape(B, L, LRU_HEADS, LRU_BLOCK)
    r = jax.nn.sigmoid(jnp.einsum('blhi,hij->blhj', xb, w_r.astype(F32)).reshape(B, L, D_LRU) + b_r.astype(F32))
    i = jax.nn.sigmoid(jnp.einsum('blhi,hij->blhj', xb, w_i.astype(F32)).reshape(B, L, D_LRU) + b_i.astype(F32))
    log_a = -LRU_C * r * jax.nn.softplus(-lam.astype(F32))
    a = jnp.exp(log_a)
    b = jnp.sqrt(-jnp.expm1(2.0 * log_a)) * (i * xc)
    b = b.at[:, 0].add(a[:, 0] * h0.astype(F32))
    h = lax.associative_scan(_lin_combine, (a, b), axis=1)[1]
    out = h * jax.nn.gelu(gate.astype(F32))
    return out.astype(u.dtype), new_buf, h[:, -1].astype(u.dtype)


def _s5_mixer(u, s0_re, s0_im, a_re, a_im, b_re, b_im, c_re, c_im, d, log_dt, w_glu, b_glu):
    B, L = u.shape[0], u.shape[1]
    dt = jnp.exp(log_dt.astype(F32))[:, None]
    lr, li = a_re.astype(F32), a_im.astype(F32)
    mag = jnp.exp(lr * dt)
    abar_r, abar_i = mag * jnp.cos(li * dt), mag * jnp.sin(li * dt)
    den = lr * lr + li * li
    nr = abar_r - 1.0
    fr = (nr * lr + abar_i * li) / den
    fi = (abar_i * lr - nr * li) / den
    br, bi = b_re.astype(F32), b_im.astype(F32)
    bb_r = fr[..., None] * br - fi[..., None] * bi
    bb_i = fr[..., None] * bi + fi[..., None] * br
    uf = u.astype(F32)
    ug = uf.reshape(B, L, SSM_GROUPS, SSM_GROUP)
    bu_r = jnp.einsum('blgc,gpc->blgp', ug, bb_r)
    bu_i = jnp.einsum('blgc,gpc->blgp', ug, bb_i)
    sr0, si0 = s0_re.astype(F32), s0_im.astype(F32)
    bu_r = bu_r.at[:, 0].add(abar_r * sr0 - abar_i * si0)
    bu_i = bu_i.at[:, 0].add(abar_r * si0 + abar_i * sr0)
    ar = jnp.broadcast_to(abar_r, bu_r.shape)
    ai = jnp.broadcast_to(abar_i, bu_i.shape)
    _, _, s_r, s_i = lax.associative_scan(_cplx_combine, (ar, ai, bu_r, bu_i), axis=1)
    y = (jnp.einsum('blgp,gcp->blgc', s_r, c_re.astype(F32))
         - jnp.einsum('blgp,gcp->blgc', s_i, c_im.astype(F32))).reshape(B, L, D_SSM)
    y = y + d.astype(F32) * uf
    z = jax.nn.gelu(y)
    out = z * jax.nn.sigmoid(z @ w_glu.astype(F32) + b_glu.astype(F32))
    return out.astype(u.dtype), s_r[:, -1].astype(u.dtype), s_i[:, -1].astype(u.dtype)


def _sink_softmax(scores, sinks, mask=None):
    s = sinks.astype(F32).reshape(N_KV_HEADS, GQA_GROUP, 1, 1)
    if mask is not None:
        scores = jnp.where(mask, scores, NEG_INF)
    m = jnp.maximum(jnp.max(scores, axis=-1, keepdims=True), s)
    e = jnp.exp(scores - m)
    return e / (jnp.sum(e, axis=-1, keepdims=True) + jnp.exp(s - m))


def _swa_prompt(q, k, v, sinks):
    B, L = q.shape[0], q.shape[1]
    nc = L // CHUNK
    scale = HEAD_DIM ** -0.5
    qc = q.astype(F32).reshape(B, nc, CHUNK, N_KV_HEADS, GQA_GROUP, HEAD_DIM)
    kc = k.astype(F32).reshape(B, nc, CHUNK, N_KV_HEADS, HEAD_DIM)
    vc = v.astype(F32).reshape(B, nc, CHUNK, N_KV_HEADS, HEAD_DIM)
    pad = ((0, 0), (BAND_PREV, 0), (0, 0), (0, 0), (0, 0))
    kp, vp = jnp.pad(kc, pad), jnp.pad(vc, pad)
    kb = jnp.concatenate([kp[:, j:j + nc] for j in range(BAND_PREV + 1)], axis=2)
    vb = jnp.concatenate([vp[:, j:j + nc] for j in range(BAND_PREV + 1)], axis=2)
    scores = jnp.einsum('bnqkgd,bnskd->bnkgqs', qc, kb) * scale
    key_chunk = jnp.arange(nc)[:, None] + (jnp.arange((BAND_PREV + 1) * CHUNK) // CHUNK)[None, :] - BAND_PREV
    mask = (key_chunk >= 0)[None, :, None, None, None, :]
    probs = _sink_softmax(scores, sinks, mask)
    out = jnp.einsum('bnkgqs,bnskd->bnqkgd', probs, vb)
    return out.reshape(B, L, D_ATT)


def _swa_cached(q, k, v, cache_k, cache_v, sinks):
    Bd, S = q.shape[0], q.shape[1]
    scale = HEAD_DIM ** -0.5
    kk = jnp.concatenate([cache_k.astype(F32), k.astype(F32)], axis=1)
    vv = jnp.concatenate([cache_v.astype(F32), v.astype(F32)], axis=1)
    qs = q.astype(F32).reshape(Bd, S, N_KV_HEADS, GQA_GROUP, HEAD_DIM)
    scores = jnp.einsum('bqkgd,bskd->bkgqs', qs, kk) * scale
    probs = _sink_softmax(scores, sinks)
    out = jnp.einsum('bkgqs,bskd->bqkgd', probs, vv)
    return out.reshape(Bd, S, D_ATT)


def _hybrid_layer(x, pos, conv_buf, h0, s0_re, s0_im, cache_k, cache_v, p):
    B, L = x.shape[0], x.shape[1]
    hn = _rms_norm(x, p['norm1'])
    proj = hn @ p['w_in']
    o1 = D_LRU
    o2 = o1 + D_LRU
    o3 = o2 + D_SSM
    o4 = o3 + D_ATT
    o5 = o4 + D_KV
    u_a, g_a, u_b = proj[..., :o1], proj[..., o1:o2], proj[..., o2:o3]
    q = proj[..., o3:o4].reshape(B, L, N_Q_HEADS, HEAD_DIM)
    k = proj[..., o4:o5].reshape(B, L, N_KV_HEADS, HEAD_DIM)
    v = proj[..., o5:].reshape(B, L, N_KV_HEADS, HEAD_DIM)

    out_a, conv_new, h_last = _rglru_mixer(u_a, g_a, conv_buf, h0, p['conv_w'], p['conv_b'],
                                           p['w_rg'], p['b_rg'], p['w_ig'], p['b_ig'], p['lru_lambda'])
    out_b, s_re, s_im = _s5_mixer(u_b, s0_re, s0_im, p['ssm_a_re'], p['ssm_a_im'], p['ssm_b_re'],
                                  p['ssm_b_im'], p['ssm_c_re'], p['ssm_c_im'], p['ssm_d'],
                                  p['ssm_log_dt'], p['w_glu'], p['b_glu'])
    q = _rope(q, pos)
    k = _rope(k, pos)
    if cache_k is None:
        out_c = _swa_prompt(q, k, v, p['attn_sinks'])
        k_new, v_new = k[:, -WINDOW:], v[:, -WINDOW:]
    else:
        out_c = _swa_cached(q, k, v, cache_k, cache_v, p['attn_sinks'])
        k_new, v_new = k, v

    mix = jnp.concatenate([out_a, out_b, out_c.astype(x.dtype)], axis=-1) @ p['w_out']
    x = x + mix
    hm = _rms_norm(x, p['norm2'])
    act = jnp.square(jax.nn.relu(hm @ p['w_up']))
    x = x + act @ p['w_down']
    return x, (conv_new, h_last, s_re, s_im, k_new, v_new)


def setup_inputs(seed: int = 0) -> dict:
    key = jax.random.key(seed)
    ks = jax.random.split(key, 40)
    nrm = lambda k, shape, s: jax.random.normal(k, shape, F32) * s
    a0 = jax.random.uniform(ks[10], (DEPTH, D_LRU), F32, 0.9, 0.999) ** (1.0 / LRU_C)
    lru_lambda = jnp.log(a0 / (1.0 - a0))
    ssm_a_re = -0.5 * jnp.exp(nrm(ks[11], (DEPTH, SSM_GROUPS, SSM_STATE), 0.05))
    ssm_a_im = math.pi * jnp.arange(SSM_STATE, dtype=F32)[None, None, :] + nrm(ks[12], (DEPTH, SSM_GROUPS, SSM_STATE), 0.01)
    ssm_log_dt = jax.random.uniform(ks[19], (DEPTH, SSM_GROUPS), F32, math.log(1e-3), math.log(1e-1))
    return {
        'x_prompt': nrm(ks[0], (BATCH, SEQ, D_MODEL), 1.0),
        'x_sample': nrm(ks[1], (DEC_BATCH, DEC_SEQ, D_MODEL), 1.0),
        'cache_conv_a': nrm(ks[2], (DEPTH, DEC_BATCH, CONV_W - 1, D_LRU), 1.0),
        'state_lru': nrm(ks[3], (DEPTH, DEC_BATCH, D_LRU), 0.5),
        'state_ssm_re': nrm(ks[4], (DEPTH, DEC_BATCH, SSM_GROUPS, SSM_STATE), 0.1),
        'state_ssm_im': nrm(ks[5], (DEPTH, DEC_BATCH, SSM_GROUPS, SSM_STATE), 0.1),
        'cache_k': nrm(ks[6], (DEPTH, DEC_BATCH, WINDOW, N_KV_HEADS, HEAD_DIM), 1.0),
        'cache_v': nrm(ks[7], (DEPTH, DEC_BATCH, WINDOW, N_KV_HEADS, HEAD_DIM), 1.0),
        'norm1': 1.0 + nrm(ks[8], (DEPTH, D_MODEL), 0.01),
        'w_in': nrm(ks[9], (DEPTH, D_MODEL, D_IN), D_MODEL ** -0.5),
        'conv_w': nrm(ks[13], (DEPTH, CONV_W, D_LRU), 0.5),
        'conv_b': nrm(ks[14], (DEPTH, D_LRU), 0.01),
        'w_rg': nrm(ks[15], (DEPTH, LRU_HEADS, LRU_BLOCK, LRU_BLOCK), LRU_BLOCK ** -0.5),
        'b_rg': nrm(ks[16], (DEPTH, D_LRU), 0.01),
        'w_ig': nrm(ks[17], (DEPTH, LRU_HEADS, LRU_BLOCK, LRU_BLOCK), LRU_BLOCK ** -0.5),
        'b_ig': nrm(ks[18], (DEPTH, D_LRU), 0.01),
        'lru_lambda': lru_lambda,
        'ssm_a_re': ssm_a_re,
        'ssm_a_im': ssm_a_im,
        'ssm_b_re': nrm(ks[20], (DEPTH, SSM_GROUPS, SSM_STATE, SSM_GROUP), (2 * SSM_GROUP) ** -0.5),
        'ssm_b_im': nrm(ks[21], (DEPTH, SSM_GROUPS, SSM_STATE, SSM_GROUP), (2 * SSM_GROUP) ** -0.5),
        'ssm_c_re': nrm(ks[22], (DEPTH, SSM_GROUPS, SSM_GROUP, SSM_STATE), SSM_STATE ** -0.5),
        'ssm_c_im': nrm(ks[23], (DEPTH, SSM_GROUPS, SSM_GROUP, SSM_STATE), SSM_STATE ** -0.5),
        'ssm_d': nrm(ks[24], (DEPTH, D_SSM), 0.5),
        'ssm_log_dt': ssm_log_dt,
        'w_glu': nrm(ks[25], (DEPTH, D_SSM, D_SSM), D_SSM ** -0.5),
        'b_glu': nrm(ks[26], (DEPTH, D_SSM), 0.01),
        'attn_sinks': nrm(ks[27], (DEPTH, N_Q_HEADS), 0.5),
        'w_out': nrm(ks[28], (DEPTH, D_MIX, D_MODEL), D_MIX ** -0.5),
        'norm2': 1.0 + nrm(ks[29], (DEPTH, D_MODEL), 0.01),
        'w_up': nrm(ks[30], (DEPTH, D_MODEL, D_FF), D_MODEL ** -0.5),
        'w_down': nrm(ks[31], (DEPTH, D_FF, D_MODEL), D_FF ** -0.5),
        'norm_f': 1.0 + nrm(ks[32], (D_MODEL,), 0.01),
    }


def reference(x_prompt, x_sample, cache_conv_a, state_lru, state_ssm_re, state_ssm_im, cache_k, cache_v,
              norm1, w_in, conv_w, conv_b, w_rg, b_rg, w_ig, b_ig, lru_lambda,
              ssm_a_re, ssm_a_im, ssm_b_re, ssm_b_im, ssm_c_re, ssm_c_im, ssm_d, ssm_log_dt,
              w_glu, b_glu, attn_sinks, w_out, norm2, w_up, w_down, norm_f):
    Bp, Lp = x_prompt.shape[0], x_prompt.shape[1]
    Ls = x_sample.shape[1]
    pos_p = jnp.arange(Lp, dtype=jnp.int32)
    pos_s = PAST_LEN + jnp.arange(Ls, dtype=jnp.int32)
    dt = x_prompt.dtype
    zero_conv = jnp.zeros((Bp, CONV_W - 1, D_LRU), dt)
    zero_h = jnp.zeros((Bp, D_LRU), dt)
    zero_s = jnp.zeros((Bp, SSM_GROUPS, SSM_STATE), dt)

    xp, xs = x_prompt, x_sample
    st_p = []
    st_s = []
    for l in range(DEPTH):
        p = dict(norm1=norm1[l], w_in=w_in[l], conv_w=conv_w[l], conv_b=conv_b[l], w_rg=w_rg[l],
                 b_rg=b_rg[l], w_ig=w_ig[l], b_ig=b_ig[l], lru_lambda=lru_lambda[l],
                 ssm_a_re=ssm_a_re[l], ssm_a_im=ssm_a_im[l], ssm_b_re=ssm_b_re[l], ssm_b_im=ssm_b_im[l],
                 ssm_c_re=ssm_c_re[l], ssm_c_im=ssm_c_im[l], ssm_d=ssm_d[l], ssm_log_dt=ssm_log_dt[l],
                 w_glu=w_glu[l], b_glu=b_glu[l], attn_sinks=attn_sinks[l], w_out=w_out[l],
                 norm2=norm2[l], w_up=w_up[l], w_down=w_down[l])
        xp, sp = _hybrid_layer(xp, pos_p, zero_conv, zero_h, zero_s, zero_s, None, None, p)
        xs, ss = _hybrid_layer(xs, pos_s, cache_conv_a[l], state_lru[l], state_ssm_re[l], state_ssm_im[l],
                               cache_k[l], cache_v[l], p)
        st_p.append(sp)
        st_s.append(ss)

    y_prompt = _rms_norm(xp, norm_f)
    y_sample = _rms_norm(xs, norm_f)
    conv_a_p = jnp.stack([s[0] for s in st_p], 0)
    lru_p = jnp.stack([s[1] for s in st_p], 0)
    ssm_re_p = jnp.stack([s[2] for s in st_p], 0)
    ssm_im_p = jnp.stack([s[3] for s in st_p], 0)
    k_p = jnp.stack([s[4] for s in st_p], 0)
    v_p = jnp.stack([s[5] for s in st_p], 0)
    conv_a_s = jnp.stack([s[0] for s in st_s], 0)
    lru_s = jnp.stack([s[1] for s in st_s], 0)
    ssm_re_s = jnp.stack([s[2] for s in st_s], 0)
    ssm_im_s = jnp.stack([s[3] for s in st_s], 0)
    k_s = jnp.stack([s[4] for s in st_s], 0)
    v_s = jnp.stack([s[5] for s in st_s], 0)
    return (y_prompt, y_sample, conv_a_p, lru_p, ssm_re_p, ssm_im_p, k_p, v_p,
            conv_a_s, lru_s, ssm_re_s, ssm_im_s, k_s, v_s)
```

```python
import math
import os
from contextlib import ExitStack

import numpy as np
import concourse.bass as bass
import concourse.mybir as mybir
from concourse.bass_utils import run_bass_kernel_spmd

F32 = mybir.dt.float32
BF16 = mybir.dt.bfloat16
I32 = mybir.dt.int32
AF = mybir.ActivationFunctionType
ALU = mybir.AluOpType

L = 2
D = 1024
SEQ = 4096
NSAMP = 16
NT = 512
NCOL = 2 * SEQ + NSAMP
NCH = 23
CH_IN, CH_OUT, CH_UP, CH_DN = 0, 5, 7, 15
GELU_K = 1.5957691216057308
TWO_PI = 2.0 * math.pi

PV = {}
_off = 0


def _pv(name, n):
    global _off
    PV[name] = _off
    _off += n


for _l in range(L):
    for _nm, _n in (("norm1", 8), ("norm2", 8), ("conv_w", 8), ("conv_b", 2), ("b_rg", 2), ("b_ig", 2),
                    ("lam", 2), ("ssm_d", 2), ("b_glu", 2), ("a_re", 8), ("a_im", 8), ("log_dt", 8),
                    ("sinks", 4)):
        _pv(f"{_nm}{_l}", _n)
_pv("norm_f", 8)
_pv("halfpi", 1)
NPV = _off

AUX = {}
_o = 0
for _nm, _w in (("pv", NPV), ("gbd", L * 2 * 2 * 128), ("bl", L * 2 * 8 * 128), ("cl", L * 2 * 8 * 64),
                ("wglu", L * 2 * 256), ("rope", 2 * (SEQ + NSAMP)), ("jj", 128), ("m16", 64),
                ("conv0", L * 6), ("h0", L * 2), ("s0", L * 16), ("kc", L * 128), ("vc", L * 128)):
    AUX[_nm] = (_o, _w)
    _o += _w
AUXW = _o
OS = {}
_o = 0
for _nm, _w in (("o_conv", L * 3 * 6), ("o_h", L * 3 * 2), ("o_s", L * 3 * 16), ("o_k", L * 272), ("o_v", L * 3 * 128)):
    OS[_nm] = (_o, _w)
    _o += _w
OSW = _o

EPOCH = 8000
NDSEM = 16


class Op:
    __slots__ = ("eng", "fn", "deps", "sig", "dma", "sem", "val", "cover")


class Prog:
    ENGS = ("tensor", "vector", "scalar", "gpsimd", "sync")

    def __init__(self):
        self.q = {e: [] for e in self.ENGS}
        self.state = {}

    def op(self, eng, fn, r=(), w=(), sig=True, dma=False):
        o = Op()
        o.eng, o.fn, o.sig, o.dma = eng, fn, sig, dma
        o.sem = o.val = o.cover = None
        deps = []
        for k in r:
            st = self.state.setdefault(k, [[], []])
            deps.extend(st[0])
            st[1].append(o)
        for k in w:
            st = self.state.setdefault(k, [[], []])
            rd = [x for x in st[1] if x is not o]
            if rd or (o in st[1]):
                deps.extend(rd)
                deps.extend(st[0])
                st[0] = [o]
                st[1] = []
            else:
                for x in st[0]:
                    if x.eng != eng or x.dma:
                        deps.append(x)
                st[0].append(o)
                if len(st[0]) > 64:
                    st[0] = st[0][-64:]
        seen = set()
        dd = []
        for x in deps:
            if x is o or id(x) in seen:
                continue
            seen.add(id(x))
            if x.eng == "tensor" and eng == "tensor" and not x.dma:
                continue
            dd.append(x)
        o.deps = dd
        self.q[eng].append(o)
        return o

    def emit(self, nc, es):
        csem = {}
        for e in ("tensor", "vector", "scalar", "gpsimd"):
            ops = [o for o in self.q[e] if not o.dma]
            nsig = sum(1 for o in ops if o.sig)
            nep = max(1, (nsig + EPOCH - 1) // EPOCH)
            csem[e] = [es.enter_context(nc.semaphore(f"c_{e}_{i}")) for i in range(nep)]
            cnt = 0
            for o in ops:
                if o.sig:
                    o.sem = csem[e][cnt // EPOCH]
                    o.val = cnt % EPOCH + 1
                    cnt += 1
            nxt = None
            for o in reversed(ops):
                if o.sig:
                    nxt = o
                    o.cover = o
                else:
                    assert nxt is not None, "unsignaled trailing op"
                    o.cover = nxt
        for e in self.ENGS:
            dops = [o for o in self.q[e] if o.dma]
            if not dops:
                continue
            pool = [es.enter_context(nc.semaphore(f"d_{e}_{i}")) for i in range(NDSEM)]
            for i, o in enumerate(dops):
                o.sem = pool[i % NDSEM]
                o.val = 16 * (i // NDSEM + 1)
                o.cover = o
                if i >= NDSEM:
                    o.deps.append(dops[i - NDSEM])
        block = es.enter_context(nc.Block())
        prog = self

        def run(engobj, ename):
            maxw = {}
            for o in prog.q[ename]:
                need = {}
                for d in o.deps:
                    c = d.cover
                    key = id(c.sem)
                    if maxw.get(key, 0) >= c.val:
                        continue
                    if key not in need or need[key][1] < c.val:
                        need[key] = (c.sem, c.val)
                for key, (sem_, val_) in need.items():
                    engobj.wait_ge(sem_, val_)
                    maxw[key] = val_
                ins = o.fn(engobj)
                if o.dma:
                    ins.then_inc(o.sem, 16)
                elif o.sig:
                    ins.then_inc(o.sem, 1)

        @block.tensor
        def _(t):
            run(t, "tensor")

        @block.vector
        def _(v):
            run(v, "vector")

        @block.scalar
        def _(s):
            run(s, "scalar")

        @block.gpsimd
        def _(g):
            run(g, "gpsimd")

        @block.sync
        def _(sy):
            run(sy, "sync")


class _Stop(Exception):
    pass


def build_nc(n_tiles_limit=None, stop=None, tile_sel=None):
    nc = bass.Bass("TRN2", target_bir_lowering=False)

    def chk(name):
        if stop == name:
            raise _Stop()
    dr = lambda n, s, dt=F32, kind="ExternalInput": nc.dram_tensor(n, list(s), dt, kind=kind).ap()
    xT = dr("xT", [8, 128, NCOL])
    wf = dr("wf", [L * NCH, 128, 4096])
    aux = dr("aux", [128, AUXW])
    ax = lambda nm: aux[:, AUX[nm][0]:AUX[nm][0] + AUX[nm][1]]
    pvd, gbd_d, bl_d, cl_d, wglu_d = ax("pv"), ax("gbd"), ax("bl"), ax("cl"), ax("wglu")
    rope_d = ax("rope").rearrange("p (a b) -> p a b", a=2)
    jj_d, m16_d, conv0_d, h0_d, s0_d, kc_d, vc_d = (ax(k_) for k_ in ("jj", "m16", "conv0", "h0", "s0", "kc", "vc"))
    yT = dr("yT", [8, 128, NCOL], kind="ExternalOutput")
    osm = dr("osm", [128, OSW], kind="ExternalOutput")
    ox = lambda nm: osm[:, OS[nm][0]:OS[nm][0] + OS[nm][1]]
    o_conv, o_h, o_s, o_k, o_v = ox("o_conv"), ox("o_h"), ox("o_s"), ox("o_k"), ox("o_v")
    wb = dr("wb", [L * NCH, 128, 4096], BF16, kind="Internal")

    P = Prog()
    es = ExitStack()
    sb = lambda n, s, dt=F32: es.enter_context(nc.sbuf_tensor(n, list(s), dt))

    x = sb("x", [128, 8, NT])
    hn = sb("hn", [128, 8, NT], BF16)
    sqb = [sb(f"sq{i}", [128, NT], BF16) for i in range(2)]
    ua = sb("ua", [128, 2, 3 + NT])
    ga = sb("ga", [128, 2, NT], BF16)
    ub = sb("ub", [128, 2, NT])
    ubb = sb("ubb", [128, 2, NT], BF16)
    tA = [sb(f"tA{i}", [128, NT]) for i in range(2)]
    qT = sb("qT", [128, 4, NT], BF16)
    kT = [sb(f"kT{l}", [128, 128 + NT], BF16) for l in range(L)]
    vv = [sb(f"vv{l}", [128, 5, 128], BF16) for l in range(L)]
    vf = sb("vf", [128, 128])
    mix = sb("mix", [128, 8, NT], BF16)
    act = sb("act", [128, 32, NT], BF16)
    NW = 3
    wring = [sb(f"wr{i}", [128, 8, 512], BF16) for i in range(NW)]
    s5t = [sb(f"s5t{i}", [128, NT]) for i in range(8)]
    rstd, kf = s5t[1], s5t[3]
    tA = tA + [s5t[7], s5t[2]]
    rr, ig, hh = s5t[4], s5t[5], s5t[6]
    zre = sb("zre", [128, NT])
    zim = sb("zim", [128, NT])
    Sre = sb("Sre", [128, NT], BF16)
    Sim = sb("Sim", [128, NT], BF16)
    Tc = [sb(f"Tc{l}", [128, 8, 128]) for l in range(L)]
    Ts = [sb(f"Ts{l}", [128, 8, 128]) for l in range(L)]
    Uc = [sb(f"Uc{l}", [128, 8, 128]) for l in range(L)]
    Us = [sb(f"Us{l}", [128, 8, 128]) for l in range(L)]
    magt = [sb(f"magt{l}", [128, 8, 128]) for l in range(L)]
    ET = [sb(f"ET{i}", [128, 2, 256], BF16) for i in range(2)]
    rcp = sb("rcp", [128, 256])
    ropeb = sb("ropeb", [128, 2, NT])
    pv = sb("pvs", [128, NPV])
    jj = sb("jjs", [128, 128])
    gbd = sb("gbds", [128, L * 2 * 2 * 128], BF16)
    blb = sb("blb", [128, L * 2 * 8 * 128], BF16)
    clb = sb("clb", [128, L * 2 * 8 * 64], BF16)
    wglu = sb("wglus", [128, L * 2 * 256], BF16)
    ones_bf = sb("ones_bf", [128, 128], BF16)
    m16b = sb("m16b", [128, 64], BF16)
    c8 = sb("c8", [128, L * 2])
    c16 = sb("c16", [128, L * 2])
    esink = sb("esink", [128, L * 4])
    hst = sb("hst", [128, L * 2])
    s5s = sb("s5s", [128, L * 8 * 2])
    xcb = sb("xcb", [128, NT], BF16)
    zbf = sb("zbf", [128, 2, NT], BF16)
    tiny = sb("tiny", [128, 8])
    uahist = [sb(f"uahist{l}", [128, 2, 3]) for l in range(L)]
    ps = es.enter_context(nc.psum_tensor("ps", [128, 8, 512], F32))

    V, A, G, T, S = "vector", "scalar", "gpsimd", "tensor", "sync"
    _cfg = os.environ.get("S5CFG", "GGGG")
    E3, E7, E8, E9 = (V if c == "V" else G for c in _cfg)

    def dma(out, in_, r, w, eng=S, **kw):
        return P.op(eng, lambda e: e.dma_start(out=out, in_=in_, **kw), r=r, w=w, dma=True)

    def act_(out, in_, func, r, w, bias=None, scale=None):
        kw = {}
        if bias is not None:
            kw["bias"] = bias
        if scale is not None:
            kw["scale"] = scale
        return P.op(A, lambda e: e.activation(out=out, in_=in_, func=func, **kw), r=r, w=w)

    def tt(eng, out, a, b, op, r, w):
        return P.op(eng, lambda e: e.tensor_tensor(out, a, b, op), r=r, w=w)

    def ts(eng, out, a, s1, s2, op0, op1, r, w):
        if s2 is None:
            return P.op(eng, lambda e: e.tensor_scalar(out, a, s1, None, op0), r=r, w=w)
        return P.op(eng, lambda e: e.tensor_scalar(out, a, s1, s2, op0, op1), r=r, w=w)

    def stt(out, a, s, b, op0, op1, r, w):
        return P.op(V, lambda e: e.scalar_tensor_tensor(out, a, s, b, op0, op1), r=r, w=w)

    def cp(eng, out, in_, r, w):
        if eng == A:
            return P.op(A, lambda e: e.copy(out, in_), r=r, w=w)
        return P.op(eng, lambda e: e.tensor_copy(out, in_), r=r, w=w)

    def scan(out, d0, d1, init, r, w):
        return P.op(V, lambda e: e.tensor_tensor_scan(out, d0, d1, init, ALU.mult, ALU.add), r=r, w=w)

    def recip(out, in_, r, w):
        return P.op(V, lambda e: e.reciprocal(out, in_), r=r, w=w)

    def memset(eng, ap_, val, w):
        return P.op(eng, lambda e: e.memset(ap_, val), w=w)

    def mm(out, lhsT, rhs, start, stop, r, w, sig):
        return P.op(T, lambda e: e.matmul(out, lhsT, rhs, start=start, stop=stop), r=r, w=w, sig=sig)

    for c in range(L * NCH):
        dma(wb[c], wf[c], r=(("castq", c % 4),), w=(("wb", c), ("castq", c % 4)), eng=G, max_dma_last_dim=2048)

    dma(pv[:, :], pvd[:, :], (), ("pv",))
    dma(jj[:, :], jj_d[:, :], (), ("jj",))
    xflat = x[:, :, :].rearrange("p a b -> p (a b)")
    XK = tuple(("x", k) for k in range(8))
    dma(xflat[:, 0:L * 512], gbd_d[:, :], (), XK)
    cp(V, gbd[:, :], xflat[:, 0:L * 512], XK, ("gbd",))
    dma(xflat[:, 0:L * 512], wglu_d[:, :], XK, XK)
    cp(V, wglu[:, :], xflat[:, 0:L * 512], XK, ("wglu",))
    dma(xflat[:, 0:L * 1024], cl_d[:, :], XK, XK)
    for l in range(L):
        o0 = l * 1024
        cp(V, clb[:, o0:o0 + 512], xflat[:, o0:o0 + 512], XK, ("clb",))
        ts(V, clb[:, o0 + 512:o0 + 1024], xflat[:, o0 + 512:o0 + 1024], -1.0, None, ALU.mult, None, XK, ("clb",))
    for l in range(L):
        dma(xflat[:, 0:2048], bl_d[:, l * 2048:(l + 1) * 2048], XK + ("clb",), XK)
        cp(V, blb[:, l * 2048:(l + 1) * 2048], xflat[:, 0:2048], XK, ("blb",))
    memset(V, ones_bf[:, :], 1.0, ("ones",))
    dma(xflat[:, 0:64], m16_d[:, :], XK + ("blb",), XK)
    cp(V, m16b[:, :], xflat[:, 0:64], XK, ("m16b",))
    for l in range(L):
        lam = pv[:, PV[f"lam{l}"]:PV[f"lam{l}"] + 2]
        act_(tA[0][:, 0:2], lam, AF.Exp, ("pv",), ("tA0",), scale=-1.0)
        act_(tA[0][:, 2:4], tA[0][:, 0:2], AF.Ln, ("tA0",), ("tA0",), bias=1.0)
        ts(V, c8[:, 2 * l:2 * l + 2], tA[0][:, 2:4], -8.0, None, ALU.mult, None, ("tA0",), ("c8",))
        ts(V, c16[:, 2 * l:2 * l + 2], tA[0][:, 2:4], -16.0, None, ALU.mult, None, ("tA0",), ("c8",))
        sk = pv[:, PV[f"sinks{l}"]:PV[f"sinks{l}"] + 4]
        act_(esink[:, 4 * l:4 * l + 4], sk, AF.Exp, ("pv",), ("esink",))
    hp = pv[:, PV["halfpi"]:PV["halfpi"] + 1]
    for l in range(L):
        are = pv[:, PV[f"a_re{l}"]:PV[f"a_re{l}"] + 8]
        aim = pv[:, PV[f"a_im{l}"]:PV[f"a_im{l}"] + 8]
        ldt = pv[:, PV[f"log_dt{l}"]:PV[f"log_dt{l}"] + 8]
        sm = tA[1]
        k_ = ("tA1",)
        dtv, lrdt, magv, th = sm[:, 0:8], sm[:, 8:16], sm[:, 16:24], sm[:, 24:32]
        act_(dtv, ldt, AF.Exp, ("pv",), k_)
        tt(V, lrdt, are, dtv, ALU.mult, ("pv",) + k_, k_)
        act_(magv, lrdt, AF.Exp, k_, k_)
        tt(V, th, aim, dtv, ALU.mult, ("pv",) + k_, k_)
        ang = [s5t[0], s5t[1]]
        def view(tl):
            return [tl[i][:, :].rearrange("p (a b) -> p a b", a=4) for i in range(2)]
        angv = view(ang)
        for pr in range(8):
            ts(V, angv[pr // 4][:, pr % 4, :], jj[:, :], th[:, pr:pr + 1], None, ALU.mult, None,
               ("jj",) + k_, ("s5t0", "s5t1"))
            ts(V, magt[l][:, pr, :], jj[:, :], 0.0, magv[:, pr:pr + 1], ALU.mult, ALU.add, ("jj",) + k_, (f"magt{l}",))
        Ucv = [Uc[l][:, 0:4, :], Uc[l][:, 4:8, :]]
        Usv = [Us[l][:, 0:4, :], Us[l][:, 4:8, :]]
        for hf in range(2):
            a_ = angv[hf]
            t1 = view([s5t[2], s5t[3]])[hf]
            t2 = view([s5t[4], s5t[5]])[hf]
            t3 = view([s5t[6], s5t[7]])[hf]
            ti = s5t[2 + hf][:, :].bitcast(I32).rearrange("p (a b) -> p a b", a=4)
            kk = tuple(f"s5t{i}" for i in range(8))
            ts(V, t2, a_, 1.0 / TWO_PI, None, ALU.mult, None, kk, kk)
            cp(V, ti, t2, kk, kk)
            cp(V, t2, ti, kk, kk)
            stt(t3, t2, -TWO_PI, a_, ALU.mult, ALU.add, kk, kk)
            act_(t1, t3, AF.Sin, kk, kk, scale=0.25)
            act_(t2, t3, AF.Sin, kk, kk, scale=0.25, bias=hp)
            stt(t3, t1, 2.0, t2, ALU.mult, ALU.mult, kk, kk)
            tt(V, t2, t1, t1, ALU.mult, kk, kk)
            ts(V, t2, t2, -2.0, 1.0, ALU.mult, ALU.add, kk, kk)
            stt(Usv[hf], t3, 2.0, t2, ALU.mult, ALU.mult, kk, (f"U{l}",))
            tt(V, t1, t3, t3, ALU.mult, kk, kk)
            ts(V, Ucv[hf], t1, -2.0, 1.0, ALU.mult, ALU.add, kk, (f"U{l}",))
        abr, abi, den, nr_, fr, fi, w1, w2 = (sm[:, 32 + 8 * i:40 + 8 * i] for i in range(8))
        kU = (f"U{l}",)
        tt(V, abr, magv, Uc[l][:, :, 0], ALU.mult, k_ + kU, k_)
        tt(V, abi, magv, Us[l][:, :, 0], ALU.mult, k_ + kU, k_)
        tt(V, den, are, are, ALU.mult, ("pv",), k_)
        tt(V, w1, aim, aim, ALU.mult, ("pv",), k_)
        tt(V, den, den, w1, ALU.add, k_, k_)
        recip(den, den, k_, k_)
        ts(V, nr_, abr, -1.0, None, ALU.add, None, k_, k_)
        tt(V, w1, nr_, are, ALU.mult, k_ + ("pv",), k_)
        tt(V, w2, abi, aim, ALU.mult, k_ + ("pv",), k_)
        tt(V, w1, w1, w2, ALU.add, k_, k_)
        tt(V, fr, w1, den, ALU.mult, k_, k_)
        tt(V, w1, abi, are, ALU.mult, k_ + ("pv",), k_)
        tt(V, w2, nr_, aim, ALU.mult, k_ + ("pv",), k_)
        tt(V, w1, w1, w2, ALU.subtract, k_, k_)
        tt(V, fi, w1, den, ALU.mult, k_, k_)
        for pr in range(8):
            tmp = tA[2][:, 0:128]
            ts(V, tmp, Us[l][:, pr, :], fi[:, pr:pr + 1], None, ALU.mult, None, k_ + kU, ("s5t7",))
            stt(Tc[l][:, pr, :], Uc[l][:, pr, :], fr[:, pr:pr + 1], tmp, ALU.mult, ALU.add, k_ + kU + ("s5t7",), (f"T{l}",))
            ts(V, tmp, Us[l][:, pr, :], fr[:, pr:pr + 1], None, ALU.mult, None, k_ + kU, ("s5t7",))
            stt(Ts[l][:, pr, :], Uc[l][:, pr, :], fi[:, pr:pr + 1], tmp, ALU.mult, ALU.subtract, k_ + kU + ("s5t7",), (f"T{l}",))

    wstate = {"n": 0}
    wsched = []

    def wload(ci_global):
        slot = wstate["n"] % NW
        wstate["n"] += 1
        dma(wring[slot][:, :, :].rearrange("p a b -> p (a b)"), wb[ci_global], r=(("wb", ci_global),), w=(("w", slot),))
        return slot

    tiles = []
    for s in range(2):
        for j in range(SEQ // NT):
            tiles.append((s, j * NT, NT, s * SEQ + j * NT, j * NT))
    tiles.append((2, 0, NSAMP, 2 * SEQ, SEQ))
    if n_tiles_limit is not None:
        tiles = tiles[:n_tiles_limit]
    if tile_sel is not None:
        tiles = [tiles[i] for i in tile_sel]
    out_dmas = []

    order = []
    for ti in range(len(tiles)):
        for l in range(L):
            order.extend(l * NCH + c for c in range(NCH))
    wq = {"issued": 0, "slots": []}

    def wprefetch(upto):
        while wq["issued"] < min(upto, len(order)):
            wq["slots"].append(wload(order[wq["issued"]]))
            wq["issued"] += 1

    wcur = {"i": 0}

    def wnext():
        i = wcur["i"]
        wprefetch(i + NW)
        wcur["i"] += 1
        return wq["slots"][i]

    psn = {"i": 0}

    def bank4():
        b = psn["i"] % 4
        psn["i"] += 1
        return b

    def rmsnorm(n, gcol, dst_hn):
        b = 4 + (psn["i"] % 2)
        psn["i"] += 1
        for k in range(8):
            sq = sqb[k % 2]
            act_(sq[:, 0:n], x[:, k, 0:n], AF.Square, (("x", k),), (f"sq{k % 2}",))
            mm(ps[:, b, 0:n], ones_bf[:, :], sq[:, 0:n], k == 0, k == 7, (f"sq{k % 2}", "ones"), (("ps", b),), True)
        act_(rstd[:, 0:n], ps[:, b, 0:n], AF.Ln, (("ps", b),), ("s5t1",), scale=1.0 / D, bias=1e-6)
        act_(rstd[:, 0:n], rstd[:, 0:n], AF.Exp, ("s5t1",), ("s5t1",), scale=-0.5)
        for k in range(8):
            if dst_hn:
                stt(hn[:, k, 0:n], x[:, k, 0:n], pv[:, gcol + k:gcol + k + 1], rstd[:, 0:n], ALU.mult, ALU.mult,
                    (("x", k), "pv", "s5t1"), (("hn", k),))
            else:
                yv = act[:, 2 * k:2 * k + 2, :].rearrange("p a b -> p (a b)").bitcast(F32)
                stt(yv[:, 0:n], x[:, k, 0:n], pv[:, gcol + k:gcol + k + 1], rstd[:, 0:n], ALU.mult, ALU.mult,
                    (("x", k), "pv", "s5t1"), (("act", 2 * k), ("act", 2 * k + 1)))

    def gelu_from(src, srckeys, n, dst, dstkeys, tmpi):
        a, b_ = tA[tmpi], tA[tmpi + 1]
        ka, kb = (f"tA{tmpi}",), (f"tA{tmpi + 1}",)
        act_(a[:, 0:n], src, AF.Square, srckeys, ka)
        ts(V, a[:, 0:n], a[:, 0:n], 0.044715, 1.0, ALU.mult, ALU.add, ka, ka)
        tt(V, b_[:, 0:n], src, a[:, 0:n], ALU.mult, srckeys + ka, kb)
        act_(a[:, 0:n], b_[:, 0:n], AF.Sigmoid, kb, ka, scale=GELU_K)
        tt(V, dst, src, a[:, 0:n], ALU.mult, srckeys + ka, dstkeys)

    def main_loop():
      for ti, (sq_id, t0, n, col0, pos0) in enumerate(tiles):
        first = (t0 == 0)
        last = (t0 + n == (SEQ if sq_id < 2 else NSAMP))
        samp = (sq_id == 2)
        nsub = (n + 127) // 128
        if samp:
            memset(V, hn[:, :, NSAMP:128], 0.0, tuple(("hn", k) for k in range(8)))
            for l_ in range(L):
                memset(V, kT[l_][:, 128 + NSAMP:256], 0.0, (f"kT{l_}",))
        for k in range(8):
            dma(x[:, k, 0:n], xT[k, :, col0:col0 + n], (), (("x", k),))
        dma(ropeb[:, :, 0:n], rope_d[:, :, pos0:pos0 + n], (), ("ropeb",))
        for l in range(L):
            pvl = lambda nm: PV[f"{nm}{l}"]
            if first:
                if samp:
                    dma(uahist[l][:, :, :], conv0_d[:, l * 6:(l + 1) * 6].rearrange("p (t j) -> p t j", t=2), (), (f"uahs{l}",))
                    dma(hst[:, 2 * l:2 * l + 2], h0_d[:, 2 * l:2 * l + 2], (), (f"hst{l}",))
                    dma(s5s[:, 16 * l:16 * l + 16], s0_d[:, 16 * l:16 * l + 16], (), (f"s5s{l}",))
                    dma(tA[3][:, 0:128], kc_d[:, l * 128:(l + 1) * 128], (), ("s5t2",))
                    cp(V, kT[l][:, 0:128], tA[3][:, 0:128], ("s5t2",), (f"kT{l}",))
                    dma(tA[3][:, 128:256], vc_d[:, l * 128:(l + 1) * 128], ("s5t2",), ("s5t2",))
                    cp(V, vv[l][:, 0, :], tA[3][:, 128:256], ("s5t2",), (f"vv{l}",))
                else:
                    memset(V, uahist[l][:, :, :], 0.0, (f"uahs{l}",))
                    memset(V, hst[:, 2 * l:2 * l + 2], 0.0, (f"hst{l}",))
                    memset(V, s5s[:, 16 * l:16 * l + 16], 0.0, (f"s5s{l}",))
            rmsnorm(n, pvl("norm1"), True)
            chk("norm1" if l == 0 else f"norm1_{l}")
            cp(A, ua[:, :, 0:3], uahist[l][:, :, :], (f"uahs{l}",), ("uah",))
            wslot = None
            for ci in range(4):
                wslot = wnext()
                wt = wring[wslot]
                for mi in range(4):
                    if ci in (2, 3) and mi % 2 == 1:
                        continue
                    mlist = [mi] if ci < 2 else [mi, mi + 1]
                    banks = []
                    for m in mlist:
                        b = bank4()
                        banks.append(b)
                        for k in range(8):
                            mm(ps[:, b, 0:n], wt[:, k, m * 128:(m + 1) * 128], hn[:, k, 0:n], k == 0, k == 7,
                               (("w", wslot), ("hn", k)), (("ps", b),), k == 7)
                    b = banks[0]
                    src = ps[:, b, 0:n]
                    sk = (("ps", b),)
                    if ci == 0 and mi < 2:
                        cp(A, ua[:, mi, 3:3 + n], src, sk, (f"ua{mi}",))
                    elif ci == 0:
                        gelu_from(src, sk, n, ga[:, mi - 2, 0:n], (f"ga{mi - 2}",), 0)
                    elif ci == 1 and mi < 2:
                        cp(A, ub[:, mi, 0:n], src, sk, (f"ub{mi}",))
                        cp(V, ubb[:, mi, 0:n], src, sk, (f"ubb{mi}",))
                    elif ci == 1 and mi == 2:
                        kb0 = b
                    elif ci == 1 and mi == 3:
                        b1 = b
                        tt(V, ps[:, kb0, 0:n], ps[:, kb0, 0:n], ropeb[:, 0, 0:n], ALU.mult, (("ps", kb0), "ropeb"), (("ps", kb0),))
                        tt(V, tA[1][:, 0:n], ps[:, b1, 0:n], ropeb[:, 1, 0:n], ALU.mult, (("ps", b1), "ropeb"), ("tA1",))
                        tt(V, kf[:, 0:n], ps[:, kb0, 0:n], tA[1][:, 0:n], ALU.add, (("ps", kb0), "tA1"), ("s5t3",))
                        cp(A, kT[l][:, 128:128 + n], kf[:, 0:n], ("s5t3",), (f"kT{l}",))
                        if last:
                            nk = min(n, 128)
                            oc = l * 272 + (sq_id * 128)
                            out_dmas.append(dma(o_k[:, oc:oc + nk], kf[:, n - nk:n], ("s5t3",), ()))
                    else:
                        tq = (ci - 2) * 2 + mi // 2
                        b0, b1 = banks
                        tt(V, ps[:, b0, 0:n], ps[:, b0, 0:n], ropeb[:, 0, 0:n], ALU.mult, (("ps", b0), "ropeb"), (("ps", b0),))
                        tt(V, tA[1][:, 0:n], ps[:, b1, 0:n], ropeb[:, 1, 0:n], ALU.mult, (("ps", b1), "ropeb"), ("tA1",))
                        tt(V, qT[:, tq, 0:n], ps[:, b0, 0:n], tA[1][:, 0:n], ALU.add, (("ps", b0), "tA1"), ("qT",))
            chk("inproj_a")
            wslot = wnext()
            wt = wring[wslot]
            b = bank4()
            for sbi in range(nsub):
                m_ = 128
                for k in range(8):
                    mm(ps[0:m_, b, sbi * 128:(sbi + 1) * 128], hn[:, k, sbi * 128:sbi * 128 + m_], wt[:, k, 0:128],
                       k == 0, k == 7, (("w", wslot), ("hn", k)), (("ps", b),), k == 7)
            cp(A, vv[l][:, 1:1 + nsub, :], ps[:, b, 0:nsub * 128].rearrange("p (a b) -> p a b", b=128), (("ps", b),), (f"vv{l}",))
            chk("v_mm")
            if last:
                vfb, vfk = tA[0][:, 0:128], "tA0"
                cp(A, vfb, ps[:, b, (nsub - 1) * 128:nsub * 128], (("ps", b),), (vfk,))
                oc = (l * 3 + sq_id) * 128
                out_dmas.append(dma(o_v[:, oc:oc + 128], vfb, (vfk,), ()))

            chk("inproj" if l == 0 else f"inproj_{l}")
            def actf(slot):
                return act[:, slot:slot + 2, :].rearrange("p a b -> p (a b)").bitcast(F32), (("act", slot), ("act", slot + 1))

            zf_ap = [actf(30), actf(0)]
            L_xc, K_xc = actf(0)
            L_rr, K_rr = actf(2)
            L_ig, K_ig = actf(4)
            L_hh, K_hh = actf(6)
            s5set = [dict(t=[s5t[i] for i in range(8)], tk=[(f"s5t{i}",) for i in range(8)],
                          zre=zre, zim=zim, kzre=("zre",), kzim=("zim",), Sre=Sre, Sim=Sim, kSre=("Sre",), kSim=("Sim",),
                          tiny=tiny[:, 0:4], ktiny=("tiny0",), banks=(0, 1)),
                     dict(t=[actf(8 + 2 * i)[0] for i in range(8)], tk=[actf(8 + 2 * i)[1] for i in range(8)],
                          zre=actf(24)[0], zim=actf(26)[0], kzre=actf(24)[1], kzim=actf(26)[1],
                          Sre=act[:, 28, :], Sim=act[:, 29, :], kSre=(("act", 28),), kSim=(("act", 29),),
                          tiny=tiny[:, 4:8], ktiny=("tiny1",), banks=(2, 3))]

            def lru_gen():
                for t in range(2):
                    cw = pvl("conv_w") + 4 * t
                    kua = (f"ua{t}", "uah")
                    xc, kx = L_xc, K_xc
                    ts(V, xc[:, 0:n], ua[:, t, 3:3 + n], pv[:, cw + 3:cw + 4], pv[:, pvl("conv_b") + t:pvl("conv_b") + t + 1],
                       ALU.mult, ALU.add, kua + ("pv",), kx)
                    for j in range(3):
                        stt(xc[:, 0:n], ua[:, t, j:j + n], pv[:, cw + j:cw + j + 1], xc[:, 0:n], ALU.mult, ALU.add,
                            kua + ("pv",) + kx, kx)
                        yield
                    cp(A, xcb[:, 0:n], xc[:, 0:n], kx, ("xcb",))
                    yield
                    g0 = ((l * 2 + 0) * 2 + t) * 128
                    g1 = ((l * 2 + 1) * 2 + t) * 128
                    bg = 7
                    mm(ps[:, bg, 0:n], gbd[:, g0:g0 + 128], xcb[:, 0:n], True, True, ("gbd", "xcb"), (("ps", bg),), True)
                    act_(L_rr[:, 0:n], ps[:, bg, 0:n], AF.Sigmoid, (("ps", bg), "pv"), K_rr,
                         bias=pv[:, pvl("b_rg") + t:pvl("b_rg") + t + 1])
                    yield
                    mm(ps[:, bg, 0:n], gbd[:, g1:g1 + 128], xcb[:, 0:n], True, True, ("gbd", "xcb"), (("ps", bg),), True)
                    act_(L_ig[:, 0:n], ps[:, bg, 0:n], AF.Sigmoid, (("ps", bg), "pv"), K_ig,
                         bias=pv[:, pvl("b_ig") + t:pvl("b_ig") + t + 1])
                    yield
                    aa, a2 = tA[0], tA[1]
                    act_(aa[:, 0:n], L_rr[:, 0:n], AF.Exp, K_rr + ("c8",), ("tA0",), scale=c8[:, 2 * l + t:2 * l + t + 1])
                    act_(a2[:, 0:n], L_rr[:, 0:n], AF.Exp, K_rr + ("c8",), ("tA1",), scale=c16[:, 2 * l + t:2 * l + t + 1])
                    yield
                    act_(a2[:, 0:n], a2[:, 0:n], AF.Sqrt, ("tA1",), ("tA1",), scale=-1.0, bias=1.0)
                    tt(V, L_ig[:, 0:n], L_ig[:, 0:n], xc[:, 0:n], ALU.mult, K_ig + kx, K_ig)
                    yield
                    tt(V, L_ig[:, 0:n], L_ig[:, 0:n], a2[:, 0:n], ALU.mult, K_ig + ("tA1",), K_ig)
                    yield
                    scan(L_hh[:, 0:n], aa[:, 0:n], L_ig[:, 0:n], hst[:, 2 * l + t:2 * l + t + 1],
                         ("tA0", f"hst{l}") + K_ig, K_hh)
                    yield
                    cp(A, hst[:, 2 * l + t:2 * l + t + 1], L_hh[:, n - 1:n], K_hh, (f"hst{l}",))
                    tt(V, mix[:, t, 0:n], L_hh[:, 0:n], ga[:, t, 0:n], ALU.mult, K_hh + (f"ga{t}",), (("mix", t),))
                    yield
                if last:
                    oc = (l * 3 + sq_id) * 6
                    out_dmas.append(dma(o_conv[:, oc:oc + 6].rearrange("p (t j) -> p t j", t=2), ua[:, :, n:n + 3],
                                        ("ua0", "ua1", "uah"), ()))
                    oc = (l * 3 + sq_id) * 2
                    out_dmas.append(dma(o_h[:, oc:oc + 2], hst[:, 2 * l:2 * l + 2], (f"hst{l}",), ()))
                cp(A, uahist[l][:, :, :], ua[:, :, n:n + 3], ("ua0", "ua1", "uah"), (f"uahs{l}",))

            nsb = max(1, n // 128)
            sl = min(n, 128)
            YB = 6

            def s5_pair_steps(pr, S_):
                ut = pr // 4
                b_re, b_im = S_["banks"]
                q, qk = S_["t"], S_["tk"]
                z_re, z_im, kzr, kzi = S_["zre"], S_["zim"], S_["kzre"], S_["kzim"]
                kT_, kU_ = (f"T{l}",), (f"U{l}",)
                ks = (f"s5s{l}",)
                sre = s5s[:, 16 * l + 2 * pr:16 * l + 2 * pr + 1]
                sim = s5s[:, 16 * l + 2 * pr + 1:16 * l + 2 * pr + 2]

                def v3(ap_):
                    return ap_.rearrange("p (a b) -> p a b", b=sl)

                def tb(tab):
                    return tab[:, pr, 0:sl].unsqueeze(1).broadcast_to([128, nsb, sl])

                steps = []

                def st0():
                    o_re = ((l * 2 + 0) * 8 + pr) * 128
                    o_im = ((l * 2 + 1) * 8 + pr) * 128
                    mm(ps[:, b_re, 0:n], blb[:, o_re:o_re + 128], ubb[:, ut, 0:n], True, True, ("blb", f"ubb{ut}"), (("ps", b_re),), True)
                    mm(ps[:, b_im, 0:n], blb[:, o_im:o_im + 128], ubb[:, ut, 0:n], True, True, ("blb", f"ubb{ut}"), (("ps", b_im),), True)
                steps.append(st0)

                kbr, kbi = (("ps", b_re),), (("ps", b_im),)

                def st1():
                    tt(V, v3(q[1][:, 0:n]), v3(ps[:, b_im, 0:n]), tb(Ts[l]), ALU.mult, kbi + kT_, qk[1])
                    tt(V, v3(q[3][:, 0:n]), v3(ps[:, b_re, 0:n]), tb(Ts[l]), ALU.mult, kbr + kT_, qk[3])
                steps.append(st1)

                def st2():
                    tt(V, v3(ps[:, b_re, 0:n]), v3(ps[:, b_re, 0:n]), tb(Tc[l]), ALU.mult, kbr + kT_, kbr)
                    tt(V, v3(ps[:, b_im, 0:n]), v3(ps[:, b_im, 0:n]), tb(Tc[l]), ALU.mult, kbi + kT_, kbi)
                steps.append(st2)

                def st3():
                    tt(V, q[0][:, 0:n], ps[:, b_re, 0:n], q[1][:, 0:n], ALU.subtract, kbr + qk[1], qk[0])
                    tt(V, q[2][:, 0:n], ps[:, b_im, 0:n], q[3][:, 0:n], ALU.add, kbi + qk[3], qk[2])
                steps.append(st3)
                for sbi in range(nsb):
                    c0, c1 = sbi * sl, (sbi + 1) * sl

                    def sa(c0=c0, c1=c1):
                        scan(ps[:, b_re, c0:c1], magt[l][:, pr, 0:sl], q[0][:, c0:c1], sre, qk[0] + (f"magt{l}",) + ks, kbr)
                        scan(ps[:, b_im, c0:c1], magt[l][:, pr, 0:sl], q[2][:, c0:c1], sim, qk[2] + (f"magt{l}",) + ks, kbi)
                    steps.append(sa)

                    def sb_(c0=c0, c1=c1):
                        ucl = Uc[l][:, pr, sl - 1:sl]
                        usl = Us[l][:, pr, sl - 1:sl]
                        w1 = S_["tiny"][:, 0:1]
                        w2 = S_["tiny"][:, 1:2]
                        kt = S_["ktiny"]
                        tt(V, w1, ps[:, b_im, c1 - 1:c1], usl, ALU.mult, kbi + kU_, kt)
                        tt(V, w2, ps[:, b_re, c1 - 1:c1], usl, ALU.mult, kbr + kU_, kt)
                        stt(sre, ps[:, b_re, c1 - 1:c1], ucl, w1, ALU.mult, ALU.subtract, kbr + kt + kU_, ks)
                        stt(sim, ps[:, b_im, c1 - 1:c1], ucl, w2, ALU.mult, ALU.add, kbi + kt + kU_, ks)
                    steps.append(sb_)

                def st7():
                    tt(V, v3(q[5][:, 0:n]), v3(ps[:, b_im, 0:n]), tb(Us[l]), ALU.mult, kbi + kU_, qk[5])
                    tt(V, v3(q[7][:, 0:n]), v3(ps[:, b_re, 0:n]), tb(Us[l]), ALU.mult, kbr + kU_, qk[7])
                steps.append(st7)

                def st8():
                    tt(V, v3(ps[:, b_re, 0:n]), v3(ps[:, b_re, 0:n]), tb(Uc[l]), ALU.mult, kbr + kU_, kbr)
                    tt(V, v3(ps[:, b_im, 0:n]), v3(ps[:, b_im, 0:n]), tb(Uc[l]), ALU.mult, kbi + kU_, kbi)
                steps.append(st8)

                def st9():
                    tt(V, S_["Sre"][:, 0:n], ps[:, b_re, 0:n], q[5][:, 0:n], ALU.subtract, kbr + qk[5], S_["kSre"])
                    tt(V, S_["Sim"][:, 0:n], ps[:, b_im, 0:n], q[7][:, 0:n], ALU.add, kbi + qk[7], S_["kSim"])
                steps.append(st9)

                def st10():
                    oc_re = ((l * 2 + 0) * 8 + pr) * 64
                    oc_im = ((l * 2 + 1) * 8 + pr) * 64
                    po = 64 * ((pr % 4) // 2)
                    yo = ps[po:po + 64, YB, 0:n]
                    mm(yo, clb[:, oc_re:oc_re + 64], S_["Sre"][:, 0:n], pr % 2 == 0, False, ("clb",) + S_["kSre"], (("ps", YB),), False)
                    mm(yo, clb[:, oc_im:oc_im + 64], S_["Sim"][:, 0:n], False, pr % 2 == 1, ("clb",) + S_["kSim"], (("ps", YB),), True)
                steps.append(st10)
                return steps

            def s5_gen():
                for grp in range(4):
                    sa_ = s5_pair_steps(2 * grp, s5set[0])
                    sb2 = s5_pair_steps(2 * grp + 1, s5set[1])
                    for fa, fb in zip(sa_, sb2):
                        fa()
                        fb()
                        yield
                    if grp % 2 == 1:
                        t = grp // 2
                        stt(zf_ap[t][0][:, 0:n], ub[:, t, 0:n], pv[:, pvl("ssm_d") + t:pvl("ssm_d") + t + 1], ps[:, YB, 0:n],
                            ALU.mult, ALU.add, (f"ub{t}", "pv", ("ps", YB)), zf_ap[t][1])
                        yield
                if last:
                    oc = (l * 3 + sq_id) * 16
                    out_dmas.append(dma(o_s[:, oc:oc + 16], s5s[:, 16 * l:16 * l + 16], (f"s5s{l}",), ()))

            if samp:
                chunks = [(0, n, [(0, 0, 128), (1, 0, 128)])]
            else:
                chunks = []
                for c in range(n // 64):
                    if c % 2 == 0:
                        pcs = [(c // 2, 0, 128), (c // 2 + 1, 0, 64)]
                    else:
                        pcs = [((c - 1) // 2, 64, 128), ((c + 1) // 2, 0, 128)]
                    if first:
                        pcs = [p_ for p_ in pcs if p_[0] >= 1]
                    chunks.append((c * 64, 64, pcs))

            def attn_gen():
                for (qc0, nq, pcs) in chunks:
                    nqh = 4 * nq
                    bnd = 7
                    for kvh in range(2):
                        hp_ = slice(kvh * 64, kvh * 64 + 64)
                        et = ET[kvh]
                        bs = 4 + kvh
                        for pi, (vt, p0, p1) in enumerate(pcs):
                            kc0 = vt * 128 + p0
                            mm(ps[p0:p1, bs, pi * 256:pi * 256 + nqh].rearrange("p (a b) -> p a b", a=4),
                               kT[l][hp_, kc0:kc0 + (p1 - p0)], qT[hp_, :, qc0:qc0 + nq], True, True,
                               (f"kT{l}", "qT"), (("ps", bs),), True)
                            act_(et[p0:p1, pi, 0:nqh], ps[p0:p1, bs, pi * 256:pi * 256 + nqh], AF.Exp, (("ps", bs),), (f"ET{kvh}",), scale=0.125)
                        yield
                    for kvh in range(2):
                        hp_ = slice(kvh * 64, kvh * 64 + 64)
                        et = ET[kvh]
                        for pi, (vt, p0, p1) in enumerate(pcs):
                            stt_, stp = (pi == 0), (pi == len(pcs) - 1)
                            mm(ps[hp_, bnd, 0:nqh], vv[l][p0:p1, vt, kvh * 64:kvh * 64 + 64], et[p0:p1, pi, 0:nqh], stt_, stp,
                               (f"vv{l}", f"ET{kvh}"), (("ps", bnd),), stp)
                        for pi, (vt, p0, p1) in enumerate(pcs):
                            stt_, stp = (pi == 0), (pi == len(pcs) - 1)
                            onesl = m16b[p0:p1, 0:64] if (samp and pi == 1) else ones_bf[p0:p1, 0:64]
                            mm(ps[hp_, bnd, 256:256 + nqh], onesl, et[p0:p1, pi, 0:nqh], stt_, stp,
                               ("ones", "m16b", f"ET{kvh}"), (("ps", bnd),), stp)
                        yield
                    r3 = rcp[:, 0:nqh].rearrange("p (a b) -> p a b", a=4)
                    for t4 in range(4):
                        act_(rcp[:, t4 * nq:(t4 + 1) * nq], ps[:, bnd, 256 + t4 * nq:256 + (t4 + 1) * nq], AF.Ln,
                             (("ps", bnd), "esink"), ("rcp",), bias=esink[:, 4 * l + t4:4 * l + t4 + 1])
                    act_(rcp[:, 0:nqh], rcp[:, 0:nqh], AF.Exp, ("rcp",), ("rcp",), scale=-1.0)
                    tt(V, mix[:, 4:8, qc0:qc0 + nq], ps[:, bnd, 0:nqh].rearrange("p (a b) -> p a b", a=4), r3, ALU.mult,
                       (("ps", bnd), "rcp"), tuple(("mix", 4 + i) for i in range(4)))
                    yield

            gens = [s5_gen(), attn_gen(), lru_gen()]
            while gens:
                for g_ in list(gens):
                    try:
                        next(g_)
                    except StopIteration:
                        gens.remove(g_)
            chk("s5" if l == 0 else f"s5_{l}")
            for t in range(2):
                gelu_from(zf_ap[t][0][:, 0:n], zf_ap[t][1], n, zf_ap[t][0][:, 0:n], zf_ap[t][1], 0)
                cp(A, zbf[:, t, 0:n], zf_ap[t][0][:, 0:n], zf_ap[t][1], ("zbf",))
            for t in range(2):
                b = bank4()
                for k in range(2):
                    o_ = (l * 2 + k) * 256 + t * 128
                    mm(ps[:, b, 0:n], wglu[:, o_:o_ + 128], zbf[:, k, 0:n], k == 0, k == 1, ("wglu", "zbf"), (("ps", b),), k == 1)
                act_(tA[0][:, 0:n], ps[:, b, 0:n], AF.Sigmoid, (("ps", b), "pv"), ("tA0",),
                     bias=pv[:, pvl("b_glu") + t:pvl("b_glu") + t + 1])
                tt(V, mix[:, 2 + t, 0:n], zf_ap[t][0][:, 0:n], tA[0][:, 0:n], ALU.mult, zf_ap[t][1] + ("tA0",), (("mix", 2 + t),))
            chk("attn" if l == 0 else f"attn_{l}")
            if not last:
                cp(A, kT[l][:, 0:128], kT[l][:, n:n + 128], (f"kT{l}",), (f"kT{l}",))
                cp(A, vv[l][:, 0, :], vv[l][:, 4, :], (f"vv{l}",), (f"vv{l}",))

            for ci in range(2):
                wslot = wnext()
                wt = wring[wslot]
                for mi in range(4):
                    m = ci * 4 + mi
                    b = bank4()
                    for k in range(8):
                        mm(ps[:, b, 0:n], wt[:, k, mi * 128:(mi + 1) * 128], mix[:, k, 0:n], k == 0, k == 7,
                           (("w", wslot), ("mix", k)), (("ps", b),), k == 7)
                    tt(V, x[:, m, 0:n], x[:, m, 0:n], ps[:, b, 0:n], ALU.add, (("x", m), ("ps", b)), (("x", m),))
            chk("outproj" if l == 0 else f"outproj_{l}")
            rmsnorm(n, pvl("norm2"), True)
            for ci in range(8):
                wslot = wnext()
                wt = wring[wslot]
                for mi in range(4):
                    b = bank4()
                    for k in range(8):
                        mm(ps[:, b, 0:n], wt[:, k, mi * 128:(mi + 1) * 128], hn[:, k, 0:n], k == 0, k == 7,
                           (("w", wslot), ("hn", k)), (("ps", b),), k == 7)
                    tmp = tA[mi % 2]
                    act_(tmp[:, 0:n], ps[:, b, 0:n], AF.Relu, (("ps", b),), (f"tA{mi % 2}",))
                    tt(V, act[:, ci * 4 + mi, 0:n], ps[:, b, 0:n], tmp[:, 0:n], ALU.mult, (("ps", b), f"tA{mi % 2}"),
                       (("act", ci * 4 + mi),))
            for mh in range(2):
                for kq in range(4):
                    wslot = wnext()
                    wt = wring[wslot]
                    for mi in range(4):
                        b = mh * 4 + mi
                        for k in range(8):
                            kk = kq * 8 + k
                            mm(ps[:, b, 0:n], wt[:, k, mi * 128:(mi + 1) * 128], act[:, kk, 0:n], kk == 0, kk == 31,
                               (("w", wslot), ("act", kk)), (("ps", b),), kk == 31 or k == 7)
                for mi in range(4):
                    b = mh * 4 + mi
                    m = mh * 4 + mi
                    tt(V, x[:, m, 0:n], x[:, m, 0:n], ps[:, b, 0:n], ALU.add, (("x", m), ("ps", b)), (("x", m),))
            psn["i"] = 0
            chk(f"layer{l}")
        rmsnorm(n, PV["norm_f"], False)
        chk("fnorm")
        for k in range(8):
            yv = act[:, 2 * k:2 * k + 2, :].rearrange("p a b -> p (a b)").bitcast(F32)
            out_dmas.append(dma(yT[k, :, col0:col0 + n], yv[:, 0:n], (("act", 2 * k), ("act", 2 * k + 1)), (("yT", k),)))

    try:
        chk("prologue")
        main_loop()
    except _Stop:
        pass
    fin = Op()
    fin.eng, fin.fn, fin.sig, fin.dma = S, (lambda e: e.nop()), False, False
    fin.deps = list(out_dmas)
    fin.sem = fin.val = fin.cover = None
    P.q[S].append(fin)
    P.emit(nc, es)
    es.close()
    return nc


_NC_CACHE = {}


def _rope_tables():
    half = 32
    inv = (10000.0 ** (-np.arange(half, dtype=np.float32) / half)).astype(np.float32)
    pos = np.concatenate([np.arange(SEQ), SEQ + np.arange(NSAMP)]).astype(np.float32)
    ang = pos[None, :] * inv[:, None]
    cos = np.cos(ang).astype(np.float32)
    sin = np.sin(ang).astype(np.float32)
    d = np.arange(128) % 64
    i = d % 32
    sign = np.where(d < 32, -1.0, 1.0).astype(np.float32)
    tab = np.empty((128, 2, SEQ + NSAMP), np.float32)
    tab[:, 0, :] = cos[i]
    tab[:, 1, :] = sin[i] * sign[:, None]
    return tab


def _prep_shared(inp):
    f = lambda a: np.asarray(a, dtype=np.float32)
    w_in, w_out, w_up, w_down = f(inp["w_in"]), f(inp["w_out"]), f(inp["w_up"]), f(inp["w_down"])
    o3, o4, o5 = 768, 1280, 1408
    hd = np.arange(64)
    sw = (hd + 32) % 64
    qcols = lambda h: o3 + 64 * h + hd
    qscols = lambda h: o3 + 64 * h + sw
    cols = []
    cols += list(range(0, 256))
    cols += list(range(256, 512))
    cols += list(range(512, 768))
    cols += list(range(o4, o4 + 128))
    cols += list(np.concatenate([o4 + 64 * kv + sw for kv in range(2)]))
    for t in range(4):
        cols += list(np.concatenate([qcols(t), qcols(4 + t)]))
        cols += list(np.concatenate([qscols(t), qscols(4 + t)]))
    cols += list(range(o5, o5 + 128))
    cols = np.asarray(cols)
    assert cols.shape[0] == 2176
    rows_out = list(range(0, 512))
    for t in range(4):
        rows_out += list(512 + 64 * t + hd) + list(512 + 64 * (4 + t) + hd)
    rows_out = np.asarray(rows_out)
    wf = np.zeros((L * NCH, 128, 8, 512), np.float32)
    for l in range(L):
        we = w_in[l][:, cols].reshape(8, 128, 2176)
        for ci in range(4):
            wf[l * NCH + ci] = we[:, :, 512 * ci:512 * ci + 512].transpose(1, 0, 2)
        wf[l * NCH + 4][:, :, 0:128] = we[:, :, 2048:2176].transpose(1, 0, 2)
        wo = w_out[l][rows_out, :].reshape(8, 128, 1024)
        for ci in range(2):
            wf[l * NCH + CH_OUT + ci] = wo[:, :, 512 * ci:512 * ci + 512].transpose(1, 0, 2)
        wu = w_up[l].reshape(8, 128, 4096)
        for ci in range(8):
            wf[l * NCH + CH_UP + ci] = wu[:, :, 512 * ci:512 * ci + 512].transpose(1, 0, 2)
        wd = w_down[l].reshape(32, 128, 1024)
        for mh in range(2):
            for kq in range(4):
                wf[l * NCH + CH_DN + mh * 4 + kq] = wd[8 * kq:8 * kq + 8, :, 512 * mh:512 * mh + 512].transpose(1, 0, 2)
    wf = wf.reshape(L * NCH, 128, 4096)

    pvv = np.zeros((128, NPV), np.float32)
    t128 = lambda v, n: f(v).reshape(n, 128).T
    for l in range(L):
        pvv[:, PV[f"norm1{l}"]:PV[f"norm1{l}"] + 8] = t128(inp["norm1"][l], 8)
        pvv[:, PV[f"norm2{l}"]:PV[f"norm2{l}"] + 8] = t128(inp["norm2"][l], 8)
        cw = f(inp["conv_w"][l])
        for t in range(2):
            pvv[:, PV[f"conv_w{l}"] + 4 * t:PV[f"conv_w{l}"] + 4 * t + 4] = cw[:, t * 128:(t + 1) * 128].T
        for nm in ("conv_b", "b_rg", "b_ig", "ssm_d", "b_glu"):
            pvv[:, PV[f"{nm}{l}"]:PV[f"{nm}{l}"] + 2] = t128(inp[nm][l], 2)
        pvv[:, PV[f"lam{l}"]:PV[f"lam{l}"] + 2] = t128(inp["lru_lambda"][l], 2)
        for nm, key in (("a_re", "ssm_a_re"), ("a_im", "ssm_a_im")):
            a = f(inp[key][l]).reshape(8, 2, 64)
            pvv[:, PV[f"{nm}{l}"]:PV[f"{nm}{l}"] + 8] = a.transpose(1, 2, 0).reshape(128, 8)
        ld = f(inp["ssm_log_dt"][l]).reshape(8, 2)
        pvv[:, PV[f"log_dt{l}"]:PV[f"log_dt{l}"] + 8] = np.repeat(ld.T[:, None, :], 64, axis=1).reshape(128, 8)
        sk = f(inp["attn_sinks"][l]).reshape(2, 4)
        pvv[:, PV[f"sinks{l}"]:PV[f"sinks{l}"] + 4] = np.repeat(sk[:, None, :], 64, axis=1).reshape(128, 4)
    pvv[:, PV["norm_f"]:PV["norm_f"] + 8] = t128(inp["norm_f"], 8)
    pvv[:, PV["halfpi"]] = np.float32(math.pi / 2)

    gbd = np.zeros((128, L, 2, 2, 128), np.float32)
    for l in range(L):
        for ri, key in enumerate(("w_rg", "w_ig")):
            w = f(inp[key][l])
            for t in range(2):
                for h2 in range(2):
                    gbd[h2 * 64:(h2 + 1) * 64, l, ri, t, h2 * 64:(h2 + 1) * 64] = w[2 * t + h2]
    gbd = gbd.reshape(128, -1)

    bl = np.zeros((128, L, 2, 8, 128), np.float32)
    cl = np.zeros((128, L, 2, 8, 64), np.float32)
    for l in range(L):
        for ri, (kb, kc) in enumerate((("ssm_b_re", "ssm_c_re"), ("ssm_b_im", "ssm_c_im"))):
            b = f(inp[kb][l])
            c = f(inp[kc][l])
            for pr in range(8):
                for g2 in range(2):
                    g = 2 * pr + g2
                    gs = g % 8
                    bl[gs * 16:(gs + 1) * 16, l, ri, pr, g2 * 64:(g2 + 1) * 64] = b[g].T
                    cl[g2 * 64:(g2 + 1) * 64, l, ri, pr, (pr % 2) * 32 + g2 * 16:(pr % 2) * 32 + (g2 + 1) * 16] = c[g].T
    bl = bl.reshape(128, -1)
    cl = cl.reshape(128, -1)
    wglu = np.zeros((128, L, 2, 256), np.float32)
    for l in range(L):
        wglu[:, l] = f(inp["w_glu"][l]).reshape(2, 128, 256).transpose(1, 0, 2)
    wglu = wglu.reshape(128, -1)
    jj = np.repeat(np.arange(1, 129, dtype=np.float32)[None, :], 128, axis=0)
    m16 = np.zeros((128, 64), np.float32)
    m16[0:NSAMP, :] = 1.0
    return dict(wf=wf, pv=pvv, gbd=gbd, bl=bl, cl=cl, wglu=wglu, rope=_rope_tables().reshape(128, -1), jj=jj, m16=m16)


def kernel(**inp):
    f = lambda a: np.asarray(a, dtype=np.float32)
    shared = _prep_shared(inp)
    xp, xs = f(inp["x_prompt"]), f(inp["x_sample"])
    in_maps = []
    for c in range(8):
        xt = np.empty((8, 128, NCOL), np.float32)
        xt[:, :, 0:SEQ] = xp[2 * c].T.reshape(8, 128, SEQ)
        xt[:, :, SEQ:2 * SEQ] = xp[2 * c + 1].T.reshape(8, 128, SEQ)
        xt[:, :, 2 * SEQ:] = xs[c].T.reshape(8, 128, NSAMP)
        conv0 = np.zeros((128, L, 2, 3), np.float32)
        h0 = np.zeros((128, L, 2), np.float32)
        s0 = np.zeros((128, L, 8, 2), np.float32)
        kc = np.zeros((128, L, 128), np.float32)
        vc = np.zeros((128, L, 128), np.float32)
        for l in range(L):
            cc = f(inp["cache_conv_a"][l, c])
            conv0[:, l] = cc.T.reshape(2, 128, 3).transpose(1, 0, 2)
            h0[:, l] = f(inp["state_lru"][l, c]).reshape(2, 128).T
            for ri, key in enumerate(("state_ssm_re", "state_ssm_im")):
                s = f(inp[key][l, c]).reshape(8, 2, 64)
                s0[:, l, :, ri] = s.transpose(1, 2, 0).reshape(128, 8)
            kc[:, l] = f(inp["cache_k"][l, c]).reshape(128, 128).T
            vc[:, l] = f(inp["cache_v"][l, c]).reshape(128, 128)
        parts = dict(shared)
        parts.update(conv0=conv0.reshape(128, -1), h0=h0.reshape(128, -1), s0=s0.reshape(128, -1),
                     kc=kc.reshape(128, -1), vc=vc.reshape(128, -1))
        auxa = np.empty((128, AUXW), np.float32)
        for nm, (o_, w_) in AUX.items():
            auxa[:, o_:o_ + w_] = parts[nm]
        in_maps.append(dict(xT=xt, wf=shared["wf"], aux=auxa))
    if "nc" not in _NC_CACHE:
        _NC_CACHE["nc"] = build_nc()
    nc = _NC_CACHE["nc"]
    res = run_bass_kernel_spmd(nc, in_maps, core_ids=list(range(8)))
    return _assemble(res.results)


def _assemble(results):
    B, Bd = 16, 8
    y_p = np.empty((B, SEQ, D), np.float32)
    y_s = np.empty((Bd, NSAMP, D), np.float32)
    conv_p = np.empty((L, B, 3, 256), np.float32)
    lru_p = np.empty((L, B, 256), np.float32)
    sre_p = np.empty((L, B, 16, 64), np.float32)
    sim_p = np.empty((L, B, 16, 64), np.float32)
    k_p = np.empty((L, B, 128, 2, 64), np.float32)
    v_p = np.empty((L, B, 128, 2, 64), np.float32)
    conv_s = np.empty((L, Bd, 3, 256), np.float32)
    lru_s = np.empty((L, Bd, 256), np.float32)
    sre_s = np.empty((L, Bd, 16, 64), np.float32)
    sim_s = np.empty((L, Bd, 16, 64), np.float32)
    k_s = np.empty((L, Bd, NSAMP, 2, 64), np.float32)
    v_s = np.empty((L, Bd, NSAMP, 2, 64), np.float32)
    for c, r in enumerate(results):
        yT = np.asarray(r["yT"]).reshape(D, NCOL)
        y_p[2 * c] = yT[:, 0:SEQ].T
        y_p[2 * c + 1] = yT[:, SEQ:2 * SEQ].T
        y_s[c] = yT[:, 2 * SEQ:].T
        osm = np.asarray(r["osm"])
        og = lambda nm: osm[:, OS[nm][0]:OS[nm][0] + OS[nm][1]]
        oc = og("o_conv").reshape(128, L, 3, 2, 3)
        oh = og("o_h").reshape(128, L, 3, 2)
        os_ = og("o_s").reshape(2, 64, L, 3, 8, 2)
        ok = og("o_k").reshape(128, L, 272)
        ov = og("o_v").reshape(128, L, 3, 128)
        for l in range(L):
            for s in range(3):
                conv = oc[:, l, s].transpose(2, 1, 0).reshape(3, 256)
                hv = oh[:, l, s].T.reshape(256)
                st = os_[:, :, l, s]
                st = st.transpose(2, 0, 1, 3).reshape(16, 64, 2)
                if s < 2:
                    b = 2 * c + s
                    conv_p[l, b], lru_p[l, b] = conv, hv
                    sre_p[l, b], sim_p[l, b] = st[:, :, 0], st[:, :, 1]
                    k_p[l, b] = ok[:, l, s * 128:(s + 1) * 128].T.reshape(128, 2, 64)
                    v_p[l, b] = ov[:, l, s].reshape(128, 2, 64)
                else:
                    conv_s[l, c], lru_s[l, c] = conv, hv
                    sre_s[l, c], sim_s[l, c] = st[:, :, 0], st[:, :, 1]
                    k_s[l, c] = ok[:, l, 256:272].T.reshape(NSAMP, 2, 64)
                    v_s[l, c] = ov[0:NSAMP, l, s].reshape(NSAMP, 2, 64)
    return (y_p, y_s, conv_p, lru_p, sre_p, sim_p, k_p, v_p, conv_s, lru_s, sre_s, sim_s, k_s, v_s)
```

```python
import math
import os
from contextlib import ExitStack

import numpy as np
import concourse.bass as bass
import concourse.mybir as mybir
from concourse.bass_utils import run_bass_kernel_spmd

F32 = mybir.dt.float32
BF16 = mybir.dt.bfloat16
I32 = mybir.dt.int32
AF = mybir.ActivationFunctionType
ALU = mybir.AluOpType

L = 2
D = 1024
SEQ = 4096
NSAMP = 16
NT = 512
NCOL = 2 * SEQ + NSAMP
NCH = 23
CH_IN, CH_OUT, CH_UP, CH_DN = 0, 5, 7, 15
GELU_K = 1.5957691216057308
TWO_PI = 2.0 * math.pi

PV = {}
_off = 0


def _pv(name, n):
    global _off
    PV[name] = _off
    _off += n


for _l in range(L):
    for _nm, _n in (("norm1", 8), ("norm2", 8), ("conv_w", 8), ("conv_b", 2), ("b_rg", 2), ("b_ig", 2),
                    ("lam", 2), ("ssm_d", 2), ("b_glu", 2), ("a_re", 8), ("a_im", 8), ("log_dt", 8),
                    ("sinks", 4)):
        _pv(f"{_nm}{_l}", _n)
_pv("norm_f", 8)
_pv("halfpi", 1)
NPV = _off

AUX = {}
_o = 0
for _nm, _w in (("pv", NPV), ("gbd", L * 2 * 2 * 128), ("bl", L * 2 * 8 * 128), ("cl", L * 2 * 8 * 64),
                ("wglu", L * 2 * 256), ("rope", 2 * (SEQ + NSAMP)), ("jj", 128), ("m16", 64),
                ("conv0", L * 6), ("h0", L * 2), ("s0", L * 16), ("kc", L * 128), ("vc", L * 128)):
    AUX[_nm] = (_o, _w)
    _o += _w
AUXW = _o
OS = {}
_o = 0
for _nm, _w in (("o_conv", L * 3 * 6), ("o_h", L * 3 * 2), ("o_s", L * 3 * 16), ("o_k", L * 272), ("o_v", L * 3 * 128)):
    OS[_nm] = (_o, _w)
    _o += _w
OSW = _o

EPOCH = 8000
NDSEM = 16


class Op:
    __slots__ = ("eng", "fn", "deps", "sig", "dma", "sem", "val", "cover")


class Prog:
    ENGS = ("tensor", "vector", "scalar", "gpsimd", "sync")

    def __init__(self):
        self.q = {e: [] for e in self.ENGS}
        self.state = {}

    def op(self, eng, fn, r=(), w=(), sig=True, dma=False):
        o = Op()
        o.eng, o.fn, o.sig, o.dma = eng, fn, sig, dma
        o.sem = o.val = o.cover = None
        deps = []
        for k in r:
            st = self.state.setdefault(k, [[], []])
            deps.extend(st[0])
            st[1].append(o)
        for k in w:
            st = self.state.setdefault(k, [[], []])
            rd = [x for x in st[1] if x is not o]
            if rd or (o in st[1]):
                deps.extend(rd)
                deps.extend(st[0])
                st[0] = [o]
                st[1] = []
            else:
                for x in st[0]:
                    if x.eng != eng or x.dma:
                        deps.append(x)
                st[0].append(o)
                if len(st[0]) > 64:
                    st[0] = st[0][-64:]
        seen = set()
        dd = []
        for x in deps:
            if x is o or id(x) in seen:
                continue
            seen.add(id(x))
            if x.eng == "tensor" and eng == "tensor" and not x.dma:
                continue
            dd.append(x)
        o.deps = dd
        self.q[eng].append(o)
        return o

    def emit(self, nc, es):
        csem = {}
        for e in ("tensor", "vector", "scalar", "gpsimd"):
            ops = [o for o in self.q[e] if not o.dma]
            nsig = sum(1 for o in ops if o.sig)
            nep = max(1, (nsig + EPOCH - 1) // EPOCH)
            csem[e] = [es.enter_context(nc.semaphore(f"c_{e}_{i}")) for i in range(nep)]
            cnt = 0
            for o in ops:
                if o.sig:
                    o.sem = csem[e][cnt // EPOCH]
                    o.val = cnt % EPOCH + 1
                    cnt += 1
            nxt = None
            for o in reversed(ops):
                if o.sig:
                    nxt = o
                    o.cover = o
                else:
                    assert nxt is not None, "unsignaled trailing op"
                    o.cover = nxt
        for e in self.ENGS:
            dops = [o for o in self.q[e] if o.dma]
            if not dops:
                continue
            pool = [es.enter_context(nc.semaphore(f"d_{e}_{i}")) for i in range(NDSEM)]
            for i, o in enumerate(dops):
                o.sem = pool[i % NDSEM]
                o.val = 16 * (i // NDSEM + 1)
                o.cover = o
                if i >= NDSEM:
                    o.deps.append(dops[i - NDSEM])
        block = es.enter_context(nc.Block())
        prog = self

        def run(engobj, ename):
            maxw = {}
            for o in prog.q[ename]:
                need = {}
                for d in o.deps:
                    c = d.cover
                    key = id(c.sem)
                    if maxw.get(key, 0) >= c.val:
                        continue
                    if key not in need or need[key][1] < c.val:
                        need[key] = (c.sem, c.val)
                for key, (sem_, val_) in need.items():
                    engobj.wait_ge(sem_, val_)
                    maxw[key] = val_
                ins = o.fn(engobj)
                if o.dma:
                    ins.then_inc(o.sem, 16)
                elif o.sig:
                    ins.then_inc(o.sem, 1)

        @block.tensor
        def _(t):
            run(t, "tensor")

        @block.vector
        def _(v):
            run(v, "vector")

        @block.scalar
        def _(s):
            run(s, "scalar")

        @block.gpsimd
        def _(g):
            run(g, "gpsimd")

        @block.sync
        def _(sy):
            run(sy, "sync")


class _Stop(Exception):
    pass


def build_nc(n_tiles_limit=None, stop=None, tile_sel=None):
    nc = bass.Bass("TRN2", target_bir_lowering=False)

    def chk(name):
        if stop == name:
            raise _Stop()
    dr = lambda n, s, dt=F32, kind="ExternalInput": nc.dram_tensor(n, list(s), dt, kind=kind).ap()
    xT = dr("xT", [8, 128, NCOL])
    wf = dr("wf", [L * NCH, 128, 4096])
    aux = dr("aux", [128, AUXW])
    ax = lambda nm: aux[:, AUX[nm][0]:AUX[nm][0] + AUX[nm][1]]
    pvd, gbd_d, bl_d, cl_d, wglu_d = ax("pv"), ax("gbd"), ax("bl"), ax("cl"), ax("wglu")
    rope_d = ax("rope").rearrange("p (a b) -> p a b", a=2)
    jj_d, m16_d, conv0_d, h0_d, s0_d, kc_d, vc_d = (ax(k_) for k_ in ("jj", "m16", "conv0", "h0", "s0", "kc", "vc"))
    yT = dr("yT", [8, 128, NCOL], kind="ExternalOutput")
    osm = dr("osm", [128, OSW], kind="ExternalOutput")
    ox = lambda nm: osm[:, OS[nm][0]:OS[nm][0] + OS[nm][1]]
    o_conv, o_h, o_s, o_k, o_v = ox("o_conv"), ox("o_h"), ox("o_s"), ox("o_k"), ox("o_v")
    wb = dr("wb", [L * NCH, 128, 4096], BF16, kind="Internal")

    P = Prog()
    es = ExitStack()
    sb = lambda n, s, dt=F32: es.enter_context(nc.sbuf_tensor(n, list(s), dt))

    x = sb("x", [128, 8, NT])
    hn = sb("hn", [128, 8, NT], BF16)
    sqb = [sb(f"sq{i}", [128, NT], BF16) for i in range(2)]
    ua = sb("ua", [128, 2, 3 + NT])
    ga = sb("ga", [128, 2, NT], BF16)
    ub = sb("ub", [128, 2, NT])
    ubb = sb("ubb", [128, 2, NT], BF16)
    tA = [sb(f"tA{i}", [128, NT]) for i in range(2)]
    qT = sb("qT", [128, 4, NT], BF16)
    kT = [sb(f"kT{l}", [128, 128 + NT], BF16) for l in range(L)]
    vv = [sb(f"vv{l}", [128, 5, 128], BF16) for l in range(L)]
    vf = sb("vf", [128, 128])
    mix = sb("mix", [128, 8, NT], BF16)
    act = sb("act", [128, 32, NT], BF16)
    NW = 3
    wring = [sb(f"wr{i}", [128, 8, 512], BF16) for i in range(NW)]
    s5t = [sb(f"s5t{i}", [128, NT]) for i in range(8)]
    rstd, kf = s5t[1], s5t[3]
    tA = tA + [s5t[7], s5t[2]]
    rr, ig, hh = s5t[4], s5t[5], s5t[6]
    zre = sb("zre", [128, NT])
    zim = sb("zim", [128, NT])
    Sre = sb("Sre", [128, NT], BF16)
    Sim = sb("Sim", [128, NT], BF16)
    Tc = [sb(f"Tc{l}", [128, 8, 128]) for l in range(L)]
    Ts = [sb(f"Ts{l}", [128, 8, 128]) for l in range(L)]
    Uc = [sb(f"Uc{l}", [128, 8, 128]) for l in range(L)]
    Us = [sb(f"Us{l}", [128, 8, 128]) for l in range(L)]
    magt = [sb(f"magt{l}", [128, 8, 128]) for l in range(L)]
    ET = [sb(f"ET{i}", [128, 2, 256], BF16) for i in range(2)]
    rcp = sb("rcp", [128, 256])
    ropeb = sb("ropeb", [128, 2, NT])
    pv = sb("pvs", [128, NPV])
    jj = sb("jjs", [128, 128])
    gbd = sb("gbds", [128, L * 2 * 2 * 128], BF16)
    blb = sb("blb", [128, L * 2 * 8 * 128], BF16)
    clb = sb("clb", [128, L * 2 * 8 * 64], BF16)
    wglu = sb("wglus", [128, L * 2 * 256], BF16)
    ones_bf = sb("ones_bf", [128, 128], BF16)
    m16b = sb("m16b", [128, 64], BF16)
    c8 = sb("c8", [128, L * 2])
    c16 = sb("c16", [128, L * 2])
    esink = sb("esink", [128, L * 4])
    hst = sb("hst", [128, L * 2])
    s5s = sb("s5s", [128, L * 8 * 2])
    xcb = sb("xcb", [128, NT], BF16)
    zbf = sb("zbf", [128, 2, NT], BF16)
    tiny = sb("tiny", [128, 8])
    uahist = [sb(f"uahist{l}", [128, 2, 3]) for l in range(L)]
    ps = es.enter_context(nc.psum_tensor("ps", [128, 8, 512], F32))

    V, A, G, T, S = "vector", "scalar", "gpsimd", "tensor", "sync"
    _cfg = os.environ.get("S5CFG", "GGGG")
    E3, E7, E8, E9 = (V if c == "V" else G for c in _cfg)

    def dma(out, in_, r, w, eng=S, **kw):
        return P.op(eng, lambda e: e.dma_start(out=out, in_=in_, **kw), r=r, w=w, dma=True)

    def act_(out, in_, func, r, w, bias=None, scale=None):
        kw = {}
        if bias is not None:
            kw["bias"] = bias
        if scale is not None:
            kw["scale"] = scale
        return P.op(A, lambda e: e.activation(out=out, in_=in_, func=func, **kw), r=r, w=w)

    def tt(eng, out, a, b, op, r, w):
        return P.op(eng, lambda e: e.tensor_tensor(out, a, b, op), r=r, w=w)

    def ts(eng, out, a, s1, s2, op0, op1, r, w):
        if s2 is None:
            return P.op(eng, lambda e: e.tensor_scalar(out, a, s1, None, op0), r=r, w=w)
        return P.op(eng, lambda e: e.tensor_scalar(out, a, s1, s2, op0, op1), r=r, w=w)

    def stt(out, a, s, b, op0, op1, r, w):
        return P.op(V, lambda e: e.scalar_tensor_tensor(out, a, s, b, op0, op1), r=r, w=w)

    def cp(eng, out, in_, r, w):
        if eng == A:
            return P.op(A, lambda e: e.copy(out, in_), r=r, w=w)
        return P.op(eng, lambda e: e.tensor_copy(out, in_), r=r, w=w)

    def scan(out, d0, d1, init, r, w):
        return P.op(V, lambda e: e.tensor_tensor_scan(out, d0, d1, init, ALU.mult, ALU.add), r=r, w=w)

    def recip(out, in_, r, w):
        return P.op(V, lambda e: e.reciprocal(out, in_), r=r, w=w)

    def memset(eng, ap_, val, w):
        return P.op(eng, lambda e: e.memset(ap_, val), w=w)

    def mm(out, lhsT, rhs, start, stop, r, w, sig):
        return P.op(T, lambda e: e.matmul(out, lhsT, rhs, start=start, stop=stop), r=r, w=w, sig=sig)

    for c in range(L * NCH):
        dma(wb[c], wf[c], r=(("castq", c % 4),), w=(("wb", c), ("castq", c % 4)), eng=G, max_dma_last_dim=2048)

    dma(pv[:, :], pvd[:, :], (), ("pv",))
    dma(jj[:, :], jj_d[:, :], (), ("jj",))
    xflat = x[:, :, :].rearrange("p a b -> p (a b)")
    XK = tuple(("x", k) for k in range(8))
    dma(xflat[:, 0:L * 512], gbd_d[:, :], (), XK)
    cp(V, gbd[:, :], xflat[:, 0:L * 512], XK, ("gbd",))
    dma(xflat[:, 0:L * 512], wglu_d[:, :], XK, XK)
    cp(V, wglu[:, :], xflat[:, 0:L * 512], XK, ("wglu",))
    dma(xflat[:, 0:L * 1024], cl_d[:, :], XK, XK)
    for l in range(L):
        o0 = l * 1024
        cp(V, clb[:, o0:o0 + 512], xflat[:, o0:o0 + 512], XK, ("clb",))
        ts(V, clb[:, o0 + 512:o0 + 1024], xflat[:, o0 + 512:o0 + 1024], -1.0, None, ALU.mult, None, XK, ("clb",))
    for l in range(L):
        dma(xflat[:, 0:2048], bl_d[:, l * 2048:(l + 1) * 2048], XK + ("clb",), XK)
        cp(V, blb[:, l * 2048:(l + 1) * 2048], xflat[:, 0:2048], XK, ("blb",))
    memset(V, ones_bf[:, :], 1.0, ("ones",))
    dma(xflat[:, 0:64], m16_d[:, :], XK + ("blb",), XK)
    cp(V, m16b[:, :], xflat[:, 0:64], XK, ("m16b",))
    for l in range(L):
        lam = pv[:, PV[f"lam{l}"]:PV[f"lam{l}"] + 2]
        act_(tA[0][:, 0:2], lam, AF.Exp, ("pv",), ("tA0",), scale=-1.0)
        act_(tA[0][:, 2:4], tA[0][:, 0:2], AF.Ln, ("tA0",), ("tA0",), bias=1.0)
        ts(V, c8[:, 2 * l:2 * l + 2], tA[0][:, 2:4], -8.0, None, ALU.mult, None, ("tA0",), ("c8",))
        ts(V, c16[:, 2 * l:2 * l + 2], tA[0][:, 2:4], -16.0, None, ALU.mult, None, ("tA0",), ("c8",))
        sk = pv[:, PV[f"sinks{l}"]:PV[f"sinks{l}"] + 4]
        act_(esink[:, 4 * l:4 * l + 4], sk, AF.Exp, ("pv",), ("esink",))
    hp = pv[:, PV["halfpi"]:PV["halfpi"] + 1]
    for l in range(L):
        are = pv[:, PV[f"a_re{l}"]:PV[f"a_re{l}"] + 8]
        aim = pv[:, PV[f"a_im{l}"]:PV[f"a_im{l}"] + 8]
        ldt = pv[:, PV[f"log_dt{l}"]:PV[f"log_dt{l}"] + 8]
        sm = tA[1]
        k_ = ("tA1",)
        dtv, lrdt, magv, th = sm[:, 0:8], sm[:, 8:16], sm[:, 16:24], sm[:, 24:32]
        act_(dtv, ldt, AF.Exp, ("pv",), k_)
        tt(V, lrdt, are, dtv, ALU.mult, ("pv",) + k_, k_)
        act_(magv, lrdt, AF.Exp, k_, k_)
        tt(V, th, aim, dtv, ALU.mult, ("pv",) + k_, k_)
        ang = [s5t[0], s5t[1]]
        def view(tl):
            return [tl[i][:, :].rearrange("p (a b) -> p a b", a=4) for i in range(2)]
        angv = view(ang)
        for pr in range(8):
            ts(V, angv[pr // 4][:, pr % 4, :], jj[:, :], th[:, pr:pr + 1], None, ALU.mult, None,
               ("jj",) + k_, ("s5t0", "s5t1"))
            ts(V, magt[l][:, pr, :], jj[:, :], 0.0, magv[:, pr:pr + 1], ALU.mult, ALU.add, ("jj",) + k_, (f"magt{l}",))
        Ucv = [Uc[l][:, 0:4, :], Uc[l][:, 4:8, :]]
        Usv = [Us[l][:, 0:4, :], Us[l][:, 4:8, :]]
        for hf in range(2):
            a_ = angv[hf]
            t1 = view([s5t[2], s5t[3]])[hf]
            t2 = view([s5t[4], s5t[5]])[hf]
            t3 = view([s5t[6], s5t[7]])[hf]
            ti = s5t[2 + hf][:, :].bitcast(I32).rearrange("p (a b) -> p a b", a=4)
            kk = tuple(f"s5t{i}" for i in range(8))
            ts(V, t2, a_, 1.0 / TWO_PI, None, ALU.mult, None, kk, kk)
            cp(V, ti, t2, kk, kk)
            cp(V, t2, ti, kk, kk)
            stt(t3, t2, -TWO_PI, a_, ALU.mult, ALU.add, kk, kk)
            act_(t1, t3, AF.Sin, kk, kk, scale=0.25)
            act_(t2, t3, AF.Sin, kk, kk, scale=0.25, bias=hp)
            stt(t3, t1, 2.0, t2, ALU.mult, ALU.mult, kk, kk)
            tt(V, t2, t1, t1, ALU.mult, kk, kk)
            ts(V, t2, t2, -2.0, 1.0, ALU.mult, ALU.add, kk, kk)
            stt(Usv[hf], t3, 2.0, t2, ALU.mult, ALU.mult, kk, (f"U{l}",))
            tt(V, t1, t3, t3, ALU.mult, kk, kk)
            ts(V, Ucv[hf], t1, -2.0, 1.0, ALU.mult, ALU.add, kk, (f"U{l}",))
        abr, abi, den, nr_, fr, fi, w1, w2 = (sm[:, 32 + 8 * i:40 + 8 * i] for i in range(8))
        kU = (f"U{l}",)
        tt(V, abr, magv, Uc[l][:, :, 0], ALU.mult, k_ + kU, k_)
        tt(V, abi, magv, Us[l][:, :, 0], ALU.mult, k_ + kU, k_)
        tt(V, den, are, are, ALU.mult, ("pv",), k_)
        tt(V, w1, aim, aim, ALU.mult, ("pv",), k_)
        tt(V, den, den, w1, ALU.add, k_, k_)
        recip(den, den, k_, k_)
        ts(V, nr_, abr, -1.0, None, ALU.add, None, k_, k_)
        tt(V, w1, nr_, are, ALU.mult, k_ + ("pv",), k_)
        tt(V, w2, abi, aim, ALU.mult, k_ + ("pv",), k_)
        tt(V, w1, w1, w2, ALU.add, k_, k_)
        tt(V, fr, w1, den, ALU.mult, k_, k_)
        tt(V, w1, abi, are, ALU.mult, k_ + ("pv",), k_)
        tt(V, w2, nr_, aim, ALU.mult, k_ + ("pv",), k_)
        tt(V, w1, w1, w2, ALU.subtract, k_, k_)
        tt(V, fi, w1, den, ALU.mult, k_, k_)
        for pr in range(8):
            tmp = tA[2][:, 0:128]
            ts(V, tmp, Us[l][:, pr, :], fi[:, pr:pr + 1], None, ALU.mult, None, k_ + kU, ("s5t7",))
            stt(Tc[l][:, pr, :], Uc[l][:, pr, :], fr[:, pr:pr + 1], tmp, ALU.mult, ALU.add, k_ + kU + ("s5t7",), (f"T{l}",))
            ts(V, tmp, Us[l][:, pr, :], fr[:, pr:pr + 1], None, ALU.mult, None, k_ + kU, ("s5t7",))
            stt(Ts[l][:, pr, :], Uc[l][:, pr, :], fi[:, pr:pr + 1], tmp, ALU.mult, ALU.subtract, k_ + kU + ("s5t7",), (f"T{l}",))

    wstate = {"n": 0}
    wsched = []

    def wload(ci_global):
        slot = wstate["n"] % NW
        wstate["n"] += 1
        dma(wring[slot][:, :, :].rearrange("p a b -> p (a b)"), wb[ci_global], r=(("wb", ci_global),), w=(("w", slot),))
        return slot

    tiles = []
    for s in range(2):
        for j in range(SEQ // NT):
            tiles.append((s, j * NT, NT, s * SEQ + j * NT, j * NT))
    tiles.append((2, 0, NSAMP, 2 * SEQ, SEQ))
    if n_tiles_limit is not None:
        tiles = tiles[:n_tiles_limit]
    if tile_sel is not None:
        tiles = [tiles[i] for i in tile_sel]
    out_dmas = []

    order = []
    for ti in range(len(tiles)):
        for l in range(L):
            order.extend(l * NCH + c for c in range(NCH))
    wq = {"issued": 0, "slots": []}

    def wprefetch(upto):
        while wq["issued"] < min(upto, len(order)):
            wq["slots"].append(wload(order[wq["issued"]]))
            wq["issued"] += 1

    wcur = {"i": 0}

    def wnext():
        i = wcur["i"]
        wprefetch(i + NW)
        wcur["i"] += 1
        return wq["slots"][i]

    psn = {"i": 0}

    def bank4():
        b = psn["i"] % 4
        psn["i"] += 1
        return b

    def rmsnorm(n, gcol, dst_hn):
        b = 4 + (psn["i"] % 2)
        psn["i"] += 1
        for k in range(8):
            sq = sqb[k % 2]
            act_(sq[:, 0:n], x[:, k, 0:n], AF.Square, (("x", k),), (f"sq{k % 2}",))
            mm(ps[:, b, 0:n], ones_bf[:, :], sq[:, 0:n], k == 0, k == 7, (f"sq{k % 2}", "ones"), (("ps", b),), True)
        act_(rstd[:, 0:n], ps[:, b, 0:n], AF.Ln, (("ps", b),), ("s5t1",), scale=1.0 / D, bias=1e-6)
        act_(rstd[:, 0:n], rstd[:, 0:n], AF.Exp, ("s5t1",), ("s5t1",), scale=-0.5)
        for k in range(8):
            if dst_hn:
                stt(hn[:, k, 0:n], x[:, k, 0:n], pv[:, gcol + k:gcol + k + 1], rstd[:, 0:n], ALU.mult, ALU.mult,
                    (("x", k), "pv", "s5t1"), (("hn", k),))
            else:
                yv = act[:, 2 * k:2 * k + 2, :].rearrange("p a b -> p (a b)").bitcast(F32)
                stt(yv[:, 0:n], x[:, k, 0:n], pv[:, gcol + k:gcol + k + 1], rstd[:, 0:n], ALU.mult, ALU.mult,
                    (("x", k), "pv", "s5t1"), (("act", 2 * k), ("act", 2 * k + 1)))

    def gelu_from(src, srckeys, n, dst, dstkeys, tmpi):
        a, b_ = tA[tmpi], tA[tmpi + 1]
        ka, kb = (f"tA{tmpi}",), (f"tA{tmpi + 1}",)
        act_(a[:, 0:n], src, AF.Square, srckeys, ka)
        ts(V, a[:, 0:n], a[:, 0:n], 0.044715, 1.0, ALU.mult, ALU.add, ka, ka)
        tt(V, b_[:, 0:n], src, a[:, 0:n], ALU.mult, srckeys + ka, kb)
        act_(a[:, 0:n], b_[:, 0:n], AF.Sigmoid, kb, ka, scale=GELU_K)
        tt(V, dst, src, a[:, 0:n], ALU.mult, srckeys + ka, dstkeys)

    def main_loop():
      for ti, (sq_id, t0, n, col0, pos0) in enumerate(tiles):
        first = (t0 == 0)
        last = (t0 + n == (SEQ if sq_id < 2 else NSAMP))
        samp = (sq_id == 2)
        nsub = (n + 127) // 128
        if samp:
            memset(V, hn[:, :, NSAMP:128], 0.0, tuple(("hn", k) for k in range(8)))
            for l_ in range(L):
                memset(V, kT[l_][:, 128 + NSAMP:256], 0.0, (f"kT{l_}",))
        for k in range(8):
            dma(x[:, k, 0:n], xT[k, :, col0:col0 + n], (), (("x", k),))
        dma(ropeb[:, :, 0:n], rope_d[:, :, pos0:pos0 + n], (), ("ropeb",))
        for l in range(L):
            pvl = lambda nm: PV[f"{nm}{l}"]
            if first:
                if samp:
                    dma(uahist[l][:, :, :], conv0_d[:, l * 6:(l + 1) * 6].rearrange("p (t j) -> p t j", t=2), (), (f"uahs{l}",))
                    dma(hst[:, 2 * l:2 * l + 2], h0_d[:, 2 * l:2 * l + 2], (), (f"hst{l}",))
                    dma(s5s[:, 16 * l:16 * l + 16], s0_d[:, 16 * l:16 * l + 16], (), (f"s5s{l}",))
                    dma(tA[3][:, 0:128], kc_d[:, l * 128:(l + 1) * 128], (), ("s5t2",))
                    cp(V, kT[l][:, 0:128], tA[3][:, 0:128], ("s5t2",), (f"kT{l}",))
                    dma(tA[3][:, 128:256], vc_d[:, l * 128:(l + 1) * 128], ("s5t2",), ("s5t2",))
                    cp(V, vv[l][:, 0, :], tA[3][:, 128:256], ("s5t2",), (f"vv{l}",))
                else:
                    memset(V, uahist[l][:, :, :], 0.0, (f"uahs{l}",))
                    memset(V, hst[:, 2 * l:2 * l + 2], 0.0, (f"hst{l}",))
                    memset(V, s5s[:, 16 * l:16 * l + 16], 0.0, (f"s5s{l}",))
            rmsnorm(n, pvl("norm1"), True)
            chk("norm1" if l == 0 else f"norm1_{l}")
            cp(A, ua[:, :, 0:3], uahist[l][:, :, :], (f"uahs{l}",), ("uah",))
            wslot = None
            for ci in range(4):
                wslot = wnext()
                wt = wring[wslot]
                for mi in range(4):
                    if ci in (2, 3) and mi % 2 == 1:
                        continue
                    mlist = [mi] if ci < 2 else [mi, mi + 1]
                    banks = []
                    for m in mlist:
                        b = bank4()
                        banks.append(b)
                        for k in range(8):
                            mm(ps[:, b, 0:n], wt[:, k, m * 128:(m + 1) * 128], hn[:, k, 0:n], k == 0, k == 7,
                               (("w", wslot), ("hn", k)), (("ps", b),), k == 7)
                    b = banks[0]
                    src = ps[:, b, 0:n]
                    sk = (("ps", b),)
                    if ci == 0 and mi < 2:
                        cp(A, ua[:, mi, 3:3 + n], src, sk, (f"ua{mi}",))
                    elif ci == 0:
                        gelu_from(src, sk, n, ga[:, mi - 2, 0:n], (f"ga{mi - 2}",), 0)
                    elif ci == 1 and mi < 2:
                        cp(A, ub[:, mi, 0:n], src, sk, (f"ub{mi}",))
                        cp(V, ubb[:, mi, 0:n], src, sk, (f"ubb{mi}",))
                    elif ci == 1 and mi == 2:
                        kb0 = b
                    elif ci == 1 and mi == 3:
                        b1 = b
                        tt(V, ps[:, kb0, 0:n], ps[:, kb0, 0:n], ropeb[:, 0, 0:n], ALU.mult, (("ps", kb0), "ropeb"), (("ps", kb0),))
                        tt(V, tA[1][:, 0:n], ps[:, b1, 0:n], ropeb[:, 1, 0:n], ALU.mult, (("ps", b1), "ropeb"), ("tA1",))
                        tt(V, kf[:, 0:n], ps[:, kb0, 0:n], tA[1][:, 0:n], ALU.add, (("ps", kb0), "tA1"), ("s5t3",))
                        cp(A, kT[l][:, 128:128 + n], kf[:, 0:n], ("s5t3",), (f"kT{l}",))
                        if last:
                            nk = min(n, 128)
                            oc = l * 272 + (sq_id * 128)
                            out_dmas.append(dma(o_k[:, oc:oc + nk], kf[:, n - nk:n], ("s5t3",), ()))
                    else:
                        tq = (ci - 2) * 2 + mi // 2
                        b0, b1 = banks
                        tt(V, ps[:, b0, 0:n], ps[:, b0, 0:n], ropeb[:, 0, 0:n], ALU.mult, (("ps", b0), "ropeb"), (("ps", b0),))
                        tt(V, tA[1][:, 0:n], ps[:, b1, 0:n], ropeb[:, 1, 0:n], ALU.mult, (("ps", b1), "ropeb"), ("tA1",))
                        tt(V, qT[:, tq, 0:n], ps[:, b0, 0:n], tA[1][:, 0:n], ALU.add, (("ps", b0), "tA1"), ("qT",))
            chk("inproj_a")
            wslot = wnext()
            wt = wring[wslot]
            b = bank4()
            for sbi in range(nsub):
                m_ = 128
                for k in range(8):
                    mm(ps[0:m_, b, sbi * 128:(sbi + 1) * 128], hn[:, k, sbi * 128:sbi * 128 + m_], wt[:, k, 0:128],
                       k == 0, k == 7, (("w", wslot), ("hn", k)), (("ps", b),), k == 7)
            cp(A, vv[l][:, 1:1 + nsub, :], ps[:, b, 0:nsub * 128].rearrange("p (a b) -> p a b", b=128), (("ps", b),), (f"vv{l}",))
            chk("v_mm")
            if last:
                vfb, vfk = tA[0][:, 0:128], "tA0"
                cp(A, vfb, ps[:, b, (nsub - 1) * 128:nsub * 128], (("ps", b),), (vfk,))
                oc = (l * 3 + sq_id) * 128
                out_dmas.append(dma(o_v[:, oc:oc + 128], vfb, (vfk,), ()))

            chk("inproj" if l == 0 else f"inproj_{l}")
            def actf(slot):
                return act[:, slot:slot + 2, :].rearrange("p a b -> p (a b)").bitcast(F32), (("act", slot), ("act", slot + 1))

            zf_ap = [actf(30), actf(0)]
            L_xc, K_xc = actf(0)
            L_rr, K_rr = actf(2)
            L_ig, K_ig = actf(4)
            L_hh, K_hh = actf(6)
            s5set = [dict(t=[s5t[i] for i in range(8)], tk=[(f"s5t{i}",) for i in range(8)],
                          zre=zre, zim=zim, kzre=("zre",), kzim=("zim",), Sre=Sre, Sim=Sim, kSre=("Sre",), kSim=("Sim",),
                          tiny=tiny[:, 0:4], ktiny=("tiny0",), banks=(0, 1)),
                     dict(t=[actf(8 + 2 * i)[0] for i in range(8)], tk=[actf(8 + 2 * i)[1] for i in range(8)],
                          zre=actf(24)[0], zim=actf(26)[0], kzre=actf(24)[1], kzim=actf(26)[1],
                          Sre=act[:, 28, :], Sim=act[:, 29, :], kSre=(("act", 28),), kSim=(("act", 29),),
                          tiny=tiny[:, 4:8], ktiny=("tiny1",), banks=(2, 3))]

            def lru_gen():
                for t in range(2):
                    cw = pvl("conv_w") + 4 * t
                    kua = (f"ua{t}", "uah")
                    xc, kx = L_xc, K_xc
                    ts(V, xc[:, 0:n], ua[:, t, 3:3 + n], pv[:, cw + 3:cw + 4], pv[:, pvl("conv_b") + t:pvl("conv_b") + t + 1],
                       ALU.mult, ALU.add, kua + ("pv",), kx)
                    for j in range(3):
                        stt(xc[:, 0:n], ua[:, t, j:j + n], pv[:, cw + j:cw + j + 1], xc[:, 0:n], ALU.mult, ALU.add,
                            kua + ("pv",) + kx, kx)
                        yield
                    cp(A, xcb[:, 0:n], xc[:, 0:n], kx, ("xcb",))
                    yield
                    g0 = ((l * 2 + 0) * 2 + t) * 128
                    g1 = ((l * 2 + 1) * 2 + t) * 128
                    bg = 7
                    mm(ps[:, bg, 0:n], gbd[:, g0:g0 + 128], xcb[:, 0:n], True, True, ("gbd", "xcb"), (("ps", bg),), True)
                    act_(L_rr[:, 0:n], ps[:, bg, 0:n], AF.Sigmoid, (("ps", bg), "pv"), K_rr,
                         bias=pv[:, pvl("b_rg") + t:pvl("b_rg") + t + 1])
                    yield
                    mm(ps[:, bg, 0:n], gbd[:, g1:g1 + 128], xcb[:, 0:n], True, True, ("gbd", "xcb"), (("ps", bg),), True)
                    act_(L_ig[:, 0:n], ps[:, bg, 0:n], AF.Sigmoid, (("ps", bg), "pv"), K_ig,
                         bias=pv[:, pvl("b_ig") + t:pvl("b_ig") + t + 1])
                    yield
                    aa, a2 = tA[0], tA[1]
                    act_(aa[:, 0:n], L_rr[:, 0:n], AF.Exp, K_rr + ("c8",), ("tA0",), scale=c8[:, 2 * l + t:2 * l + t + 1])
                    act_(a2[:, 0:n], L_rr[:, 0:n], AF.Exp, K_rr + ("c8",), ("tA1",), scale=c16[:, 2 * l + t:2 * l + t + 1])
                    yield
                    act_(a2[:, 0:n], a2[:, 0:n], AF.Sqrt, ("tA1",), ("tA1",), scale=-1.0, bias=1.0)
                    tt(V, L_ig[:, 0:n], L_ig[:, 0:n], xc[:, 0:n], ALU.mult, K_ig + kx, K_ig)
                    yield
                    tt(V, L_ig[:, 0:n], L_ig[:, 0:n], a2[:, 0:n], ALU.mult, K_ig + ("tA1",), K_ig)
                    yield
                    scan(L_hh[:, 0:n], aa[:, 0:n], L_ig[:, 0:n], hst[:, 2 * l + t:2 * l + t + 1],
                         ("tA0", f"hst{l}") + K_ig, K_hh)
                    yield
                    cp(A, hst[:, 2 * l + t:2 * l + t + 1], L_hh[:, n - 1:n], K_hh, (f"hst{l}",))
                    tt(V, mix[:, t, 0:n], L_hh[:, 0:n], ga[:, t, 0:n], ALU.mult, K_hh + (f"ga{t}",), (("mix", t),))
                    yield
                if last:
                    oc = (l * 3 + sq_id) * 6
                    out_dmas.append(dma(o_conv[:, oc:oc + 6].rearrange("p (t j) -> p t j", t=2), ua[:, :, n:n + 3],
                                        ("ua0", "ua1", "uah"), ()))
                    oc = (l * 3 + sq_id) * 2
                    out_dmas.append(dma(o_h[:, oc:oc + 2], hst[:, 2 * l:2 * l + 2], (f"hst{l}",), ()))
                cp(A, uahist[l][:, :, :], ua[:, :, n:n + 3], ("ua0", "ua1", "uah"), (f"uahs{l}",))

            nsb = max(1, n // 128)
            sl = min(n, 128)
            YB = 6

            def s5_pair_steps(pr, S_):
                ut = pr // 4
                b_re, b_im = S_["banks"]
                q, qk = S_["t"], S_["tk"]
                z_re, z_im, kzr, kzi = S_["zre"], S_["zim"], S_["kzre"], S_["kzim"]
                kT_, kU_ = (f"T{l}",), (f"U{l}",)
                ks = (f"s5s{l}",)
                sre = s5s[:, 16 * l + 2 * pr:16 * l + 2 * pr + 1]
                sim = s5s[:, 16 * l + 2 * pr + 1:16 * l + 2 * pr + 2]

                def v3(ap_):
                    return ap_.rearrange("p (a b) -> p a b", b=sl)

                def tb(tab):
                    return tab[:, pr, 0:sl].unsqueeze(1).broadcast_to([128, nsb, sl])

                steps = []

                def st0():
                    o_re = ((l * 2 + 0) * 8 + pr) * 128
                    o_im = ((l * 2 + 1) * 8 + pr) * 128
                    mm(ps[:, b_re, 0:n], blb[:, o_re:o_re + 128], ubb[:, ut, 0:n], True, True, ("blb", f"ubb{ut}"), (("ps", b_re),), True)
                    mm(ps[:, b_im, 0:n], blb[:, o_im:o_im + 128], ubb[:, ut, 0:n], True, True, ("blb", f"ubb{ut}"), (("ps", b_im),), True)
                steps.append(st0)

                kbr, kbi = (("ps", b_re),), (("ps", b_im),)

                def st1():
                    tt(V, v3(q[1][:, 0:n]), v3(ps[:, b_im, 0:n]), tb(Ts[l]), ALU.mult, kbi + kT_, qk[1])
                    tt(V, v3(q[3][:, 0:n]), v3(ps[:, b_re, 0:n]), tb(Ts[l]), ALU.mult, kbr + kT_, qk[3])
                steps.append(st1)

                def st2():
                    tt(V, v3(ps[:, b_re, 0:n]), v3(ps[:, b_re, 0:n]), tb(Tc[l]), ALU.mult, kbr + kT_, kbr)
                    tt(V, v3(ps[:, b_im, 0:n]), v3(ps[:, b_im, 0:n]), tb(Tc[l]), ALU.mult, kbi + kT_, kbi)
                steps.append(st2)

                def st3():
                    tt(V, q[0][:, 0:n], ps[:, b_re, 0:n], q[1][:, 0:n], ALU.subtract, kbr + qk[1], qk[0])
                    tt(V, q[2][:, 0:n], ps[:, b_im, 0:n], q[3][:, 0:n], ALU.add, kbi + qk[3], qk[2])
                steps.append(st3)
                for sbi in range(nsb):
                    c0, c1 = sbi * sl, (sbi + 1) * sl

                    def sa(c0=c0, c1=c1):
                        scan(ps[:, b_re, c0:c1], magt[l][:, pr, 0:sl], q[0][:, c0:c1], sre, qk[0] + (f"magt{l}",) + ks, kbr)
                        scan(ps[:, b_im, c0:c1], magt[l][:, pr, 0:sl], q[2][:, c0:c1], sim, qk[2] + (f"magt{l}",) + ks, kbi)
                    steps.append(sa)

                    def sb_(c0=c0, c1=c1):
                        ucl = Uc[l][:, pr, sl - 1:sl]
                        usl = Us[l][:, pr, sl - 1:sl]
                        w1 = S_["tiny"][:, 0:1]
                        w2 = S_["tiny"][:, 1:2]
                        kt = S_["ktiny"]
                        tt(V, w1, ps[:, b_im, c1 - 1:c1], usl, ALU.mult, kbi + kU_, kt)
                        tt(V, w2, ps[:, b_re, c1 - 1:c1], usl, ALU.mult, kbr + kU_, kt)
                        stt(sre, ps[:, b_re, c1 - 1:c1], ucl, w1, ALU.mult, ALU.subtract, kbr + kt + kU_, ks)
                        stt(sim, ps[:, b_im, c1 - 1:c1], ucl, w2, ALU.mult, ALU.add, kbi + kt + kU_, ks)
                    steps.append(sb_)

                def st7():
                    tt(V, v3(q[5][:, 0:n]), v3(ps[:, b_im, 0:n]), tb(Us[l]), ALU.mult, kbi + kU_, qk[5])
                    tt(V, v3(q[7][:, 0:n]), v3(ps[:, b_re, 0:n]), tb(Us[l]), ALU.mult, kbr + kU_, qk[7])
                steps.append(st7)

                def st8():
                    tt(V, v3(ps[:, b_re, 0:n]), v3(ps[:, b_re, 0:n]), tb(Uc[l]), ALU.mult, kbr + kU_, kbr)
                    tt(V, v3(ps[:, b_im, 0:n]), v3(ps[:, b_im, 0:n]), tb(Uc[l]), ALU.mult, kbi + kU_, kbi)
                steps.append(st8)

                def st9():
                    tt(V, S_["Sre"][:, 0:n], ps[:, b_re, 0:n], q[5][:, 0:n], ALU.subtract, kbr + qk[5], S_["kSre"])
                    tt(V, S_["Sim"][:, 0:n], ps[:, b_im, 0:n], q[7][:, 0:n], ALU.add, kbi + qk[7], S_["kSim"])
                steps.append(st9)

                def st10():
                    oc_re = ((l * 2 + 0) * 8 + pr) * 64
                    oc_im = ((l * 2 + 1) * 8 + pr) * 64
                    po = 64 * ((pr % 4) // 2)
                    yo = ps[po:po + 64, YB, 0:n]
                    mm(yo, clb[:, oc_re:oc_re + 64], S_["Sre"][:, 0:n], pr % 2 == 0, False, ("clb",) + S_["kSre"], (("ps", YB),), False)
                    mm(yo, clb[:, oc_im:oc_im + 64], S_["Sim"][:, 0:n], False, pr % 2 == 1, ("clb",) + S_["kSim"], (("ps", YB),), True)
                steps.append(st10)
                return steps

            def s5_gen():
                for grp in range(4):
                    sa_ = s5_pair_steps(2 * grp, s5set[0])
                    sb2 = s5_pair_steps(2 * grp + 1, s5set[1])
                    for fa, fb in zip(sa_, sb2):
                        fa()
                        fb()
                        yield
                    if grp % 2 == 1:
                        t = grp // 2
                        stt(zf_ap[t][0][:, 0:n], ub[:, t, 0:n], pv[:, pvl("ssm_d") + t:pvl("ssm_d") + t + 1], ps[:, YB, 0:n],
                            ALU.mult, ALU.add, (f"ub{t}", "pv", ("ps", YB)), zf_ap[t][1])
                        yield
                if last:
                    oc = (l * 3 + sq_id) * 16
                    out_dmas.append(dma(o_s[:, oc:oc + 16], s5s[:, 16 * l:16 * l + 16], (f"s5s{l}",), ()))

            if samp:
                chunks = [(0, n, [(0, 0, 128), (1, 0, 128)])]
            else:
                chunks = []
                for c in range(n // 64):
                    if c % 2 == 0:
                        pcs = [(c // 2, 0, 128), (c // 2 + 1, 0, 64)]
                    else:
                        pcs = [((c - 1) // 2, 64, 128), ((c + 1) // 2, 0, 128)]
                    if first:
                        pcs = [p_ for p_ in pcs if p_[0] >= 1]
                    chunks.append((c * 64, 64, pcs))

            def attn_gen():
                for (qc0, nq, pcs) in chunks:
                    nqh = 4 * nq
                    bnd = 7
                    for kvh in range(2):
                        hp_ = slice(kvh * 64, kvh * 64 + 64)
                        et = ET[kvh]
                        bs = 4 + kvh
                        for pi, (vt, p0, p1) in enumerate(pcs):
                            kc0 = vt * 128 + p0
                            mm(ps[p0:p1, bs, pi * 256:pi * 256 + nqh].rearrange("p (a b) -> p a b", a=4),
                               kT[l][hp_, kc0:kc0 + (p1 - p0)], qT[hp_, :, qc0:qc0 + nq], True, True,
                               (f"kT{l}", "qT"), (("ps", bs),), True)
                            act_(et[p0:p1, pi, 0:nqh], ps[p0:p1, bs, pi * 256:pi * 256 + nqh], AF.Exp, (("ps", bs),), (f"ET{kvh}",), scale=0.125)
                        yield
                    for kvh in range(2):
                        hp_ = slice(kvh * 64, kvh * 64 + 64)
                        et = ET[kvh]
                        for pi, (vt, p0, p1) in enumerate(pcs):
                            stt_, stp = (pi == 0), (pi == len(pcs) - 1)
                            mm(ps[hp_, bnd, 0:nqh], vv[l][p0:p1, vt, kvh * 64:kvh * 64 + 64], et[p0:p1, pi, 0:nqh], stt_, stp,
                               (f"vv{l}", f"ET{kvh}"), (("ps", bnd),), stp)
                        for pi, (vt, p0, p1) in enumerate(pcs):
                            stt_, stp = (pi == 0), (pi == len(pcs) - 1)
                            onesl = m16b[p0:p1, 0:64] if (samp and pi == 1) else ones_bf[p0:p1, 0:64]
                            mm(ps[hp_, bnd, 256:256 + nqh], onesl, et[p0:p1, pi, 0:nqh], stt_, stp,
                               ("ones", "m16b", f"ET{kvh}"), (("ps", bnd),), stp)
                        yield
                    r3 = rcp[:, 0:nqh].rearrange("p (a b) -> p a b", a=4)
                    for t4 in range(4):
                        act_(rcp[:, t4 * nq:(t4 + 1) * nq], ps[:, bnd, 256 + t4 * nq:256 + (t4 + 1) * nq], AF.Ln,
                             (("ps", bnd), "esink"), ("rcp",), bias=esink[:, 4 * l + t4:4 * l + t4 + 1])
                    act_(rcp[:, 0:nqh], rcp[:, 0:nqh], AF.Exp, ("rcp",), ("rcp",), scale=-1.0)
                    tt(V, mix[:, 4:8, qc0:qc0 + nq], ps[:, bnd, 0:nqh].rearrange("p (a b) -> p a b", a=4), r3, ALU.mult,
                       (("ps", bnd), "rcp"), tuple(("mix", 4 + i) for i in range(4)))
                    yield

            gens = [[s5_gen(), 1], [attn_gen(), 2], [lru_gen(), 2]]
            rnd = 0
            while gens:
                for ge in list(gens):
                    if rnd % ge[1] != 0 and len(gens) > 1:
                        continue
                    try:
                        next(ge[0])
                    except StopIteration:
                        gens.remove(ge)
                rnd += 1
            chk("s5" if l == 0 else f"s5_{l}")
            for t in range(2):
                gelu_from(zf_ap[t][0][:, 0:n], zf_ap[t][1], n, zf_ap[t][0][:, 0:n], zf_ap[t][1], 0)
                cp(A, zbf[:, t, 0:n], zf_ap[t][0][:, 0:n], zf_ap[t][1], ("zbf",))
            for t in range(2):
                b = bank4()
                for k in range(2):
                    o_ = (l * 2 + k) * 256 + t * 128
                    mm(ps[:, b, 0:n], wglu[:, o_:o_ + 128], zbf[:, k, 0:n], k == 0, k == 1, ("wglu", "zbf"), (("ps", b),), k == 1)
                act_(tA[0][:, 0:n], ps[:, b, 0:n], AF.Sigmoid, (("ps", b), "pv"), ("tA0",),
                     bias=pv[:, pvl("b_glu") + t:pvl("b_glu") + t + 1])
                tt(V, mix[:, 2 + t, 0:n], zf_ap[t][0][:, 0:n], tA[0][:, 0:n], ALU.mult, zf_ap[t][1] + ("tA0",), (("mix", 2 + t),))
            chk("attn" if l == 0 else f"attn_{l}")
            if not last:
                cp(A, kT[l][:, 0:128], kT[l][:, n:n + 128], (f"kT{l}",), (f"kT{l}",))
                cp(A, vv[l][:, 0, :], vv[l][:, 4, :], (f"vv{l}",), (f"vv{l}",))

            for ci in range(2):
                wslot = wnext()
                wt = wring[wslot]
                for mi in range(4):
                    m = ci * 4 + mi
                    b = bank4()
                    for k in range(8):
                        mm(ps[:, b, 0:n], wt[:, k, mi * 128:(mi + 1) * 128], mix[:, k, 0:n], k == 0, k == 7,
                           (("w", wslot), ("mix", k)), (("ps", b),), k == 7)
                    tt(V, x[:, m, 0:n], x[:, m, 0:n], ps[:, b, 0:n], ALU.add, (("x", m), ("ps", b)), (("x", m),))
            chk("outproj" if l == 0 else f"outproj_{l}")
            rmsnorm(n, pvl("norm2"), True)
            for ci in range(8):
                wslot = wnext()
                wt = wring[wslot]
                for mi in range(4):
                    b = bank4()
                    for k in range(8):
                        mm(ps[:, b, 0:n], wt[:, k, mi * 128:(mi + 1) * 128], hn[:, k, 0:n], k == 0, k == 7,
                           (("w", wslot), ("hn", k)), (("ps", b),), k == 7)
                    tmp = tA[mi % 2]
                    act_(tmp[:, 0:n], ps[:, b, 0:n], AF.Relu, (("ps", b),), (f"tA{mi % 2}",))
                    tt(V, act[:, ci * 4 + mi, 0:n], ps[:, b, 0:n], tmp[:, 0:n], ALU.mult, (("ps", b), f"tA{mi % 2}"),
                       (("act", ci * 4 + mi),))
            for mh in range(2):
                for kq in range(4):
                    wslot = wnext()
                    wt = wring[wslot]
                    for mi in range(4):
                        b = mh * 4 + mi
                        for k in range(8):
                            kk = kq * 8 + k
                            mm(ps[:, b, 0:n], wt[:, k, mi * 128:(mi + 1) * 128], act[:, kk, 0:n], kk == 0, kk == 31,
                               (("w", wslot), ("act", kk)), (("ps", b),), kk == 31 or k == 7)
                for mi in range(4):
                    b = mh * 4 + mi
                    m = mh * 4 + mi
                    tt(V, x[:, m, 0:n], x[:, m, 0:n], ps[:, b, 0:n], ALU.add, (("x", m), ("ps", b)), (("x", m),))
            psn["i"] = 0
            chk(f"layer{l}")
        rmsnorm(n, PV["norm_f"], False)
        chk("fnorm")
        for k in range(8):
            yv = act[:, 2 * k:2 * k + 2, :].rearrange("p a b -> p (a b)").bitcast(F32)
            out_dmas.append(dma(yT[k, :, col0:col0 + n], yv[:, 0:n], (("act", 2 * k), ("act", 2 * k + 1)), (("yT", k),)))

    try:
        chk("prologue")
        main_loop()
    except _Stop:
        pass
    fin = Op()
    fin.eng, fin.fn, fin.sig, fin.dma = S, (lambda e: e.nop()), False, False
    fin.deps = list(out_dmas)
    fin.sem = fin.val = fin.cover = None
    P.q[S].append(fin)
    P.emit(nc, es)
    es.close()
    return nc


_NC_CACHE = {}


def _rope_tables():
    half = 32
    inv = (10000.0 ** (-np.arange(half, dtype=np.float32) / half)).astype(np.float32)
    pos = np.concatenate([np.arange(SEQ), SEQ + np.arange(NSAMP)]).astype(np.float32)
    ang = pos[None, :] * inv[:, None]
    cos = np.cos(ang).astype(np.float32)
    sin = np.sin(ang).astype(np.float32)
    d = np.arange(128) % 64
    i = d % 32
    sign = np.where(d < 32, -1.0, 1.0).astype(np.float32)
    tab = np.empty((128, 2, SEQ + NSAMP), np.float32)
    tab[:, 0, :] = cos[i]
    tab[:, 1, :] = sin[i] * sign[:, None]
    return tab


def _prep_shared(inp):
    f = lambda a: np.asarray(a, dtype=np.float32)
    w_in, w_out, w_up, w_down = f(inp["w_in"]), f(inp["w_out"]), f(inp["w_up"]), f(inp["w_down"])
    o3, o4, o5 = 768, 1280, 1408
    hd = np.arange(64)
    sw = (hd + 32) % 64
    qcols = lambda h: o3 + 64 * h + hd
    qscols = lambda h: o3 + 64 * h + sw
    cols = []
    cols += list(range(0, 256))
    cols += list(range(256, 512))
    cols += list(range(512, 768))
    cols += list(range(o4, o4 + 128))
    cols += list(np.concatenate([o4 + 64 * kv + sw for kv in range(2)]))
    for t in range(4):
        cols += list(np.concatenate([qcols(t), qcols(4 + t)]))
        cols += list(np.concatenate([qscols(t), qscols(4 + t)]))
    cols += list(range(o5, o5 + 128))
    cols = np.asarray(cols)
    assert cols.shape[0] == 2176
    rows_out = list(range(0, 512))
    for t in range(4):
        rows_out += list(512 + 64 * t + hd) + list(512 + 64 * (4 + t) + hd)
    rows_out = np.asarray(rows_out)
    wf = np.zeros((L * NCH, 128, 8, 512), np.float32)
    for l in range(L):
        we = w_in[l][:, cols].reshape(8, 128, 2176)
        for ci in range(4):
            wf[l * NCH + ci] = we[:, :, 512 * ci:512 * ci + 512].transpose(1, 0, 2)
        wf[l * NCH + 4][:, :, 0:128] = we[:, :, 2048:2176].transpose(1, 0, 2)
        wo = w_out[l][rows_out, :].reshape(8, 128, 1024)
        for ci in range(2):
            wf[l * NCH + CH_OUT + ci] = wo[:, :, 512 * ci:512 * ci + 512].transpose(1, 0, 2)
        wu = w_up[l].reshape(8, 128, 4096)
        for ci in range(8):
            wf[l * NCH + CH_UP + ci] = wu[:, :, 512 * ci:512 * ci + 512].transpose(1, 0, 2)
        wd = w_down[l].reshape(32, 128, 1024)
        for mh in range(2):
            for kq in range(4):
                wf[l * NCH + CH_DN + mh * 4 + kq] = wd[8 * kq:8 * kq + 8, :, 512 * mh:512 * mh + 512].transpose(1, 0, 2)
    wf = wf.reshape(L * NCH, 128, 4096)

    pvv = np.zeros((128, NPV), np.float32)
    t128 = lambda v, n: f(v).reshape(n, 128).T
    for l in range(L):
        pvv[:, PV[f"norm1{l}"]:PV[f"norm1{l}"] + 8] = t128(inp["norm1"][l], 8)
        pvv[:, PV[f"norm2{l}"]:PV[f"norm2{l}"] + 8] = t128(inp["norm2"][l], 8)
        cw = f(inp["conv_w"][l])
        for t in range(2):
            pvv[:, PV[f"conv_w{l}"] + 4 * t:PV[f"conv_w{l}"] + 4 * t + 4] = cw[:, t * 128:(t + 1) * 128].T
        for nm in ("conv_b", "b_rg", "b_ig", "ssm_d", "b_glu"):
            pvv[:, PV[f"{nm}{l}"]:PV[f"{nm}{l}"] + 2] = t128(inp[nm][l], 2)
        pvv[:, PV[f"lam{l}"]:PV[f"lam{l}"] + 2] = t128(inp["lru_lambda"][l], 2)
        for nm, key in (("a_re", "ssm_a_re"), ("a_im", "ssm_a_im")):
            a = f(inp[key][l]).reshape(8, 2, 64)
            pvv[:, PV[f"{nm}{l}"]:PV[f"{nm}{l}"] + 8] = a.transpose(1, 2, 0).reshape(128, 8)
        ld = f(inp["ssm_log_dt"][l]).reshape(8, 2)
        pvv[:, PV[f"log_dt{l}"]:PV[f"log_dt{l}"] + 8] = np.repeat(ld.T[:, None, :], 64, axis=1).reshape(128, 8)
        sk = f(inp["attn_sinks"][l]).reshape(2, 4)
        pvv[:, PV[f"sinks{l}"]:PV[f"sinks{l}"] + 4] = np.repeat(sk[:, None, :], 64, axis=1).reshape(128, 4)
    pvv[:, PV["norm_f"]:PV["norm_f"] + 8] = t128(inp["norm_f"], 8)
    pvv[:, PV["halfpi"]] = np.float32(math.pi / 2)

    gbd = np.zeros((128, L, 2, 2, 128), np.float32)
    for l in range(L):
        for ri, key in enumerate(("w_rg", "w_ig")):
            w = f(inp[key][l])
            for t in range(2):
                for h2 in range(2):
                    gbd[h2 * 64:(h2 + 1) * 64, l, ri, t, h2 * 64:(h2 + 1) * 64] = w[2 * t + h2]
    gbd = gbd.reshape(128, -1)

    bl = np.zeros((128, L, 2, 8, 128), np.float32)
    cl = np.zeros((128, L, 2, 8, 64), np.float32)
    for l in range(L):
        for ri, (kb, kc) in enumerate((("ssm_b_re", "ssm_c_re"), ("ssm_b_im", "ssm_c_im"))):
            b = f(inp[kb][l])
            c = f(inp[kc][l])
            for pr in range(8):
                for g2 in range(2):
                    g = 2 * pr + g2
                    gs = g % 8
                    bl[gs * 16:(gs + 1) * 16, l, ri, pr, g2 * 64:(g2 + 1) * 64] = b[g].T
                    cl[g2 * 64:(g2 + 1) * 64, l, ri, pr, (pr % 2) * 32 + g2 * 16:(pr % 2) * 32 + (g2 + 1) * 16] = c[g].T
    bl = bl.reshape(128, -1)
    cl = cl.reshape(128, -1)
    wglu = np.zeros((128, L, 2, 256), np.float32)
    for l in range(L):
        wglu[:, l] = f(inp["w_glu"][l]).reshape(2, 128, 256).transpose(1, 0, 2)
    wglu = wglu.reshape(128, -1)
    jj = np.repeat(np.arange(1, 129, dtype=np.float32)[None, :], 128, axis=0)
    m16 = np.zeros((128, 64), np.float32)
    m16[0:NSAMP, :] = 1.0
    return dict(wf=wf, pv=pvv, gbd=gbd, bl=bl, cl=cl, wglu=wglu, rope=_rope_tables().reshape(128, -1), jj=jj, m16=m16)


def kernel(**inp):
    f = lambda a: np.asarray(a, dtype=np.float32)
    shared = _prep_shared(inp)
    xp, xs = f(inp["x_prompt"]), f(inp["x_sample"])
    in_maps = []
    for c in range(8):
        xt = np.empty((8, 128, NCOL), np.float32)
        xt[:, :, 0:SEQ] = xp[2 * c].T.reshape(8, 128, SEQ)
        xt[:, :, SEQ:2 * SEQ] = xp[2 * c + 1].T.reshape(8, 128, SEQ)
        xt[:, :, 2 * SEQ:] = xs[c].T.reshape(8, 128, NSAMP)
        conv0 = np.zeros((128, L, 2, 3), np.float32)
        h0 = np.zeros((128, L, 2), np.float32)
        s0 = np.zeros((128, L, 8, 2), np.float32)
        kc = np.zeros((128, L, 128), np.float32)
        vc = np.zeros((128, L, 128), np.float32)
        for l in range(L):
            cc = f(inp["cache_conv_a"][l, c])
            conv0[:, l] = cc.T.reshape(2, 128, 3).transpose(1, 0, 2)
            h0[:, l] = f(inp["state_lru"][l, c]).reshape(2, 128).T
            for ri, key in enumerate(("state_ssm_re", "state_ssm_im")):
                s = f(inp[key][l, c]).reshape(8, 2, 64)
                s0[:, l, :, ri] = s.transpose(1, 2, 0).reshape(128, 8)
            kc[:, l] = f(inp["cache_k"][l, c]).reshape(128, 128).T
            vc[:, l] = f(inp["cache_v"][l, c]).reshape(128, 128)
        parts = dict(shared)
        parts.update(conv0=conv0.reshape(128, -1), h0=h0.reshape(128, -1), s0=s0.reshape(128, -1),
                     kc=kc.reshape(128, -1), vc=vc.reshape(128, -1))
        auxa = np.empty((128, AUXW), np.float32)
        for nm, (o_, w_) in AUX.items():
            auxa[:, o_:o_ + w_] = parts[nm]
        in_maps.append(dict(xT=xt, wf=shared["wf"], aux=auxa))
    if "nc" not in _NC_CACHE:
        _NC_CACHE["nc"] = build_nc()
    nc = _NC_CACHE["nc"]
    res = run_bass_kernel_spmd(nc, in_maps, core_ids=list(range(8)))
    return _assemble(res.results)


def _assemble(results):
    B, Bd = 16, 8
    y_p = np.empty((B, SEQ, D), np.float32)
    y_s = np.empty((Bd, NSAMP, D), np.float32)
    conv_p = np.empty((L, B, 3, 256), np.float32)
    lru_p = np.empty((L, B, 256), np.float32)
    sre_p = np.empty((L, B, 16, 64), np.float32)
    sim_p = np.empty((L, B, 16, 64), np.float32)
    k_p = np.empty((L, B, 128, 2, 64), np.float32)
    v_p = np.empty((L, B, 128, 2, 64), np.float32)
    conv_s = np.empty((L, Bd, 3, 256), np.float32)
    lru_s = np.empty((L, Bd, 256), np.float32)
    sre_s = np.empty((L, Bd, 16, 64), np.float32)
    sim_s = np.empty((L, Bd, 16, 64), np.float32)
    k_s = np.empty((L, Bd, NSAMP, 2, 64), np.float32)
    v_s = np.empty((L, Bd, NSAMP, 2, 64), np.float32)
    for c, r in enumerate(results):
        yT = np.asarray(r["yT"]).reshape(D, NCOL)
        y_p[2 * c] = yT[:, 0:SEQ].T
        y_p[2 * c + 1] = yT[:, SEQ:2 * SEQ].T
        y_s[c] = yT[:, 2 * SEQ:].T
        osm = np.asarray(r["osm"])
        og = lambda nm: osm[:, OS[nm][0]:OS[nm][0] + OS[nm][1]]
        oc = og("o_conv").reshape(128, L, 3, 2, 3)
        oh = og("o_h").reshape(128, L, 3, 2)
        os_ = og("o_s").reshape(2, 64, L, 3, 8, 2)
        ok = og("o_k").reshape(128, L, 272)
        ov = og("o_v").reshape(128, L, 3, 128)
        for l in range(L):
            for s in range(3):
                conv = oc[:, l, s].transpose(2, 1, 0).reshape(3, 256)
                hv = oh[:, l, s].T.reshape(256)
                st = os_[:, :, l, s]
                st = st.transpose(2, 0, 1, 3).reshape(16, 64, 2)
                if s < 2:
                    b = 2 * c + s
                    conv_p[l, b], lru_p[l, b] = conv, hv
                    sre_p[l, b], sim_p[l, b] = st[:, :, 0], st[:, :, 1]
                    k_p[l, b] = ok[:, l, s * 128:(s + 1) * 128].T.reshape(128, 2, 64)
                    v_p[l, b] = ov[:, l, s].reshape(128, 2, 64)
                else:
                    conv_s[l, c], lru_s[l, c] = conv, hv
                    sre_s[l, c], sim_s[l, c] = st[:, :, 0], st[:, :, 1]
                    k_s[l, c] = ok[:, l, 256:272].T.reshape(NSAMP, 2, 64)
                    v_s[l, c] = ov[0:NSAMP, l, s].reshape(NSAMP, 2, 64)
    return (y_p, y_s, conv_p, lru_p, sre_p, sim_p, k_p, v_p, conv_s, lru_s, sre_s, sim_s, k_s, v_s)
```

```python
import math
import os
from contextlib import ExitStack

import numpy as np
import concourse.bass as bass
import concourse.mybir as mybir
from concourse.bass_utils import run_bass_kernel_spmd

F32 = mybir.dt.float32
BF16 = mybir.dt.bfloat16
I32 = mybir.dt.int32
AF = mybir.ActivationFunctionType
ALU = mybir.AluOpType

L = 2
D = 1024
SEQ = 4096
NSAMP = 16
NT = 512
NCOL = 2 * SEQ + NSAMP
NCH = 23
CH_IN, CH_OUT, CH_UP, CH_DN = 0, 5, 7, 15
GELU_K = 1.5957691216057308
TWO_PI = 2.0 * math.pi

PV = {}
_off = 0


def _pv(name, n):
    global _off
    PV[name] = _off
    _off += n


for _l in range(L):
    for _nm, _n in (("norm1", 8), ("norm2", 8), ("conv_w", 8), ("conv_b", 2), ("b_rg", 2), ("b_ig", 2),
                    ("lam", 2), ("ssm_d", 2), ("b_glu", 2), ("a_re", 8), ("a_im", 8), ("log_dt", 8),
                    ("sinks", 4)):
        _pv(f"{_nm}{_l}", _n)
_pv("norm_f", 8)
_pv("halfpi", 1)
NPV = _off

AUX = {}
_o = 0
for _nm, _w in (("pv", NPV), ("gbd", L * 2 * 2 * 128), ("bl", L * 2 * 8 * 128), ("cl", L * 2 * 8 * 64),
                ("wglu", L * 2 * 256), ("rope", 2 * (SEQ + NSAMP)), ("jj", 128), ("m16", 64),
                ("conv0", L * 6), ("h0", L * 2), ("s0", L * 16), ("kc", L * 128), ("vc", L * 128)):
    AUX[_nm] = (_o, _w)
    _o += _w
AUXW = _o
OS = {}
_o = 0
for _nm, _w in (("o_conv", L * 3 * 6), ("o_h", L * 3 * 2), ("o_s", L * 3 * 16), ("o_k", L * 272), ("o_v", L * 3 * 128)):
    OS[_nm] = (_o, _w)
    _o += _w
OSW = _o

EPOCH = 8000
NDSEM = 16


class Op:
    __slots__ = ("eng", "fn", "deps", "sig", "dma", "sem", "val", "cover")


class Prog:
    ENGS = ("tensor", "vector", "scalar", "gpsimd", "sync")

    def __init__(self):
        self.q = {e: [] for e in self.ENGS}
        self.state = {}

    def op(self, eng, fn, r=(), w=(), sig=True, dma=False):
        o = Op()
        o.eng, o.fn, o.sig, o.dma = eng, fn, sig, dma
        o.sem = o.val = o.cover = None
        deps = []
        for k in r:
            st = self.state.setdefault(k, [[], []])
            deps.extend(st[0])
            st[1].append(o)
        for k in w:
            st = self.state.setdefault(k, [[], []])
            rd = [x for x in st[1] if x is not o]
            if rd or (o in st[1]):
                deps.extend(rd)
                deps.extend(st[0])
                st[0] = [o]
                st[1] = []
            else:
                for x in st[0]:
                    if x.eng != eng or x.dma:
                        deps.append(x)
                st[0].append(o)
                if len(st[0]) > 64:
                    st[0] = st[0][-64:]
        seen = set()
        dd = []
        for x in deps:
            if x is o or id(x) in seen:
                continue
            seen.add(id(x))
            if x.eng == "tensor" and eng == "tensor" and not x.dma:
                continue
            dd.append(x)
        o.deps = dd
        self.q[eng].append(o)
        return o

    def emit(self, nc, es):
        csem = {}
        for e in ("tensor", "vector", "scalar", "gpsimd"):
            ops = [o for o in self.q[e] if not o.dma]
            nsig = sum(1 for o in ops if o.sig)
            nep = max(1, (nsig + EPOCH - 1) // EPOCH)
            csem[e] = [es.enter_context(nc.semaphore(f"c_{e}_{i}")) for i in range(nep)]
            cnt = 0
            for o in ops:
                if o.sig:
                    o.sem = csem[e][cnt // EPOCH]
                    o.val = cnt % EPOCH + 1
                    cnt += 1
            nxt = None
            for o in reversed(ops):
                if o.sig:
                    nxt = o
                    o.cover = o
                else:
                    assert nxt is not None, "unsignaled trailing op"
                    o.cover = nxt
        for e in self.ENGS:
            dops = [o for o in self.q[e] if o.dma]
            if not dops:
                continue
            pool = [es.enter_context(nc.semaphore(f"d_{e}_{i}")) for i in range(NDSEM)]
            for i, o in enumerate(dops):
                o.sem = pool[i % NDSEM]
                o.val = 16 * (i // NDSEM + 1)
                o.cover = o
                if i >= NDSEM:
                    o.deps.append(dops[i - NDSEM])
        block = es.enter_context(nc.Block())
        prog = self

        def run(engobj, ename):
            maxw = {}
            for o in prog.q[ename]:
                need = {}
                for d in o.deps:
                    c = d.cover
                    key = id(c.sem)
                    if maxw.get(key, 0) >= c.val:
                        continue
                    if key not in need or need[key][1] < c.val:
                        need[key] = (c.sem, c.val)
                for key, (sem_, val_) in need.items():
                    engobj.wait_ge(sem_, val_)
                    maxw[key] = val_
                ins = o.fn(engobj)
                if o.dma:
                    ins.then_inc(o.sem, 16)
                elif o.sig:
                    ins.then_inc(o.sem, 1)

        @block.tensor
        def _(t):
            run(t, "tensor")

        @block.vector
        def _(v):
            run(v, "vector")

        @block.scalar
        def _(s):
            run(s, "scalar")

        @block.gpsimd
        def _(g):
            run(g, "gpsimd")

        @block.sync
        def _(sy):
            run(sy, "sync")


class _Stop(Exception):
    pass


def build_nc(n_tiles_limit=None, stop=None, tile_sel=None):
    nc = bass.Bass("TRN2", target_bir_lowering=False)

    def chk(name):
        if stop == name:
            raise _Stop()
    dr = lambda n, s, dt=F32, kind="ExternalInput": nc.dram_tensor(n, list(s), dt, kind=kind).ap()
    xT = dr("xT", [8, 128, NCOL])
    wf = dr("wf", [L * NCH, 128, 4096])
    aux = dr("aux", [128, AUXW])
    ax = lambda nm: aux[:, AUX[nm][0]:AUX[nm][0] + AUX[nm][1]]
    pvd, gbd_d, bl_d, cl_d, wglu_d = ax("pv"), ax("gbd"), ax("bl"), ax("cl"), ax("wglu")
    rope_d = ax("rope").rearrange("p (a b) -> p a b", a=2)
    jj_d, m16_d, conv0_d, h0_d, s0_d, kc_d, vc_d = (ax(k_) for k_ in ("jj", "m16", "conv0", "h0", "s0", "kc", "vc"))
    yT = dr("yT", [8, 128, NCOL], kind="ExternalOutput")
    osm = dr("osm", [128, OSW], kind="ExternalOutput")
    ox = lambda nm: osm[:, OS[nm][0]:OS[nm][0] + OS[nm][1]]
    o_conv, o_h, o_s, o_k, o_v = ox("o_conv"), ox("o_h"), ox("o_s"), ox("o_k"), ox("o_v")
    wb = dr("wb", [L * NCH, 128, 4096], BF16, kind="Internal")

    P = Prog()
    es = ExitStack()
    sb = lambda n, s, dt=F32: es.enter_context(nc.sbuf_tensor(n, list(s), dt))

    x = sb("x", [128, 8, NT])
    hn = sb("hn", [128, 8, NT], BF16)
    sqb = [sb(f"sq{i}", [128, NT], BF16) for i in range(2)]
    ua = sb("ua", [128, 2, 3 + NT])
    ga = sb("ga", [128, 2, NT], BF16)
    ub = sb("ub", [128, 2, NT])
    ubb = sb("ubb", [128, 2, NT], BF16)
    tA = [sb(f"tA{i}", [128, NT]) for i in range(2)]
    qT = sb("qT", [128, 4, NT], BF16)
    kT = [sb(f"kT{l}", [128, 128 + NT], BF16) for l in range(L)]
    vv = [sb(f"vv{l}", [128, 5, 128], BF16) for l in range(L)]
    vf = sb("vf", [128, 128])
    mix = sb("mix", [128, 8, NT], BF16)
    act = sb("act", [128, 32, NT], BF16)
    NW = 3
    wring = [sb(f"wr{i}", [128, 8, 512], BF16) for i in range(NW)]
    s5t = [sb(f"s5t{i}", [128, NT]) for i in range(8)]
    rstd, kf = s5t[1], s5t[3]
    tA = tA + [s5t[7], s5t[2]]
    rr, ig, hh = s5t[4], s5t[5], s5t[6]
    zre = sb("zre", [128, NT])
    zim = sb("zim", [128, NT])
    Tc = [sb(f"Tc{l}", [128, 8, 128]) for l in range(L)]
    Ts = [sb(f"Ts{l}", [128, 8, 128]) for l in range(L)]
    Uc = [sb(f"Uc{l}", [128, 8, 128]) for l in range(L)]
    Us = [sb(f"Us{l}", [128, 8, 128]) for l in range(L)]
    magt = [sb(f"magt{l}", [128, 8, 128]) for l in range(L)]
    ET = [sb(f"ET{i}", [128, 2, 256], BF16) for i in range(2)]
    rcp = sb("rcp", [128, 256])
    ropeb = sb("ropeb", [128, 2, NT])
    pv = sb("pvs", [128, NPV])
    jj = sb("jjs", [128, 128])
    gbd = sb("gbds", [128, L * 2 * 2 * 128], BF16)
    blb = sb("blb", [128, L * 2 * 8 * 128], BF16)
    clb = sb("clb", [128, L * 3 * 8 * 64], BF16)
    wglu = sb("wglus", [128, L * 2 * 256], BF16)
    ones_bf = sb("ones_bf", [128, 128], BF16)
    m16b = sb("m16b", [128, 64], BF16)
    c8 = sb("c8", [128, L * 2])
    c16 = sb("c16", [128, L * 2])
    esink = sb("esink", [128, L * 4])
    hst = sb("hst", [128, L * 2])
    s5s = sb("s5s", [128, L * 8 * 2])
    xcb = sb("xcb", [128, NT], BF16)
    zbf = sb("zbf", [128, 2, NT], BF16)
    tiny = sb("tiny", [128, 8])
    uahist = [sb(f"uahist{l}", [128, 2, 3]) for l in range(L)]
    ps = es.enter_context(nc.psum_tensor("ps", [128, 8, 512], F32))

    V, A, G, T, S = "vector", "scalar", "gpsimd", "tensor", "sync"
    _cfg = os.environ.get("S5CFG", "GGGG")
    E3, E7, E8, E9 = (V if c == "V" else G for c in _cfg)

    def dma(out, in_, r, w, eng=S, **kw):
        return P.op(eng, lambda e: e.dma_start(out=out, in_=in_, **kw), r=r, w=w, dma=True)

    def act_(out, in_, func, r, w, bias=None, scale=None):
        kw = {}
        if bias is not None:
            kw["bias"] = bias
        if scale is not None:
            kw["scale"] = scale
        return P.op(A, lambda e: e.activation(out=out, in_=in_, func=func, **kw), r=r, w=w)

    def tt(eng, out, a, b, op, r, w):
        return P.op(eng, lambda e: e.tensor_tensor(out, a, b, op), r=r, w=w)

    def ts(eng, out, a, s1, s2, op0, op1, r, w):
        if s2 is None:
            return P.op(eng, lambda e: e.tensor_scalar(out, a, s1, None, op0), r=r, w=w)
        return P.op(eng, lambda e: e.tensor_scalar(out, a, s1, s2, op0, op1), r=r, w=w)

    def stt(out, a, s, b, op0, op1, r, w):
        return P.op(V, lambda e: e.scalar_tensor_tensor(out, a, s, b, op0, op1), r=r, w=w)

    def cp(eng, out, in_, r, w):
        if eng == A:
            return P.op(A, lambda e: e.copy(out, in_), r=r, w=w)
        return P.op(eng, lambda e: e.tensor_copy(out, in_), r=r, w=w)

    def scan(out, d0, d1, init, r, w):
        return P.op(V, lambda e: e.tensor_tensor_scan(out, d0, d1, init, ALU.mult, ALU.add), r=r, w=w)

    def recip(out, in_, r, w):
        return P.op(V, lambda e: e.reciprocal(out, in_), r=r, w=w)

    def memset(eng, ap_, val, w):
        return P.op(eng, lambda e: e.memset(ap_, val), w=w)

    def mm(out, lhsT, rhs, start, stop, r, w, sig):
        return P.op(T, lambda e: e.matmul(out, lhsT, rhs, start=start, stop=stop), r=r, w=w, sig=sig)

    for c in range(L * NCH):
        dma(wb[c], wf[c], r=(("castq", c % 4),), w=(("wb", c), ("castq", c % 4)), eng=G, max_dma_last_dim=2048)

    dma(pv[:, :], pvd[:, :], (), ("pv",))
    dma(jj[:, :], jj_d[:, :], (), ("jj",))
    xflat = x[:, :, :].rearrange("p a b -> p (a b)")
    XK = tuple(("x", k) for k in range(8))
    dma(xflat[:, 0:L * 512], gbd_d[:, :], (), XK)
    cp(V, gbd[:, :], xflat[:, 0:L * 512], XK, ("gbd",))
    dma(xflat[:, 0:L * 512], wglu_d[:, :], XK, XK)
    cp(V, wglu[:, :], xflat[:, 0:L * 512], XK, ("wglu",))
    dma(xflat[:, 0:L * 1024], cl_d[:, :], XK, XK)
    for l in range(L):
        o0 = l * 1024
        c0_ = l * 1536
        cp(V, clb[:, c0_:c0_ + 512], xflat[:, o0:o0 + 512], XK, ("clb",))
        ts(V, clb[:, c0_ + 512:c0_ + 1024], xflat[:, o0 + 512:o0 + 1024], -1.0, None, ALU.mult, None, XK, ("clb",))
        ts(V, clb[:, c0_ + 1024:c0_ + 1536], xflat[:, o0:o0 + 512], -1.0, None, ALU.mult, None, XK, ("clb",))
    for l in range(L):
        dma(xflat[:, 0:2048], bl_d[:, l * 2048:(l + 1) * 2048], XK + ("clb",), XK)
        cp(V, blb[:, l * 2048:(l + 1) * 2048], xflat[:, 0:2048], XK, ("blb",))
    memset(V, ones_bf[:, :], 1.0, ("ones",))
    dma(xflat[:, 0:64], m16_d[:, :], XK + ("blb",), XK)
    cp(V, m16b[:, :], xflat[:, 0:64], XK, ("m16b",))
    for l in range(L):
        lam = pv[:, PV[f"lam{l}"]:PV[f"lam{l}"] + 2]
        act_(tA[0][:, 0:2], lam, AF.Exp, ("pv",), ("tA0",), scale=-1.0)
        act_(tA[0][:, 2:4], tA[0][:, 0:2], AF.Ln, ("tA0",), ("tA0",), bias=1.0)
        ts(V, c8[:, 2 * l:2 * l + 2], tA[0][:, 2:4], -8.0, None, ALU.mult, None, ("tA0",), ("c8",))
        ts(V, c16[:, 2 * l:2 * l + 2], tA[0][:, 2:4], -16.0, None, ALU.mult, None, ("tA0",), ("c8",))
        sk = pv[:, PV[f"sinks{l}"]:PV[f"sinks{l}"] + 4]
        act_(esink[:, 4 * l:4 * l + 4], sk, AF.Exp, ("pv",), ("esink",))
    hp = pv[:, PV["halfpi"]:PV["halfpi"] + 1]
    for l in range(L):
        are = pv[:, PV[f"a_re{l}"]:PV[f"a_re{l}"] + 8]
        aim = pv[:, PV[f"a_im{l}"]:PV[f"a_im{l}"] + 8]
        ldt = pv[:, PV[f"log_dt{l}"]:PV[f"log_dt{l}"] + 8]
        sm = tA[1]
        k_ = ("tA1",)
        dtv, lrdt, magv, th = sm[:, 0:8], sm[:, 8:16], sm[:, 16:24], sm[:, 24:32]
        act_(dtv, ldt, AF.Exp, ("pv",), k_)
        tt(V, lrdt, are, dtv, ALU.mult, ("pv",) + k_, k_)
        act_(magv, lrdt, AF.Exp, k_, k_)
        tt(V, th, aim, dtv, ALU.mult, ("pv",) + k_, k_)
        ang = [s5t[0], s5t[1]]
        def view(tl):
            return [tl[i][:, :].rearrange("p (a b) -> p a b", a=4) for i in range(2)]
        angv = view(ang)
        for pr in range(8):
            ts(V, angv[pr // 4][:, pr % 4, :], jj[:, :], th[:, pr:pr + 1], None, ALU.mult, None,
               ("jj",) + k_, ("s5t0", "s5t1"))
            ts(V, magt[l][:, pr, :], jj[:, :], 0.0, magv[:, pr:pr + 1], ALU.mult, ALU.add, ("jj",) + k_, (f"magt{l}",))
        Ucv = [Uc[l][:, 0:4, :], Uc[l][:, 4:8, :]]
        Usv = [Us[l][:, 0:4, :], Us[l][:, 4:8, :]]
        for hf in range(2):
            a_ = angv[hf]
            t1 = view([s5t[2], s5t[3]])[hf]
            t2 = view([s5t[4], s5t[5]])[hf]
            t3 = view([s5t[6], s5t[7]])[hf]
            ti = s5t[2 + hf][:, :].bitcast(I32).rearrange("p (a b) -> p a b", a=4)
            kk = tuple(f"s5t{i}" for i in range(8))
            ts(V, t2, a_, 1.0 / TWO_PI, None, ALU.mult, None, kk, kk)
            cp(V, ti, t2, kk, kk)
            cp(V, t2, ti, kk, kk)
            stt(t3, t2, -TWO_PI, a_, ALU.mult, ALU.add, kk, kk)
            act_(t1, t3, AF.Sin, kk, kk, scale=0.25)
            act_(t2, t3, AF.Sin, kk, kk, scale=0.25, bias=hp)
            stt(t3, t1, 2.0, t2, ALU.mult, ALU.mult, kk, kk)
            tt(V, t2, t1, t1, ALU.mult, kk, kk)
            ts(V, t2, t2, -2.0, 1.0, ALU.mult, ALU.add, kk, kk)
            stt(Usv[hf], t3, 2.0, t2, ALU.mult, ALU.mult, kk, (f"U{l}",))
            tt(V, t1, t3, t3, ALU.mult, kk, kk)
            ts(V, Ucv[hf], t1, -2.0, 1.0, ALU.mult, ALU.add, kk, (f"U{l}",))
        abr, abi, den, nr_, fr, fi, w1, w2 = (sm[:, 32 + 8 * i:40 + 8 * i] for i in range(8))
        kU = (f"U{l}",)
        tt(V, abr, magv, Uc[l][:, :, 0], ALU.mult, k_ + kU, k_)
        tt(V, abi, magv, Us[l][:, :, 0], ALU.mult, k_ + kU, k_)
        tt(V, den, are, are, ALU.mult, ("pv",), k_)
        tt(V, w1, aim, aim, ALU.mult, ("pv",), k_)
        tt(V, den, den, w1, ALU.add, k_, k_)
        recip(den, den, k_, k_)
        ts(V, nr_, abr, -1.0, None, ALU.add, None, k_, k_)
        tt(V, w1, nr_, are, ALU.mult, k_ + ("pv",), k_)
        tt(V, w2, abi, aim, ALU.mult, k_ + ("pv",), k_)
        tt(V, w1, w1, w2, ALU.add, k_, k_)
        tt(V, fr, w1, den, ALU.mult, k_, k_)
        tt(V, w1, abi, are, ALU.mult, k_ + ("pv",), k_)
        tt(V, w2, nr_, aim, ALU.mult, k_ + ("pv",), k_)
        tt(V, w1, w1, w2, ALU.subtract, k_, k_)
        tt(V, fi, w1, den, ALU.mult, k_, k_)
        for pr in range(8):
            tmp = tA[2][:, 0:128]
            ts(V, tmp, Us[l][:, pr, :], fi[:, pr:pr + 1], None, ALU.mult, None, k_ + kU, ("s5t7",))
            stt(Tc[l][:, pr, :], Uc[l][:, pr, :], fr[:, pr:pr + 1], tmp, ALU.mult, ALU.add, k_ + kU + ("s5t7",), (f"T{l}",))
            ts(V, tmp, Us[l][:, pr, :], fr[:, pr:pr + 1], None, ALU.mult, None, k_ + kU, ("s5t7",))
            stt(Ts[l][:, pr, :], Uc[l][:, pr, :], fi[:, pr:pr + 1], tmp, ALU.mult, ALU.subtract, k_ + kU + ("s5t7",), (f"T{l}",))

    wstate = {"n": 0}
    wsched = []

    def wload(ci_global):
        slot = wstate["n"] % NW
        wstate["n"] += 1
        dma(wring[slot][:, :, :].rearrange("p a b -> p (a b)"), wb[ci_global], r=(("wb", ci_global),), w=(("w", slot),))
        return slot

    tiles = []
    for s in range(2):
        for j in range(SEQ // NT):
            tiles.append((s, j * NT, NT, s * SEQ + j * NT, j * NT))
    tiles.append((2, 0, NSAMP, 2 * SEQ, SEQ))
    if n_tiles_limit is not None:
        tiles = tiles[:n_tiles_limit]
    if tile_sel is not None:
        tiles = [tiles[i] for i in tile_sel]
    out_dmas = []

    order = []
    for ti in range(len(tiles)):
        for l in range(L):
            order.extend(l * NCH + c for c in range(NCH))
    wq = {"issued": 0, "slots": []}

    def wprefetch(upto):
        while wq["issued"] < min(upto, len(order)):
            wq["slots"].append(wload(order[wq["issued"]]))
            wq["issued"] += 1

    wcur = {"i": 0}

    def wnext():
        i = wcur["i"]
        wprefetch(i + NW)
        wcur["i"] += 1
        return wq["slots"][i]

    psn = {"i": 0}

    def bank4():
        b = psn["i"] % 4
        psn["i"] += 1
        return b

    psn8 = {"i": 0}

    def bank8():
        b = psn8["i"] % 8
        psn8["i"] += 1
        return b

    def rmsnorm(n, gcol, dst_hn):
        b = 4 + (psn["i"] % 2)
        psn["i"] += 1
        for k in range(8):
            sq = sqb[k % 2]
            act_(sq[:, 0:n], x[:, k, 0:n], AF.Square, (("x", k),), (f"sq{k % 2}",))
            mm(ps[:, b, 0:n], ones_bf[:, :], sq[:, 0:n], k == 0, k == 7, (f"sq{k % 2}", "ones"), (("ps", b),), True)
        act_(rstd[:, 0:n], ps[:, b, 0:n], AF.Ln, (("ps", b),), ("s5t1",), scale=1.0 / D, bias=1e-6)
        act_(rstd[:, 0:n], rstd[:, 0:n], AF.Exp, ("s5t1",), ("s5t1",), scale=-0.5)
        for k in range(8):
            if dst_hn:
                stt(hn[:, k, 0:n], x[:, k, 0:n], pv[:, gcol + k:gcol + k + 1], rstd[:, 0:n], ALU.mult, ALU.mult,
                    (("x", k), "pv", "s5t1"), (("hn", k),))
            else:
                yv = act[:, 2 * k:2 * k + 2, :].rearrange("p a b -> p (a b)").bitcast(F32)
                stt(yv[:, 0:n], x[:, k, 0:n], pv[:, gcol + k:gcol + k + 1], rstd[:, 0:n], ALU.mult, ALU.mult,
                    (("x", k), "pv", "s5t1"), (("act", 2 * k), ("act", 2 * k + 1)))

    def gelu_from(src, srckeys, n, dst, dstkeys, tmpi):
        a, b_ = tA[tmpi], tA[tmpi + 1]
        ka, kb = (f"tA{tmpi}",), (f"tA{tmpi + 1}",)
        act_(a[:, 0:n], src, AF.Square, srckeys, ka)
        ts(V, a[:, 0:n], a[:, 0:n], 0.044715, 1.0, ALU.mult, ALU.add, ka, ka)
        tt(V, b_[:, 0:n], src, a[:, 0:n], ALU.mult, srckeys + ka, kb)
        act_(a[:, 0:n], b_[:, 0:n], AF.Sigmoid, kb, ka, scale=GELU_K)
        tt(V, dst, src, a[:, 0:n], ALU.mult, srckeys + ka, dstkeys)

    def main_loop():
      for ti, (sq_id, t0, n, col0, pos0) in enumerate(tiles):
        first = (t0 == 0)
        last = (t0 + n == (SEQ if sq_id < 2 else NSAMP))
        samp = (sq_id == 2)
        nsub = (n + 127) // 128
        if samp:
            memset(V, hn[:, :, NSAMP:128], 0.0, tuple(("hn", k) for k in range(8)))
            for l_ in range(L):
                memset(V, kT[l_][:, 128 + NSAMP:256], 0.0, (f"kT{l_}",))
        for k in range(8):
            dma(x[:, k, 0:n], xT[k, :, col0:col0 + n], (), (("x", k),))
        dma(ropeb[:, :, 0:n], rope_d[:, :, pos0:pos0 + n], (), ("ropeb",))
        for l in range(L):
            pvl = lambda nm: PV[f"{nm}{l}"]
            if first:
                if samp:
                    dma(uahist[l][:, :, :], conv0_d[:, l * 6:(l + 1) * 6].rearrange("p (t j) -> p t j", t=2), (), (f"uahs{l}",))
                    dma(hst[:, 2 * l:2 * l + 2], h0_d[:, 2 * l:2 * l + 2], (), (f"hst{l}",))
                    dma(s5s[:, 16 * l:16 * l + 16], s0_d[:, 16 * l:16 * l + 16], (), (f"s5s{l}",))
                    dma(tA[3][:, 0:128], kc_d[:, l * 128:(l + 1) * 128], (), ("s5t2",))
                    cp(V, kT[l][:, 0:128], tA[3][:, 0:128], ("s5t2",), (f"kT{l}",))
                    dma(tA[3][:, 128:256], vc_d[:, l * 128:(l + 1) * 128], ("s5t2",), ("s5t2",))
                    cp(V, vv[l][:, 0, :], tA[3][:, 128:256], ("s5t2",), (f"vv{l}",))
                else:
                    memset(V, uahist[l][:, :, :], 0.0, (f"uahs{l}",))
                    memset(V, hst[:, 2 * l:2 * l + 2], 0.0, (f"hst{l}",))
                    memset(V, s5s[:, 16 * l:16 * l + 16], 0.0, (f"s5s{l}",))
            rmsnorm(n, pvl("norm1"), True)
            chk("norm1" if l == 0 else f"norm1_{l}")
            cp(A, ua[:, :, 0:3], uahist[l][:, :, :], (f"uahs{l}",), ("uah",))
            wslot = None
            for ci in range(4):
                wslot = wnext()
                wt = wring[wslot]
                for mi in range(4):
                    if ci in (2, 3) and mi % 2 == 1:
                        continue
                    mlist = [mi] if ci < 2 else [mi, mi + 1]
                    banks = []
                    for m in mlist:
                        b = bank8()
                        banks.append(b)
                        for k in range(8):
                            mm(ps[:, b, 0:n], wt[:, k, m * 128:(m + 1) * 128], hn[:, k, 0:n], k == 0, k == 7,
                               (("w", wslot), ("hn", k)), (("ps", b),), k == 7)
                    b = banks[0]
                    src = ps[:, b, 0:n]
                    sk = (("ps", b),)
                    if ci == 0 and mi < 2:
                        cp(A, ua[:, mi, 3:3 + n], src, sk, (f"ua{mi}",))
                    elif ci == 0:
                        gelu_from(src, sk, n, ga[:, mi - 2, 0:n], (f"ga{mi - 2}",), 0)
                    elif ci == 1 and mi < 2:
                        cp(A, ub[:, mi, 0:n], src, sk, (f"ub{mi}",))
                        cp(V, ubb[:, mi, 0:n], src, sk, (f"ubb{mi}",))
                    elif ci == 1 and mi == 2:
                        kb0 = b
                    elif ci == 1 and mi == 3:
                        b1 = b
                        tt(V, ps[:, kb0, 0:n], ps[:, kb0, 0:n], ropeb[:, 0, 0:n], ALU.mult, (("ps", kb0), "ropeb"), (("ps", kb0),))
                        tt(V, tA[1][:, 0:n], ps[:, b1, 0:n], ropeb[:, 1, 0:n], ALU.mult, (("ps", b1), "ropeb"), ("tA1",))
                        tt(V, kf[:, 0:n], ps[:, kb0, 0:n], tA[1][:, 0:n], ALU.add, (("ps", kb0), "tA1"), ("s5t3",))
                        cp(A, kT[l][:, 128:128 + n], kf[:, 0:n], ("s5t3",), (f"kT{l}",))
                        if last:
                            nk = min(n, 128)
                            oc = l * 272 + (sq_id * 128)
                            out_dmas.append(dma(o_k[:, oc:oc + nk], kf[:, n - nk:n], ("s5t3",), ()))
                    else:
                        tq = (ci - 2) * 2 + mi // 2
                        b0, b1 = banks
                        tt(V, ps[:, b0, 0:n], ps[:, b0, 0:n], ropeb[:, 0, 0:n], ALU.mult, (("ps", b0), "ropeb"), (("ps", b0),))
                        tt(V, tA[1][:, 0:n], ps[:, b1, 0:n], ropeb[:, 1, 0:n], ALU.mult, (("ps", b1), "ropeb"), ("tA1",))
                        tt(V, qT[:, tq, 0:n], ps[:, b0, 0:n], tA[1][:, 0:n], ALU.add, (("ps", b0), "tA1"), ("qT",))
            chk("inproj_a")
            wslot = wnext()
            wt = wring[wslot]
            b = bank4()
            for sbi in range(nsub):
                m_ = 128
                for k in range(8):
                    mm(ps[0:m_, b, sbi * 128:(sbi + 1) * 128], hn[:, k, sbi * 128:sbi * 128 + m_], wt[:, k, 0:128],
                       k == 0, k == 7, (("w", wslot), ("hn", k)), (("ps", b),), k == 7)
            cp(A, vv[l][:, 1:1 + nsub, :], ps[:, b, 0:nsub * 128].rearrange("p (a b) -> p a b", b=128), (("ps", b),), (f"vv{l}",))
            chk("v_mm")
            if last:
                vfb, vfk = tA[0][:, 0:128], "tA0"
                cp(A, vfb, ps[:, b, (nsub - 1) * 128:nsub * 128], (("ps", b),), (vfk,))
                oc = (l * 3 + sq_id) * 128
                out_dmas.append(dma(o_v[:, oc:oc + 128], vfb, (vfk,), ()))

            chk("inproj" if l == 0 else f"inproj_{l}")
            def actf(slot):
                return act[:, slot:slot + 2, :].rearrange("p a b -> p (a b)").bitcast(F32), (("act", slot), ("act", slot + 1))

            zf_ap = [actf(30), actf(0)]
            L_xc, K_xc = actf(0)
            L_rr, K_rr = actf(2)
            L_ig, K_ig = actf(4)
            L_hh, K_hh = actf(6)
            s5set = [dict(t=[s5t[i] for i in range(8)], tk=[(f"s5t{i}",) for i in range(8)],
                          zre=zre, zim=zim, kzre=("zre",), kzim=("zim",),
                          rb=[s5t[4 + i][:, :].bitcast(BF16) for i in range(4)], rbk=[(f"s5t{4 + i}",) for i in range(4)],
                          tiny=tiny[:, 0:4], ktiny=("tiny0",), banks=(0, 1)),
                     dict(t=[actf(8 + 2 * i)[0] for i in range(8)], tk=[actf(8 + 2 * i)[1] for i in range(8)],
                          zre=actf(24)[0], zim=actf(26)[0], kzre=actf(24)[1], kzim=actf(26)[1],
                          rb=[act[:, 16 + 2 * i, :] for i in range(4)], rbk=[actf(16 + 2 * i)[1] for i in range(4)],
                          tiny=tiny[:, 4:8], ktiny=("tiny1",), banks=(2, 3))]

            def lru_gen():
                for t in range(2):
                    cw = pvl("conv_w") + 4 * t
                    kua = (f"ua{t}", "uah")
                    xc, kx = L_xc, K_xc
                    ts(V, xc[:, 0:n], ua[:, t, 3:3 + n], pv[:, cw + 3:cw + 4], pv[:, pvl("conv_b") + t:pvl("conv_b") + t + 1],
                       ALU.mult, ALU.add, kua + ("pv",), kx)
                    for j in range(3):
                        stt(xc[:, 0:n], ua[:, t, j:j + n], pv[:, cw + j:cw + j + 1], xc[:, 0:n], ALU.mult, ALU.add,
                            kua + ("pv",) + kx, kx)
                        yield
                    cp(A, xcb[:, 0:n], xc[:, 0:n], kx, ("xcb",))
                    yield
                    g0 = ((l * 2 + 0) * 2 + t) * 128
                    g1 = ((l * 2 + 1) * 2 + t) * 128
                    bg = 7
                    mm(ps[:, bg, 0:n], gbd[:, g0:g0 + 128], xcb[:, 0:n], True, True, ("gbd", "xcb"), (("ps", bg),), True)
                    act_(L_rr[:, 0:n], ps[:, bg, 0:n], AF.Sigmoid, (("ps", bg), "pv"), K_rr,
                         bias=pv[:, pvl("b_rg") + t:pvl("b_rg") + t + 1])
                    yield
                    mm(ps[:, bg, 0:n], gbd[:, g1:g1 + 128], xcb[:, 0:n], True, True, ("gbd", "xcb"), (("ps", bg),), True)
                    act_(L_ig[:, 0:n], ps[:, bg, 0:n], AF.Sigmoid, (("ps", bg), "pv"), K_ig,
                         bias=pv[:, pvl("b_ig") + t:pvl("b_ig") + t + 1])
                    yield
                    aa, a2 = tA[0], tA[1]
                    act_(aa[:, 0:n], L_rr[:, 0:n], AF.Exp, K_rr + ("c8",), ("tA0",), scale=c8[:, 2 * l + t:2 * l + t + 1])
                    act_(a2[:, 0:n], L_rr[:, 0:n], AF.Exp, K_rr + ("c8",), ("tA1",), scale=c16[:, 2 * l + t:2 * l + t + 1])
                    yield
                    act_(a2[:, 0:n], a2[:, 0:n], AF.Sqrt, ("tA1",), ("tA1",), scale=-1.0, bias=1.0)
                    tt(V, L_ig[:, 0:n], L_ig[:, 0:n], xc[:, 0:n], ALU.mult, K_ig + kx, K_ig)
                    yield
                    tt(V, L_ig[:, 0:n], L_ig[:, 0:n], a2[:, 0:n], ALU.mult, K_ig + ("tA1",), K_ig)
                    yield
                    scan(L_hh[:, 0:n], aa[:, 0:n], L_ig[:, 0:n], hst[:, 2 * l + t:2 * l + t + 1],
                         ("tA0", f"hst{l}") + K_ig, K_hh)
                    yield
                    cp(A, hst[:, 2 * l + t:2 * l + t + 1], L_hh[:, n - 1:n], K_hh, (f"hst{l}",))
                    tt(V, mix[:, t, 0:n], L_hh[:, 0:n], ga[:, t, 0:n], ALU.mult, K_hh + (f"ga{t}",), (("mix", t),))
                    yield
                if last:
                    oc = (l * 3 + sq_id) * 6
                    out_dmas.append(dma(o_conv[:, oc:oc + 6].rearrange("p (t j) -> p t j", t=2), ua[:, :, n:n + 3],
                                        ("ua0", "ua1", "uah"), ()))
                    oc = (l * 3 + sq_id) * 2
                    out_dmas.append(dma(o_h[:, oc:oc + 2], hst[:, 2 * l:2 * l + 2], (f"hst{l}",), ()))
                cp(A, uahist[l][:, :, :], ua[:, :, n:n + 3], ("ua0", "ua1", "uah"), (f"uahs{l}",))

            nsb = max(1, n // 128)
            sl = min(n, 128)
            YB = 6

            def s5_pair_steps(pr, S_):
                ut = pr // 4
                b_re, b_im = S_["banks"]
                q, qk = S_["t"], S_["tk"]
                z_re, z_im, kzr, kzi = S_["zre"], S_["zim"], S_["kzre"], S_["kzim"]
                kT_, kU_ = (f"T{l}",), (f"U{l}",)
                ks = (f"s5s{l}",)
                sre = s5s[:, 16 * l + 2 * pr:16 * l + 2 * pr + 1]
                sim = s5s[:, 16 * l + 2 * pr + 1:16 * l + 2 * pr + 2]

                def v3(ap_):
                    return ap_.rearrange("p (a b) -> p a b", b=sl)

                def tb(tab):
                    return tab[:, pr, 0:sl].unsqueeze(1).broadcast_to([128, nsb, sl])

                steps = []

                def st0():
                    o_re = ((l * 2 + 0) * 8 + pr) * 128
                    o_im = ((l * 2 + 1) * 8 + pr) * 128
                    mm(ps[:, b_re, 0:n], blb[:, o_re:o_re + 128], ubb[:, ut, 0:n], True, True, ("blb", f"ubb{ut}"), (("ps", b_re),), True)
                    mm(ps[:, b_im, 0:n], blb[:, o_im:o_im + 128], ubb[:, ut, 0:n], True, True, ("blb", f"ubb{ut}"), (("ps", b_im),), True)
                steps.append(st0)

                kbr, kbi = (("ps", b_re),), (("ps", b_im),)

                def st1():
                    tt(V, v3(q[1][:, 0:n]), v3(ps[:, b_im, 0:n]), tb(Ts[l]), ALU.mult, kbi + kT_, qk[1])
                    tt(V, v3(q[3][:, 0:n]), v3(ps[:, b_re, 0:n]), tb(Ts[l]), ALU.mult, kbr + kT_, qk[3])
                steps.append(st1)

                def st2():
                    tt(V, v3(ps[:, b_re, 0:n]), v3(ps[:, b_re, 0:n]), tb(Tc[l]), ALU.mult, kbr + kT_, kbr)
                    tt(V, v3(ps[:, b_im, 0:n]), v3(ps[:, b_im, 0:n]), tb(Tc[l]), ALU.mult, kbi + kT_, kbi)
                steps.append(st2)

                def st3():
                    tt(V, q[0][:, 0:n], ps[:, b_re, 0:n], q[1][:, 0:n], ALU.subtract, kbr + qk[1], qk[0])
                    tt(V, q[2][:, 0:n], ps[:, b_im, 0:n], q[3][:, 0:n], ALU.add, kbi + qk[3], qk[2])
                steps.append(st3)
                for sbi in range(nsb):
                    c0, c1 = sbi * sl, (sbi + 1) * sl

                    def sa(c0=c0, c1=c1):
                        scan(ps[:, b_re, c0:c1], magt[l][:, pr, 0:sl], q[0][:, c0:c1], sre, qk[0] + (f"magt{l}",) + ks, kbr)
                        scan(ps[:, b_im, c0:c1], magt[l][:, pr, 0:sl], q[2][:, c0:c1], sim, qk[2] + (f"magt{l}",) + ks, kbi)
                    steps.append(sa)

                    def sb_(c0=c0, c1=c1):
                        ucl = Uc[l][:, pr, sl - 1:sl]
                        usl = Us[l][:, pr, sl - 1:sl]
                        w1 = S_["tiny"][:, 0:1]
                        w2 = S_["tiny"][:, 1:2]
                        kt = S_["ktiny"]
                        tt(V, w1, ps[:, b_im, c1 - 1:c1], usl, ALU.mult, kbi + kU_, kt)
                        tt(V, w2, ps[:, b_re, c1 - 1:c1], usl, ALU.mult, kbr + kU_, kt)
                        stt(sre, ps[:, b_re, c1 - 1:c1], ucl, w1, ALU.mult, ALU.subtract, kbr + kt + kU_, ks)
                        stt(sim, ps[:, b_im, c1 - 1:c1], ucl, w2, ALU.mult, ALU.add, kbi + kt + kU_, ks)
                    steps.append(sb_)

                rb, rbk = S_["rb"], S_["rbk"]

                def v3b(ap_):
                    return ap_[:, 0:n].rearrange("p (a b) -> p a b", b=sl)

                def st7():
                    tt(V, v3b(rb[1]), v3(ps[:, b_im, 0:n]), tb(Us[l]), ALU.mult, kbi + kU_, rbk[1])
                    tt(V, v3b(rb[3]), v3(ps[:, b_re, 0:n]), tb(Us[l]), ALU.mult, kbr + kU_, rbk[3])
                steps.append(st7)

                def st8():
                    tt(V, v3b(rb[0]), v3(ps[:, b_re, 0:n]), tb(Uc[l]), ALU.mult, kbr + kU_, rbk[0])
                    tt(V, v3b(rb[2]), v3(ps[:, b_im, 0:n]), tb(Uc[l]), ALU.mult, kbi + kU_, rbk[2])
                steps.append(st8)

                def st10():
                    oc_re = ((l * 3 + 0) * 8 + pr) * 64
                    oc_nim = ((l * 3 + 1) * 8 + pr) * 64
                    oc_nre = ((l * 3 + 2) * 8 + pr) * 64
                    po = 64 * ((pr % 4) // 2)
                    yo = ps[po:po + 64, YB, 0:n]
                    mm(yo, clb[:, oc_re:oc_re + 64], rb[0][:, 0:n], pr % 2 == 0, False, ("clb",) + rbk[0], (("ps", YB),), False)
                    mm(yo, clb[:, oc_nre:oc_nre + 64], rb[1][:, 0:n], False, False, ("clb",) + rbk[1], (("ps", YB),), False)
                    mm(yo, clb[:, oc_nim:oc_nim + 64], rb[2][:, 0:n], False, False, ("clb",) + rbk[2], (("ps", YB),), False)
                    mm(yo, clb[:, oc_nim:oc_nim + 64], rb[3][:, 0:n], False, pr % 2 == 1, ("clb",) + rbk[3], (("ps", YB),), True)
                steps.append(st10)
                return steps

            def s5_gen():
                for grp in range(4):
                    sa_ = s5_pair_steps(2 * grp, s5set[0])
                    sb2 = s5_pair_steps(2 * grp + 1, s5set[1])
                    for fa, fb in zip(sa_, sb2):
                        fa()
                        fb()
                        yield
                    if grp % 2 == 1:
                        t = grp // 2
                        stt(zf_ap[t][0][:, 0:n], ub[:, t, 0:n], pv[:, pvl("ssm_d") + t:pvl("ssm_d") + t + 1], ps[:, YB, 0:n],
                            ALU.mult, ALU.add, (f"ub{t}", "pv", ("ps", YB)), zf_ap[t][1])
                        yield
                if last:
                    oc = (l * 3 + sq_id) * 16
                    out_dmas.append(dma(o_s[:, oc:oc + 16], s5s[:, 16 * l:16 * l + 16], (f"s5s{l}",), ()))

            if samp:
                chunks = [(0, n, [(0, 0, 128), (1, 0, 128)])]
            else:
                chunks = []
                for c in range(n // 64):
                    if c % 2 == 0:
                        pcs = [(c // 2, 0, 128), (c // 2 + 1, 0, 64)]
                    else:
                        pcs = [((c - 1) // 2, 64, 128), ((c + 1) // 2, 0, 128)]
                    if first:
                        pcs = [p_ for p_ in pcs if p_[0] >= 1]
                    chunks.append((c * 64, 64, pcs))

            def attn_gen():
                for (qc0, nq, pcs) in chunks:
                    nqh = 4 * nq
                    bnd = 7
                    for kvh in range(2):
                        hp_ = slice(kvh * 64, kvh * 64 + 64)
                        et = ET[kvh]
                        bs = 4 + kvh
                        for pi, (vt, p0, p1) in enumerate(pcs):
                            kc0 = vt * 128 + p0
                            mm(ps[p0:p1, bs, pi * 256:pi * 256 + nqh].rearrange("p (a b) -> p a b", a=4),
                               kT[l][hp_, kc0:kc0 + (p1 - p0)], qT[hp_, :, qc0:qc0 + nq], True, True,
                               (f"kT{l}", "qT"), (("ps", bs),), True)
                            act_(et[p0:p1, pi, 0:nqh], ps[p0:p1, bs, pi * 256:pi * 256 + nqh], AF.Exp, (("ps", bs),), (f"ET{kvh}",), scale=0.125)
                        yield
                    for kvh in range(2):
                        hp_ = slice(kvh * 64, kvh * 64 + 64)
                        et = ET[kvh]
                        for pi, (vt, p0, p1) in enumerate(pcs):
                            stt_, stp = (pi == 0), (pi == len(pcs) - 1)
                            mm(ps[hp_, bnd, 0:nqh], vv[l][p0:p1, vt, kvh * 64:kvh * 64 + 64], et[p0:p1, pi, 0:nqh], stt_, stp,
                               (f"vv{l}", f"ET{kvh}"), (("ps", bnd),), stp)
                        for pi, (vt, p0, p1) in enumerate(pcs):
                            stt_, stp = (pi == 0), (pi == len(pcs) - 1)
                            onesl = m16b[p0:p1, 0:64] if (samp and pi == 1) else ones_bf[p0:p1, 0:64]
                            mm(ps[hp_, bnd, 256:256 + nqh], onesl, et[p0:p1, pi, 0:nqh], stt_, stp,
                               ("ones", "m16b", f"ET{kvh}"), (("ps", bnd),), stp)
                        yield
                    r3 = rcp[:, 0:nqh].rearrange("p (a b) -> p a b", a=4)
                    for t4 in range(4):
                        act_(rcp[:, t4 * nq:(t4 + 1) * nq], ps[:, bnd, 256 + t4 * nq:256 + (t4 + 1) * nq], AF.Ln,
                             (("ps", bnd), "esink"), ("rcp",), bias=esink[:, 4 * l + t4:4 * l + t4 + 1])
                    act_(rcp[:, 0:nqh], rcp[:, 0:nqh], AF.Exp, ("rcp",), ("rcp",), scale=-1.0)
                    tt(V, mix[:, 4:8, qc0:qc0 + nq], ps[:, bnd, 0:nqh].rearrange("p (a b) -> p a b", a=4), r3, ALU.mult,
                       (("ps", bnd), "rcp"), tuple(("mix", 4 + i) for i in range(4)))
                    yield

            gens = [[s5_gen(), 1], [attn_gen(), 2], [lru_gen(), 2]]
            rnd = 0
            while gens:
                for ge in list(gens):
                    if rnd % ge[1] != 0 and len(gens) > 1:
                        continue
                    try:
                        next(ge[0])
                    except StopIteration:
                        gens.remove(ge)
                rnd += 1
            chk("s5" if l == 0 else f"s5_{l}")
            for t in range(2):
                gelu_from(zf_ap[t][0][:, 0:n], zf_ap[t][1], n, zf_ap[t][0][:, 0:n], zf_ap[t][1], 0)
                cp(A, zbf[:, t, 0:n], zf_ap[t][0][:, 0:n], zf_ap[t][1], ("zbf",))
            for t in range(2):
                b = bank4()
                for k in range(2):
                    o_ = (l * 2 + k) * 256 + t * 128
                    mm(ps[:, b, 0:n], wglu[:, o_:o_ + 128], zbf[:, k, 0:n], k == 0, k == 1, ("wglu", "zbf"), (("ps", b),), k == 1)
                act_(tA[0][:, 0:n], ps[:, b, 0:n], AF.Sigmoid, (("ps", b), "pv"), ("tA0",),
                     bias=pv[:, pvl("b_glu") + t:pvl("b_glu") + t + 1])
                tt(V, mix[:, 2 + t, 0:n], zf_ap[t][0][:, 0:n], tA[0][:, 0:n], ALU.mult, zf_ap[t][1] + ("tA0",), (("mix", 2 + t),))
            chk("attn" if l == 0 else f"attn_{l}")
            if not last:
                cp(A, kT[l][:, 0:128], kT[l][:, n:n + 128], (f"kT{l}",), (f"kT{l}",))
                cp(A, vv[l][:, 0, :], vv[l][:, 4, :], (f"vv{l}",), (f"vv{l}",))

            for ci in range(2):
                wslot = wnext()
                wt = wring[wslot]
                for mi in range(4):
                    m = ci * 4 + mi
                    b = bank4()
                    for k in range(8):
                        mm(ps[:, b, 0:n], wt[:, k, mi * 128:(mi + 1) * 128], mix[:, k, 0:n], k == 0, k == 7,
                           (("w", wslot), ("mix", k)), (("ps", b),), k == 7)
                    tt(V, x[:, m, 0:n], x[:, m, 0:n], ps[:, b, 0:n], ALU.add, (("x", m), ("ps", b)), (("x", m),))
            chk("outproj" if l == 0 else f"outproj_{l}")
            rmsnorm(n, pvl("norm2"), True)
            for ci in range(8):
                wslot = wnext()
                wt = wring[wslot]
                for mi in range(4):
                    b = bank4()
                    for k in range(8):
                        mm(ps[:, b, 0:n], wt[:, k, mi * 128:(mi + 1) * 128], hn[:, k, 0:n], k == 0, k == 7,
                           (("w", wslot), ("hn", k)), (("ps", b),), k == 7)
                    tmp = tA[mi % 2]
                    act_(tmp[:, 0:n], ps[:, b, 0:n], AF.Relu, (("ps", b),), (f"tA{mi % 2}",))
                    tt(V, act[:, ci * 4 + mi, 0:n], ps[:, b, 0:n], tmp[:, 0:n], ALU.mult, (("ps", b), f"tA{mi % 2}"),
                       (("act", ci * 4 + mi),))
            for mh in range(2):
                for kq in range(4):
                    wslot = wnext()
                    wt = wring[wslot]
                    for mi in range(4):
                        b = mh * 4 + mi
                        for k in range(8):
                            kk = kq * 8 + k
                            mm(ps[:, b, 0:n], wt[:, k, mi * 128:(mi + 1) * 128], act[:, kk, 0:n], kk == 0, kk == 31,
                               (("w", wslot), ("act", kk)), (("ps", b),), kk == 31 or k == 7)
                for mi in range(4):
                    b = mh * 4 + mi
                    m = mh * 4 + mi
                    tt(V, x[:, m, 0:n], x[:, m, 0:n], ps[:, b, 0:n], ALU.add, (("x", m), ("ps", b)), (("x", m),))
            psn["i"] = 0
            chk(f"layer{l}")
        rmsnorm(n, PV["norm_f"], False)
        chk("fnorm")
        for k in range(8):
            yv = act[:, 2 * k:2 * k + 2, :].rearrange("p a b -> p (a b)").bitcast(F32)
            out_dmas.append(dma(yT[k, :, col0:col0 + n], yv[:, 0:n], (("act", 2 * k), ("act", 2 * k + 1)), (("yT", k),)))

    try:
        chk("prologue")
        main_loop()
    except _Stop:
        pass
    fin = Op()
    fin.eng, fin.fn, fin.sig, fin.dma = S, (lambda e: e.nop()), False, False
    fin.deps = list(out_dmas)
    fin.sem = fin.val = fin.cover = None
    P.q[S].append(fin)
    P.emit(nc, es)
    es.close()
    return nc


_NC_CACHE = {}


def _rope_tables():
    half = 32
    inv = (10000.0 ** (-np.arange(half, dtype=np.float32) / half)).astype(np.float32)
    pos = np.concatenate([np.arange(SEQ), SEQ + np.arange(NSAMP)]).astype(np.float32)
    ang = pos[None, :] * inv[:, None]
    cos = np.cos(ang).astype(np.float32)
    sin = np.sin(ang).astype(np.float32)
    d = np.arange(128) % 64
    i = d % 32
    sign = np.where(d < 32, -1.0, 1.0).astype(np.float32)
    tab = np.empty((128, 2, SEQ + NSAMP), np.float32)
    tab[:, 0, :] = cos[i]
    tab[:, 1, :] = sin[i] * sign[:, None]
    return tab


def _prep_shared(inp):
    f = lambda a: np.asarray(a, dtype=np.float32)
    w_in, w_out, w_up, w_down = f(inp["w_in"]), f(inp["w_out"]), f(inp["w_up"]), f(inp["w_down"])
    o3, o4, o5 = 768, 1280, 1408
    hd = np.arange(64)
    sw = (hd + 32) % 64
    qcols = lambda h: o3 + 64 * h + hd
    qscols = lambda h: o3 + 64 * h + sw
    cols = []
    cols += list(range(0, 256))
    cols += list(range(256, 512))
    cols += list(range(512, 768))
    cols += list(range(o4, o4 + 128))
    cols += list(np.concatenate([o4 + 64 * kv + sw for kv in range(2)]))
    for t in range(4):
        cols += list(np.concatenate([qcols(t), qcols(4 + t)]))
        cols += list(np.concatenate([qscols(t), qscols(4 + t)]))
    cols += list(range(o5, o5 + 128))
    cols = np.asarray(cols)
    assert cols.shape[0] == 2176
    rows_out = list(range(0, 512))
    for t in range(4):
        rows_out += list(512 + 64 * t + hd) + list(512 + 64 * (4 + t) + hd)
    rows_out = np.asarray(rows_out)
    wf = np.zeros((L * NCH, 128, 8, 512), np.float32)
    for l in range(L):
        we = w_in[l][:, cols].reshape(8, 128, 2176)
        for ci in range(4):
            wf[l * NCH + ci] = we[:, :, 512 * ci:512 * ci + 512].transpose(1, 0, 2)
        wf[l * NCH + 4][:, :, 0:128] = we[:, :, 2048:2176].transpose(1, 0, 2)
        wo = w_out[l][rows_out, :].reshape(8, 128, 1024)
        for ci in range(2):
            wf[l * NCH + CH_OUT + ci] = wo[:, :, 512 * ci:512 * ci + 512].transpose(1, 0, 2)
        wu = w_up[l].reshape(8, 128, 4096)
        for ci in range(8):
            wf[l * NCH + CH_UP + ci] = wu[:, :, 512 * ci:512 * ci + 512].transpose(1, 0, 2)
        wd = w_down[l].reshape(32, 128, 1024)
        for mh in range(2):
            for kq in range(4):
                wf[l * NCH + CH_DN + mh * 4 + kq] = wd[8 * kq:8 * kq + 8, :, 512 * mh:512 * mh + 512].transpose(1, 0, 2)
    wf = wf.reshape(L * NCH, 128, 4096)

    pvv = np.zeros((128, NPV), np.float32)
    t128 = lambda v, n: f(v).reshape(n, 128).T
    for l in range(L):
        pvv[:, PV[f"norm1{l}"]:PV[f"norm1{l}"] + 8] = t128(inp["norm1"][l], 8)
        pvv[:, PV[f"norm2{l}"]:PV[f"norm2{l}"] + 8] = t128(inp["norm2"][l], 8)
        cw = f(inp["conv_w"][l])
        for t in range(2):
            pvv[:, PV[f"conv_w{l}"] + 4 * t:PV[f"conv_w{l}"] + 4 * t + 4] = cw[:, t * 128:(t + 1) * 128].T
        for nm in ("conv_b", "b_rg", "b_ig", "ssm_d", "b_glu"):
            pvv[:, PV[f"{nm}{l}"]:PV[f"{nm}{l}"] + 2] = t128(inp[nm][l], 2)
        pvv[:, PV[f"lam{l}"]:PV[f"lam{l}"] + 2] = t128(inp["lru_lambda"][l], 2)
        for nm, key in (("a_re", "ssm_a_re"), ("a_im", "ssm_a_im")):
            a = f(inp[key][l]).reshape(8, 2, 64)
            pvv[:, PV[f"{nm}{l}"]:PV[f"{nm}{l}"] + 8] = a.transpose(1, 2, 0).reshape(128, 8)
        ld = f(inp["ssm_log_dt"][l]).reshape(8, 2)
        pvv[:, PV[f"log_dt{l}"]:PV[f"log_dt{l}"] + 8] = np.repeat(ld.T[:, None, :], 64, axis=1).reshape(128, 8)
        sk = f(inp["attn_sinks"][l]).reshape(2, 4)
        pvv[:, PV[f"sinks{l}"]:PV[f"sinks{l}"] + 4] = np.repeat(sk[:, None, :], 64, axis=1).reshape(128, 4)
    pvv[:, PV["norm_f"]:PV["norm_f"] + 8] = t128(inp["norm_f"], 8)
    pvv[:, PV["halfpi"]] = np.float32(math.pi / 2)

    gbd = np.zeros((128, L, 2, 2, 128), np.float32)
    for l in range(L):
        for ri, key in enumerate(("w_rg", "w_ig")):
            w = f(inp[key][l])
            for t in range(2):
                for h2 in range(2):
                    gbd[h2 * 64:(h2 + 1) * 64, l, ri, t, h2 * 64:(h2 + 1) * 64] = w[2 * t + h2]
    gbd = gbd.reshape(128, -1)

    bl = np.zeros((128, L, 2, 8, 128), np.float32)
    cl = np.zeros((128, L, 2, 8, 64), np.float32)
    for l in range(L):
        for ri, (kb, kc) in enumerate((("ssm_b_re", "ssm_c_re"), ("ssm_b_im", "ssm_c_im"))):
            b = f(inp[kb][l])
            c = f(inp[kc][l])
            for pr in range(8):
                for g2 in range(2):
                    g = 2 * pr + g2
                    gs = g % 8
                    bl[gs * 16:(gs + 1) * 16, l, ri, pr, g2 * 64:(g2 + 1) * 64] = b[g].T
                    cl[g2 * 64:(g2 + 1) * 64, l, ri, pr, (pr % 2) * 32 + g2 * 16:(pr % 2) * 32 + (g2 + 1) * 16] = c[g].T
    bl = bl.reshape(128, -1)
    cl = cl.reshape(128, -1)
    wglu = np.zeros((128, L, 2, 256), np.float32)
    for l in range(L):
        wglu[:, l] = f(inp["w_glu"][l]).reshape(2, 128, 256).transpose(1, 0, 2)
    wglu = wglu.reshape(128, -1)
    jj = np.repeat(np.arange(1, 129, dtype=np.float32)[None, :], 128, axis=0)
    m16 = np.zeros((128, 64), np.float32)
    m16[0:NSAMP, :] = 1.0
    return dict(wf=wf, pv=pvv, gbd=gbd, bl=bl, cl=cl, wglu=wglu, rope=_rope_tables().reshape(128, -1), jj=jj, m16=m16)


def kernel(**inp):
    f = lambda a: np.asarray(a, dtype=np.float32)
    shared = _prep_shared(inp)
    xp, xs = f(inp["x_prompt"]), f(inp["x_sample"])
    in_maps = []
    for c in range(8):
        xt = np.empty((8, 128, NCOL), np.float32)
        xt[:, :, 0:SEQ] = xp[2 * c].T.reshape(8, 128, SEQ)
        xt[:, :, SEQ:2 * SEQ] = xp[2 * c + 1].T.reshape(8, 128, SEQ)
        xt[:, :, 2 * SEQ:] = xs[c].T.reshape(8, 128, NSAMP)
        conv0 = np.zeros((128, L, 2, 3), np.float32)
        h0 = np.zeros((128, L, 2), np.float32)
        s0 = np.zeros((128, L, 8, 2), np.float32)
        kc = np.zeros((128, L, 128), np.float32)
        vc = np.zeros((128, L, 128), np.float32)
        for l in range(L):
            cc = f(inp["cache_conv_a"][l, c])
            conv0[:, l] = cc.T.reshape(2, 128, 3).transpose(1, 0, 2)
            h0[:, l] = f(inp["state_lru"][l, c]).reshape(2, 128).T
            for ri, key in enumerate(("state_ssm_re", "state_ssm_im")):
                s = f(inp[key][l, c]).reshape(8, 2, 64)
                s0[:, l, :, ri] = s.transpose(1, 2, 0).reshape(128, 8)
            kc[:, l] = f(inp["cache_k"][l, c]).reshape(128, 128).T
            vc[:, l] = f(inp["cache_v"][l, c]).reshape(128, 128)
        parts = dict(shared)
        parts.update(conv0=conv0.reshape(128, -1), h0=h0.reshape(128, -1), s0=s0.reshape(128, -1),
                     kc=kc.reshape(128, -1), vc=vc.reshape(128, -1))
        auxa = np.empty((128, AUXW), np.float32)
        for nm, (o_, w_) in AUX.items():
            auxa[:, o_:o_ + w_] = parts[nm]
        in_maps.append(dict(xT=xt, wf=shared["wf"], aux=auxa))
    if "nc" not in _NC_CACHE:
        _NC_CACHE["nc"] = build_nc()
    nc = _NC_CACHE["nc"]
    res = run_bass_kernel_spmd(nc, in_maps, core_ids=list(range(8)))
    return _assemble(res.results)


def _assemble(results):
    B, Bd = 16, 8
    y_p = np.empty((B, SEQ, D), np.float32)
    y_s = np.empty((Bd, NSAMP, D), np.float32)
    conv_p = np.empty((L, B, 3, 256), np.float32)
    lru_p = np.empty((L, B, 256), np.float32)
    sre_p = np.empty((L, B, 16, 64), np.float32)
    sim_p = np.empty((L, B, 16, 64), np.float32)
    k_p = np.empty((L, B, 128, 2, 64), np.float32)
    v_p = np.empty((L, B, 128, 2, 64), np.float32)
    conv_s = np.empty((L, Bd, 3, 256), np.float32)
    lru_s = np.empty((L, Bd, 256), np.float32)
    sre_s = np.empty((L, Bd, 16, 64), np.float32)
    sim_s = np.empty((L, Bd, 16, 64), np.float32)
    k_s = np.empty((L, Bd, NSAMP, 2, 64), np.float32)
    v_s = np.empty((L, Bd, NSAMP, 2, 64), np.float32)
    for c, r in enumerate(results):
        yT = np.asarray(r["yT"]).reshape(D, NCOL)
        y_p[2 * c] = yT[:, 0:SEQ].T
        y_p[2 * c + 1] = yT[:, SEQ:2 * SEQ].T
        y_s[c] = yT[:, 2 * SEQ:].T
        osm = np.asarray(r["osm"])
        og = lambda nm: osm[:, OS[nm][0]:OS[nm][0] + OS[nm][1]]
        oc = og("o_conv").reshape(128, L, 3, 2, 3)
        oh = og("o_h").reshape(128, L, 3, 2)
        os_ = og("o_s").reshape(2, 64, L, 3, 8, 2)
        ok = og("o_k").reshape(128, L, 272)
        ov = og("o_v").reshape(128, L, 3, 128)
        for l in range(L):
            for s in range(3):
                conv = oc[:, l, s].transpose(2, 1, 0).reshape(3, 256)
                hv = oh[:, l, s].T.reshape(256)
                st = os_[:, :, l, s]
                st = st.transpose(2, 0, 1, 3).reshape(16, 64, 2)
                if s < 2:
                    b = 2 * c + s
                    conv_p[l, b], lru_p[l, b] = conv, hv
                    sre_p[l, b], sim_p[l, b] = st[:, :, 0], st[:, :, 1]
                    k_p[l, b] = ok[:, l, s * 128:(s + 1) * 128].T.reshape(128, 2, 64)
                    v_p[l, b] = ov[:, l, s].reshape(128, 2, 64)
                else:
                    conv_s[l, c], lru_s[l, c] = conv, hv
                    sre_s[l, c], sim_s[l, c] = st[:, :, 0], st[:, :, 1]
                    k_s[l, c] = ok[:, l, 256:272].T.reshape(NSAMP, 2, 64)
                    v_s[l, c] = ov[0:NSAMP, l, s].reshape(NSAMP, 2, 64)
    return (y_p, y_s, conv_p, lru_p, sre_p, sim_p, k_p, v_p, conv_s, lru_s, sre_s, sim_s, k_s, v_s)
```
